# Optimizing a Trainium2 kernel written in Bass

```python
import math
import jax, jax.numpy as jnp
from jax import lax
import numpy as np

D_MODEL = 1024
BATCH = 8
SEQ = 4096
DEPTH = 2
DEC_BATCH = 2
DEC_SEQ = 16384
PAST_LEN = 128

HEAD_DIM = 128
N_HEADS_A = 8
N_KV_A = 2
N_HEADS_B = 8
N_KV_B = 2
WINDOW = 128
BLOCK = 128
N_POOL_GROUPS = 4
POOL_WINDOWS = (2, 4, 8, 16)
POOL_GROUP_DIM = D_MODEL // N_POOL_GROUPS
N_BRANCHES = 3
N_META = 16
GRID_W = 64
ROPE_THETA = 10000.0
D_FF = -(-8 * D_MODEL // (3 * 256)) * 256
ALPHA = (2 * DEPTH) ** 0.25
BETA = (8 * DEPTH) ** -0.25
NEG = -1e30
IN_WIDTHS = (N_HEADS_A * HEAD_DIM, N_KV_A * HEAD_DIM, N_KV_A * HEAD_DIM,
             N_HEADS_B * HEAD_DIM, N_KV_B * HEAD_DIM, N_KV_B * HEAD_DIM,
             D_MODEL, N_BRANCHES * D_MODEL)
IN_WIDTH = sum(IN_WIDTHS)

kernel_name = "hybrid_gated_bidir_encoder"


def _layer_norm(x, g, b, eps=1e-5):
    xf = x.astype(jnp.float32)
    mu = jnp.mean(xf, -1, keepdims=True)
    var = jnp.mean(jnp.square(xf - mu), -1, keepdims=True)
    y = (xf - mu) * lax.rsqrt(var + eps) * g.astype(jnp.float32) + b.astype(jnp.float32)
    return y.astype(x.dtype)


def _rms_norm(x, g, eps=1e-6):
    xf = x.astype(jnp.float32)
    y = xf * lax.rsqrt(jnp.mean(xf * xf, -1, keepdims=True) + eps) * g.astype(jnp.float32)
    return y.astype(x.dtype)


def _rope(x, pos, dim):
    inv = ROPE_THETA ** (-jnp.arange(0, dim, 2, dtype=jnp.float32) / dim)
    ang = pos.astype(jnp.float32)[:, None] * inv[None, :]
    cos = jnp.cos(ang)[:, None, :]
    sin = jnp.sin(ang)[:, None, :]
    xf = x.astype(jnp.float32)
    x1, x2 = xf[..., : dim // 2], xf[..., dim // 2:]
    return jnp.concatenate([x1 * cos - x2 * sin, x1 * sin + x2 * cos], -1).astype(x.dtype)


def _axial_rope(x, row, col):
    half = HEAD_DIM // 2
    return jnp.concatenate([_rope(x[..., :half], row, half), _rope(x[..., half:], col, half)], -1)


def _sink_softmax(s, sink):
    m = jnp.maximum(jnp.max(s, -1, keepdims=True), sink)
    e = jnp.exp(s - m)
    return e / (jnp.sum(e, -1, keepdims=True) + jnp.exp(sink - m))


def _dense_attend(q, k, v):
    s = jnp.einsum('bqkgd,bskd->bkgqs', q, k).astype(jnp.float32) * HEAD_DIM ** -0.5
    p = jax.nn.softmax(s, axis=-1).astype(v.dtype)
    return jnp.einsum('bkgqs,bskd->bqkgd', p, v)


def _mixer_global(q, k, v, q_g, k_g):
    B, L = q.shape[:2]
    S = L - N_META
    nb = S // BLOCK
    ROWS = S // GRID_W
    row = jnp.concatenate([-jnp.ones((N_META,), jnp.int32), jnp.repeat(jnp.arange(ROWS, dtype=jnp.int32), GRID_W)])
    col = jnp.concatenate([jnp.arange(N_META, dtype=jnp.int32), jnp.tile(jnp.arange(GRID_W, dtype=jnp.int32), ROWS)])
    q = _axial_rope(_rms_norm(q, q_g), row, col)
    k = _axial_rope(_rms_norm(k, k_g), row, col)
    g = N_HEADS_A // N_KV_A
    q = q.reshape(B, L, N_KV_A, g, HEAD_DIM)
    o_meta = _dense_attend(q[:, :N_META], k, v)
    qb = jnp.moveaxis(q[:, N_META:].reshape(B, nb, BLOCK, N_KV_A, g, HEAD_DIM), 1, 0)
    o_real = lax.map(lambda qi: _dense_attend(qi, k, v), qb)
    o_real = jnp.moveaxis(o_real, 0, 1).reshape(B, S, N_KV_A, g, HEAD_DIM)
    return jnp.concatenate([o_meta, o_real], 1).reshape(B, L, N_HEADS_A * HEAD_DIM)


def _band(t, B, nb):
    tb = t[:, N_META:].reshape(B, nb, BLOCK, t.shape[2], t.shape[3])
    tp = jnp.pad(tb, ((0, 0), (1, 1), (0, 0), (0, 0), (0, 0)))
    return jnp.concatenate([tp[:, :-2], tp[:, 1:-1], tp[:, 2:]], axis=2)


def _mixer_window(q, k, v, sink):
    B, L = q.shape[:2]
    S = L - N_META
    nb = S // BLOCK
    pos = jnp.arange(L)
    q = _rope(q, pos, HEAD_DIM)
    k = _rope(k, pos, HEAD_DIM)
    g = N_HEADS_B // N_KV_B
    q = q.reshape(B, L, N_KV_B, g, HEAD_DIM)
    scale = HEAD_DIM ** -0.5
    sink_f = sink.astype(jnp.float32).reshape(N_KV_B, g)
    k_meta, v_meta = k[:, :N_META], v[:, :N_META]

    n_front = N_META + BLOCK
    s = jnp.einsum('bqkgd,bskd->bkgqs', q[:, :N_META], k[:, :n_front]).astype(jnp.float32) * scale
    qi = jnp.arange(N_META)[:, None]
    kj = jnp.arange(n_front)[None, :]
    s = jnp.where((kj < N_META) | (kj - qi <= WINDOW), s, NEG)
    p = _sink_softmax(s, sink_f[None, :, :, None, None]).astype(v.dtype)
    o_meta = jnp.einsum('bkgqs,bskd->bqkgd', p, v[:, :n_front])

    qr = q[:, N_META:].reshape(B, nb, BLOCK, N_KV_B, g, HEAD_DIM)
    kb = _band(k, B, nb)
    vb = _band(v, B, nb)
    s_m = jnp.einsum('bnqkgd,bskd->bkgnqs', qr, k_meta).astype(jnp.float32) * scale
    s_b = jnp.einsum('bnqkgd,bnskd->bkgnqs', qr, kb).astype(jnp.float32) * scale
    i = jnp.arange(BLOCK)[:, None]
    j = jnp.arange(3 * BLOCK)[None, :]
    blk = jnp.arange(nb)[:, None, None]
    rel = i + BLOCK - j
    in_win = jnp.abs(rel) <= WINDOW
    in_rng = ((j >= BLOCK) | (blk > 0)) & ((j < 2 * BLOCK) | (blk < nb - 1))
    s_b = jnp.where(in_win[None] & in_rng, s_b, NEG)
    p = _sink_softmax(jnp.concatenate([s_m, s_b], -1), sink_f[None, :, :, None, None, None]).astype(v.dtype)
    o_real = (jnp.einsum('bkgnqs,bskd->bnqkgd', p[..., :N_META], v_meta)
              + jnp.einsum('bkgnqs,bnskd->bnqkgd', p[..., N_META:], vb))
    o_real = o_real.reshape(B, S, N_KV_B, g, HEAD_DIM)
    return jnp.concatenate([o_meta, o_real], 1).reshape(B, L, N_HEADS_B * HEAD_DIM)


def _mixer_pool(u, w, scale):
    B, L, C = u.shape
    cs = jnp.pad(jnp.cumsum(u.astype(jnp.float32), axis=1), ((0, 0), (1, 0), (0, 0)))
    t = jnp.arange(L)
    outs = []
    for gi, win in enumerate(POOL_WINDOWS):
        lo = jnp.clip(t - win // 2, 0, L)
        hi = jnp.clip(t - win // 2 + win, 0, L)
        sl = slice(gi * POOL_GROUP_DIM, (gi + 1) * POOL_GROUP_DIM)
        c = cs[:, :, sl]
        mean = (c[:, hi] - c[:, lo]) / (hi - lo).astype(jnp.float32)[None, :, None]
        outs.append(mean - u[:, :, sl].astype(jnp.float32))
    d = jnp.stack(outs, 2).astype(u.dtype)
    y = jnp.einsum('blgc,gcd->blgd', d, w).reshape(B, L, C)
    return y * scale


def _split_in(u):
    parts = []
    start = 0
    for wdt in IN_WIDTHS:
        parts.append(u[..., start:start + wdt])
        start += wdt
    return parts


def _trunk(x, meta_tokens, w_in, q_norm_g, k_norm_g, sink_logit, pool_w, pool_scale,
           w_branch_a, w_branch_b, w_out, ln1_g, ln1_b, w_up, w_down, ln2_g, ln2_b):
    B = x.shape[0]
    D = D_MODEL
    meta = jnp.broadcast_to(meta_tokens.astype(x.dtype)[None], (B, N_META, D))
    h = jnp.concatenate([meta, x], axis=1)
    L = h.shape[1]
    for l in range(DEPTH):
        u = h @ w_in[l]
        qa, ka, va, qb, kb, vb, uc, ug = _split_in(u)
        oa = _mixer_global(qa.reshape(B, L, N_HEADS_A, HEAD_DIM), ka.reshape(B, L, N_KV_A, HEAD_DIM),
                           va.reshape(B, L, N_KV_A, HEAD_DIM), q_norm_g[l], k_norm_g[l])
        ob = _mixer_window(qb.reshape(B, L, N_HEADS_B, HEAD_DIM), kb.reshape(B, L, N_KV_B, HEAD_DIM),
                           vb.reshape(B, L, N_KV_B, HEAD_DIM), sink_logit[l])
        ya = oa @ w_branch_a[l]
        yb = ob @ w_branch_b[l]
        yc = _mixer_pool(uc, pool_w[l], pool_scale[l])
        gates = jax.nn.sigmoid(ug.astype(jnp.float32)).reshape(B, L, N_BRANCHES, D)
        merged = (gates[:, :, 0] * ya + gates[:, :, 1] * yb + gates[:, :, 2] * yc).astype(h.dtype)
        h = _layer_norm(ALPHA * h + merged @ w_out[l], ln1_g[l], ln1_b[l])
        gu = h @ w_up[l]
        f = (jax.nn.silu(gu[..., :D_FF]) * gu[..., D_FF:]) @ w_down[l]
        h = _layer_norm(ALPHA * h + f, ln2_g[l], ln2_b[l])
    return h[:, N_META:]


def setup_inputs(seed: int = 0) -> dict:
    key = jax.random.key(seed)
    ks = jax.random.split(key, 18)
    D = D_MODEL

    def nrm(k, shape, s):
        return jax.random.normal(k, shape, jnp.float32) * s

    return {
        "x_prompt": nrm(ks[0], (BATCH, SEQ, D), 1.0),
        "x_sample": nrm(ks[1], (DEC_BATCH, DEC_SEQ, D), 1.0),
        "meta_tokens": nrm(ks[2], (N_META, D), 1.0),
        "w_in": nrm(ks[3], (DEPTH, D, IN_WIDTH), D ** -0.5),
        "q_norm_g": 1.0 + nrm(ks[4], (DEPTH, HEAD_DIM), 0.1),
        "k_norm_g": 1.0 + nrm(ks[5], (DEPTH, HEAD_DIM), 0.1),
        "sink_logit": nrm(ks[6], (DEPTH, N_HEADS_B), 0.5),
        "pool_w": nrm(ks[7], (DEPTH, N_POOL_GROUPS, POOL_GROUP_DIM, POOL_GROUP_DIM), POOL_GROUP_DIM ** -0.5),
        "pool_scale": 1.0 + nrm(ks[8], (DEPTH, D), 0.1),
        "w_branch_a": nrm(ks[9], (DEPTH, N_HEADS_A * HEAD_DIM, D), (N_HEADS_A * HEAD_DIM) ** -0.5),
        "w_branch_b": nrm(ks[10], (DEPTH, N_HEADS_B * HEAD_DIM, D), (N_HEADS_B * HEAD_DIM) ** -0.5),
        "w_out": nrm(ks[11], (DEPTH, D, D), D ** -0.5 * BETA),
        "ln1_g": 1.0 + nrm(ks[12], (DEPTH, D), 0.1),
        "ln1_b": nrm(ks[13], (DEPTH, D), 0.02),
        "w_up": nrm(ks[14], (DEPTH, D, 2 * D_FF), D ** -0.5),
        "w_down": nrm(ks[15], (DEPTH, D_FF, D), D_FF ** -0.5 * BETA),
        "ln2_g": 1.0 + nrm(ks[16], (DEPTH, D), 0.1),
        "ln2_b": nrm(ks[17], (DEPTH, D), 0.02),
    }


def reference(x_prompt, x_sample, meta_tokens, w_in, q_norm_g, k_norm_g, sink_logit, pool_w, pool_scale,
              w_branch_a, w_branch_b, w_out, ln1_g, ln1_b, w_up, w_down, ln2_g, ln2_b):
    y_prompt = _trunk(x_prompt, meta_tokens, w_in, q_norm_g, k_norm_g, sink_logit, pool_w, pool_scale,
                      w_branch_a, w_branch_b, w_out, ln1_g, ln1_b, w_up, w_down, ln2_g, ln2_b)
    y_sample = _trunk(x_sample, meta_tokens, w_in, q_norm_g, k_norm_g, sink_logit, pool_w, pool_scale,
                      w_branch_a, w_branch_b, w_out, ln1_g, ln1_b, w_up, w_down, ln2_g, ln2_b)
    return (y_prompt, y_sample)
```

```python
import contextlib
import numpy as np
import ml_dtypes
import concourse.bass as bass
import concourse.mybir as mybir
from concourse.bass_utils import run_bass_kernel_spmd

F32 = mybir.dt.float32
BF16 = mybir.dt.bfloat16
AF = mybir.ActivationFunctionType
ALU = mybir.AluOpType

D = 1024
KC = 8
NM = 16
DFF = 2816
FC = 22
INW = 7168
DEPTH = 2
ALPHA = (2 * DEPTH) ** 0.25
NVL = 50
NEGM = -30000.0


class Op:
    __slots__ = ("eng", "fn", "deps", "dma", "sem_key", "signal", "sig_idx", "dma_waits", "inc")

    def __init__(self, eng, fn, dma, sem_key, inc):
        self.eng = eng
        self.fn = fn
        self.deps = []
        self.dma = dma
        self.sem_key = sem_key
        self.signal = False
        self.sig_idx = 0
        self.dma_waits = {}
        self.inc = inc


class Sched:
    ENGS = ("pe", "act", "dve", "pool", "sp")
    EPOCH = 30000

    def __init__(self):
        self.ops = {e: [] for e in self.ENGS}
        self.last_writer = {}
        self.readers = {}
        self.dma_count = {}
        self.nops = 0

    def add(self, eng, fn, reads=(), writes=(), dma=False, sem_key=None, inc=16):
        op = Op(eng, fn, dma, sem_key, inc)
        deps = {}
        for k in reads:
            w = self.last_writer.get(k)
            if w is not None:
                deps[id(w)] = w
        for k in writes:
            w = self.last_writer.get(k)
            if w is not None:
                if not (w.eng == eng and not w.dma and not dma):
                    deps[id(w)] = w
            last_by_eng = {}
            for r in self.readers.get(k, ()):
                if r.eng == eng and not r.dma and not dma:
                    continue
                if r.dma or r.eng == "pool":
                    deps[id(r)] = r
                else:
                    last_by_eng[r.eng] = r
            for r in last_by_eng.values():
                deps[id(r)] = r
        if eng == "pe":
            deps = {i: d for i, d in deps.items() if d.eng != "pe" or d.dma}
        if dma:
            deps = {i: d for i, d in deps.items() if not (d.dma and d.sem_key == sem_key)}
        op.deps = list(deps.values())
        for d in op.deps:
            if d.dma:
                c = self.dma_count[d.sem_key]
                if op.dma_waits.get(d.sem_key, 0) < c:
                    op.dma_waits[d.sem_key] = c
        if dma:
            self.dma_count[sem_key] = self.dma_count.get(sem_key, 0) + inc
        for k in reads:
            self.readers.setdefault(k, []).append(op)
        for k in writes:
            self.last_writer[k] = op
            self.readers[k] = []
        self.ops[eng].append(op)
        self.nops += 1
        return op

    def emit(self, nc, stack):
        for e in self.ENGS:
            for op in self.ops[e]:
                for d in op.deps:
                    if not d.dma:
                        d.signal = True
        nsig = {}
        for e in self.ENGS:
            c = 0
            for op in self.ops[e]:
                if op.signal and not op.dma:
                    c += 1
                    op.sig_idx = c
            nsig[e] = c
        esem = {}
        for e in self.ENGS:
            n_ep = max(1, (nsig[e] + self.EPOCH - 1) // self.EPOCH)
            esem[e] = [stack.enter_context(nc.semaphore(f"s_{e}_{i}")) for i in range(n_ep)]
        dsem = {}
        for i, k in enumerate(self.dma_count):
            dsem[k] = stack.enter_context(nc.semaphore(f"d{i}"))
        self.n_sems = sum(len(v) for v in esem.values()) + len(dsem)
        block = stack.enter_context(nc.Block())
        EP = self.EPOCH

        def run(e, engine):
            waited = {}
            dwaited = {}
            for op in self.ops[e]:
                need = {}
                for d in op.deps:
                    if d.dma:
                        continue
                    if need.get(d.eng, 0) < d.sig_idx:
                        need[d.eng] = d.sig_idx
                for se, idx in need.items():
                    if waited.get(se, 0) >= idx:
                        continue
                    waited[se] = idx
                    ep = (idx - 1) // EP
                    engine.wait_ge(esem[se][ep], idx - ep * EP)
                for sk, cnt in op.dma_waits.items():
                    if dwaited.get(sk, 0) >= cnt:
                        continue
                    dwaited[sk] = cnt
                    engine.wait_ge(dsem[sk], cnt)
                ins = op.fn(engine)
                if ins is None:
                    if e == "sp":
                        for sk, cnt in self.dma_count.items():
                            engine.wait_ge(dsem[sk], cnt)
                        for se in self.ENGS:
                            if nsig[se] > 0:
                                ep = (nsig[se] - 1) // EP
                                engine.wait_ge(esem[se][ep], nsig[se] - ep * EP)
                    continue
                if op.dma:
                    ins.then_inc(dsem[op.sem_key], op.inc)
                elif op.signal:
                    ep = (op.sig_idx - 1) // EP
                    ins.then_inc(esem[e][ep], 1)

        @block.tensor
        def _(t):
            run("pe", t)

        @block.scalar
        def _(t):
            run("act", t)

        @block.vector
        def _(t):
            run("dve", t)

        @block.gpsimd
        def _(t):
            run("pool", t)

        @block.sync
        def _(t):
            run("sp", t)


class Builder:
    def __init__(self, NR, debug=False):
        assert NR % 512 == 0
        self.NR = NR
        self.NT = NR // 512
        self.NB = NR // 128
        self.XW = 4 * NR + 1152
        self.debug = debug
        self.S = Sched()
        self.nc = bass.Bass("TRN2", target_bir_lowering=False)
        self._tmp_i = 0
        self._ps_i = 0
        self._wr_i = 0
        self._kv_i = 0
        self._pt_i = 0

    def mm(self, out, lhsT, rhs, start, stop, reads, writes):
        self.S.add("pe", lambda e, o=out, l=lhsT, r=rhs, a=start, b=stop:
                   e.matmul(o, lhsT=l, rhs=r, start=a, stop=b, skip_group_check=True),
                   reads=reads, writes=writes)

    def act(self, out, in_, func, reads, writes, scale=None, bias=None):
        kw = {}
        if scale is not None:
            kw["scale"] = scale
        if bias is not None:
            kw["bias"] = bias
        self.S.add("act", lambda e, o=out, i=in_, f=func, kw=kw: e.activation(out=o, in_=i, func=f, **kw),
                   reads=reads, writes=writes)

    def tt(self, eng, out, in0, in1, op, reads, writes):
        self.S.add(eng, lambda e, o=out, a=in0, b=in1, p=op: e.tensor_tensor(out=o, in0=a, in1=b, op=p),
                   reads=reads, writes=writes)

    def ts(self, eng, out, in0, s1, op0, reads, writes, s2=None, op1=None):
        if op1 is None:
            self.S.add(eng, lambda e, o=out, a=in0, s=s1, p=op0: e.tensor_scalar(out=o, in0=a, scalar1=s, scalar2=None, op0=p),
                       reads=reads, writes=writes)
        else:
            self.S.add(eng, lambda e, o=out, a=in0, s=s1, p=op0, t=s2, q=op1:
                       e.tensor_scalar(out=o, in0=a, scalar1=s, scalar2=t, op0=p, op1=q),
                       reads=reads, writes=writes)

    def stt(self, out, in0, scalar, in1, op0, op1, reads, writes):
        self.S.add("dve", lambda e, o=out, a=in0, s=scalar, b=in1, p=op0, q=op1:
                   e.scalar_tensor_tensor(out=o, in0=a, scalar=s, in1=b, op0=p, op1=q),
                   reads=reads, writes=writes)

    def cp(self, eng, out, in_, reads, writes):
        if eng == "act":
            self.S.add("act", lambda e, o=out, i=in_: e.copy(out=o, in_=i), reads=reads, writes=writes)
        else:
            self.S.add(eng, lambda e, o=out, i=in_: e.tensor_copy(out=o, in_=i), reads=reads, writes=writes)

    def recip(self, out, in_, reads, writes):
        self.S.add("dve", lambda e, o=out, i=in_: e.reciprocal(out=o, in_=i), reads=reads, writes=writes)

    def memset(self, eng, ap, val, writes):
        self.S.add(eng, lambda e, a=ap, v=val: e.memset(a, v), writes=writes)

    def dma(self, q, out, in_, reads, writes, sem_key):
        self.S.add(q, lambda e, o=out, i=in_: e.dma_start(out=o, in_=i), reads=reads, writes=writes,
                   dma=True, sem_key=sem_key)

    def tmp(self):
        i = self._tmp_i % 5
        self._tmp_i += 1
        return i

    def bank(self):
        i = self._ps_i % 8
        self._ps_i += 1
        return i

    def declare(self, st):
        nc, NR = self.nc, self.NR

        def din(name, shape, dt=F32):
            return nc.dram_tensor(name, shape, dt, kind="ExternalInput").ap()

        def dout(name, shape, dt=F32):
            return nc.dram_tensor(name, shape, dt, kind="ExternalOutput").ap()

        def dscr(name, shape, dt):
            return nc.dram_tensor(name, shape, dt)

        self.x = {"p": din("xp", [128, KC, NR]), "s": din("xs", [128, KC, NR])}
        self.xm = din("xm", [128, KC, NM])
        self.w_in = din("w_in", [DEPTH, D, INW])
        self.w_ba = din("w_ba", [DEPTH, D, D])
        self.w_bb = din("w_bb", [DEPTH, D, D])
        self.w_out = din("w_out", [DEPTH, D, D])
        self.w_up = din("w_up", [DEPTH, D, 2 * DFF])
        self.w_down = din("w_down", [DEPTH, DFF, D])
        self.pool_w = din("pool_w", [DEPTH, 4, 256, 256])
        self.vecs_d = din("vecs", [128, 2 * NVL + 2])
        self.rope = {"p": din("rope_p", [128, 4, NM + NR]), "s": din("rope_s", [128, 4, NM + NR])}
        self.c32_d = din("c32", [128, 3, 128])
        self.NCB = 3 * 128 + 4 * 512 + 64
        self.cb16_d = din("cb16", [128, self.NCB], BF16)
        self.blend_d = din("blend", [128, 16])
        self.invc_d = din("invc", [128, 3, 4, 16])
        self.y = {"p": dout("yp", [128, KC, NR]), "s": dout("ys", [128, KC, NR])}
        self.wi_b = dscr("wi_b", [DEPTH, 128, KC, INW], BF16).ap()
        self.wba_b = dscr("wba_b", [DEPTH, 128, KC, D], BF16).ap()
        self.wbb_b = dscr("wbb_b", [DEPTH, 128, KC, D], BF16).ap()
        self.wo_b = dscr("wo_b", [DEPTH, 128, KC, D], BF16).ap()
        self.wu_b = dscr("wu_b", [DEPTH, 128, KC, 2 * DFF], BF16).ap()
        self.wd_b = dscr("wd_b", [DEPTH, 128, KC, FC, 128], BF16).ap()
        self.pw_b = dscr("pw_b", [DEPTH, 128, 4, 2, 256], BF16).ap()
        self.wg_b = dscr("wg_b", [DEPTH, 128, KC, 3, KC, 128], BF16).ap()
        self.wab_b = dscr("wab_b", [DEPTH, 128, KC, 2, KC, 128], BF16).ap()
        self.wu2_b = dscr("wu2_b", [DEPTH, 128, FC // 2, 2, KC, 256], BF16).ap()
        self.H1 = {s: dscr("h1_" + s, [128, KC, NM + NR], F32).ap() for s in "ps"}
        self.KAl = dscr("kal", [128, 2, NR], BF16).ap()
        self.VAl = dscr("val", [128, 2, self.NB, 128], BF16).ap()
        self.KBl = {s: dscr("kbl_" + s, [128, 2, (self.NB + 2) * 128], BF16).ap() for s in "ps"}
        self.VBl = {s: dscr("vbl_" + s, [128, self.NB + 2, 2, 128], BF16).ap() for s in "ps"}
        self.U = {s: dscr("u_" + s, [128, KC, NR + 16], F32).ap() for s in "ps"}
        self.Um = {s: dscr("um_" + s, [128, KC, 32], F32).ap() for s in "ps"}
        self.CK = [[dscr(f"ck{l}{g}", [128, NR], BF16) for g in range(2)] for l in range(DEPTH)]
        self.CV = [[dscr(f"cv{l}{g}", [128, NR], BF16) for g in range(2)] for l in range(DEPTH)]
        self.CH = [dscr(f"ch{l}", [128, 1152], BF16) for l in range(DEPTH)]
        self.GK = [[dscr(f"gk{l}{g}", [512, NR], BF16) for g in range(2)] for l in range(DEPTH)]
        self.GV = [[dscr(f"gv{l}{g}", [512, NR], BF16) for g in range(2)] for l in range(DEPTH)]
        self.GH = [dscr(f"gh{l}", [512, 1152], BF16) for l in range(DEPTH)]
        if self.debug:
            self.dbg = {s: dout("dbg_" + s, [128, KC, NM + NR]) for s in "ps"}

        def sb(name, shape, dt):
            return st.enter_context(nc.sbuf_tensor(name, shape, dt))

        self.WR = sb("WR", [128, 4, 4096], BF16)
        self.KVR = sb("KVR", [128, 2, 4096], BF16)
        self.KVB = sb("KVB", [128, 3072], BF16)
        self.H32 = sb("H32", [128, KC, 512], F32)
        self.LD32 = sb("LD32", [128, 4, 512], F32)
        self.HB = sb("HB", [128, 2, KC, 512], BF16)
        self.XA = sb("XA", [128, 24, 512], BF16)
        self.OB = sb("OB", [128, 8, 512], BF16)
        self.DP = sb("DP", [128, 8, 512], BF16)
        self.TMP = sb("TMP", [128, 8, 512], F32)
        self.RT = sb("RT", [128, 4, 512], F32)
        self.PT = sb("PT", [128, 4, 512], BF16)
        self.UT = sb("UT", [128, 3, 2, 528], F32)
        self.KST = sb("KST", [128, 2, 2, 512], BF16)
        self.VST = sb("VST", [128, 4, 512], BF16)
        self.C32 = sb("C32", [128, 3, 128], F32)
        self.CB = sb("CB", [128, self.NCB], BF16)
        self.VEC = sb("VEC", [128, 2 * NVL + 2], F32)
        self.BL = sb("BL", [128, 16], F32)
        self.INVC = sb("INVC", [128, 3, 4, 16], F32)
        self.PW = sb("PW", [128, 4, 2, 256], BF16)
        self.KM = sb("KM", [128, 2, 4, 128], BF16)
        self.VM = sb("VM", [128, 2, 4, 128], BF16)
        self.ESK = sb("ESK", [128, 8], F32)
        self.HC = sb("HC", [128, 4, 256], BF16)
        self.HA = sb("HA", [128, 256], F32)
        self.HO = sb("HO", [128, 256], BF16)
        self.ZT = sb("ZT", [128, 64], F32)
        self.PS = [st.enter_context(nc.psum_tensor(f"ps{i}", [128, 512], F32)) for i in range(8)]

    def ones32(self):
        return self.C32[:, 0, :]

    def perm(self, which):
        return self.C32[:, 1 if which == "A" else 2, :]

    def identb(self):
        return self.CB[:, 0:128]

    def onesb(self):
        return self.CB[:, 128:256]

    def ones16(self):
        return self.CB[:, 256:384]

    def mask(self, i):
        o = 384 + i * 512
        return self.CB[:, o:o + 512].rearrange("p (a b) -> p a b", a=4)

    def maskmeta(self):
        o = 384 + 4 * 512
        return self.CB[:, o:o + 64].rearrange("p (a b) -> p a b", a=4)

    def vec(self, l, col):
        return self.VEC[:, l * NVL + col: l * NVL + col + 1]

    def load_consts(self):
        for name, sbt, dr in (("C32", self.C32, self.c32_d), ("CB", self.CB, self.cb16_d),
                              ("VEC", self.VEC, self.vecs_d), ("BL", self.BL, self.blend_d),
                              ("INVC", self.INVC, self.invc_d)):
            self.dma("sp", sbt[:], dr, [], [name], "c_" + name)
        self.memset("pool", self.ZT[:], 0.0, ["ZT"])

    def convert_weights(self):
        jobs = []
        for l in range(DEPTH):
            def blocks(src, dst, n):
                for c0 in range(0, n, 512):
                    jobs.append((src[:, c0:c0 + 512].rearrange("(k p) n -> p k n", p=128),
                                 [(dst[:, :, c0:c0 + 512], None)], KC, 512))
            blocks(self.w_in[l][:, 0:4096], self.wi_b[l], 4096)
            for i in range(3):
                for c0 in range(0, D, 512):
                    src = self.w_in[l][:, 4096 + i * D + c0:4096 + i * D + c0 + 512].rearrange("(k p) n -> p k n", p=128)
                    jobs.append((src, [(self.wg_b[l][:, c0 // 128 + cc, i], cc) for cc in range(4)], KC, 512))
            for j, wsrc in enumerate((self.w_ba[l], self.w_bb[l])):
                for c0 in range(0, D, 512):
                    src = wsrc[:, c0:c0 + 512].rearrange("(k p) n -> p k n", p=128)
                    jobs.append((src, [(self.wab_b[l][:, c0 // 128 + cc, j], cc) for cc in range(4)], KC, 512))
            blocks(self.w_out[l], self.wo_b[l], D)
            for part in range(2):
                for gi in range(FC // 2):
                    src = self.w_up[l][:, part * DFF + gi * 256:part * DFF + (gi + 1) * 256].rearrange("(k p) n -> p k n", p=128)
                    jobs.append((src, [(self.wu2_b[l][:, gi, part], None)], KC, 256))
            for k0 in range(0, FC, 8):
                kn = min(8, FC - k0)
                for c0 in range(0, D, 512):
                    src = self.w_down[l][k0 * 128:(k0 + kn) * 128, c0:c0 + 512].rearrange("(k p) n -> p k n", p=128)
                    dsts = [(self.wd_b[l][:, c0 // 128 + cc, k0:k0 + kn, :], cc) for cc in range(4)]
                    jobs.append((src, dsts, kn, 512))
            for g in range(4):
                jobs.append((self.pool_w[l, g].rearrange("(k p) n -> p k n", p=128),
                             [(self.pw_b[l][:, g, :, :], None)], 2, 256))
        stg = [(self.H32, "H32"), (self.TMP, "tmp")]
        outb = [(self.OB, "OB"), (self.DP, "DP")]
        engs = ["dve", "act", "pool"]
        for i, (src, dsts, kc, bw) in enumerate(jobs):
            s32, k32 = stg[i % 2]
            s16, k16 = outb[i % 2]
            n = kc * bw
            v32 = s32[:].rearrange("p a b -> p (a b)")[:, 0:n].rearrange("p (k n) -> p k n", k=kc)
            v16 = s16[:].rearrange("p a b -> p (a b)")[:, 0:n].rearrange("p (k n) -> p k n", k=kc)
            rk = [(k32, c) for c in range(8)]
            wk = [(k16, c) for c in range(8)]
            self.dma("sp", v32, src, [], rk, "cv32_%d" % (i % 2))
            self.cp(engs[i % 3], v16, v32, rk, wk)
            for (dst, cc) in dsts:
                sv = v16 if cc is None else v16[:, :, cc * 128:(cc + 1) * 128]
                self.dma("pool", dst, sv, wk, ["wts"], "cv16_%d" % (i % 2))

    def wload(self, parts):
        s = self._wr_i % 4
        self._wr_i += 1
        key = ("WR", s)
        views = []
        off = 0
        for ap in parts:
            if len(ap.shape) == 3:
                k, n = ap.shape[1], ap.shape[2]
                v = self.WR[:, s, off:off + k * n].rearrange("p (k n) -> p k n", k=k)
                m = k * n
            else:
                m = ap.shape[1]
                v = self.WR[:, s, off:off + m]
            self.dma("sp", v, ap, ["wts"], [key], "wr%d" % s)
            views.append(v)
            off += m
        assert off <= 4096
        return key, views

    def hkey(self, l, seg):
        return [] if l == 0 else [("H1", seg)]

    def load_hb(self, src, slot, T, rk):
        for half in range(2):
            self.dma("sp", self.LD32[:, :, 0:T], src[:, half * 4:half * 4 + 4, :], rk, ["LD32"], "ld32")
            self.cp("pool", self.HB[:, slot, half * 4:half * 4 + 4, 0:T], self.LD32[:, :, 0:T], ["LD32"],
                    [("HB", slot, c) for c in range(half * 4, half * 4 + 4)])

    def hsrc(self, l, seg, t0, T, meta):
        if l == 0:
            return self.xm if meta else self.x[seg][:, :, t0:t0 + T]
        return self.H1[seg][:, :, 0:NM] if meta else self.H1[seg][:, :, NM + t0:NM + t0 + T]

    def load_rope(self, seg, col0, T):
        self.dma("sp", self.RT[:, :, 0:T], self.rope[seg][:, :, col0:col0 + T], [], ["RT"], "rt")

    def head_chain(self, bk, T, which, gcol, out_ap, out_keys):
        psk = ("ps", bk)
        ps = self.PS[bk][:, 0:T]
        ci, si = (0, 1) if which == "A" else (2, 3)
        t_q = self.tmp()
        q32 = self.TMP[:, t_q, 0:T]
        if which == "A":
            self.act(q32, ps, AF.Identity, [psk, "VEC"], [("tmp", t_q)], scale=gcol)
            t_s = self.tmp()
            sq = self.TMP[:, t_s, 0:T]
            self.act(sq, ps, AF.Square, [psk], [("tmp", t_s)])
            b2 = self.bank()
            self.mm(self.PS[b2][:, 0:T], self.ones32(), sq, True, True, ["C32", ("tmp", t_s)], [("ps", b2)])
            self.act(sq, self.PS[b2][:, 0:T], AF.Ln, [("ps", b2), "VEC"], [("tmp", t_s)],
                     bias=self.VEC[:, 2 * NVL:2 * NVL + 1])
            self.act(sq, sq, AF.Exp, [("tmp", t_s)], [("tmp", t_s)], scale=-0.5)
        else:
            self.cp("dve", q32, ps, [psk], [("tmp", t_q)])
        b3 = self.bank()
        self.mm(self.PS[b3][:, 0:T], self.perm(which), q32, True, True, ["C32", ("tmp", t_q)], [("ps", b3)])
        t_b = self.tmp()
        bb = self.TMP[:, t_b, 0:T]
        self.tt("dve", bb, self.PS[b3][:, 0:T], self.RT[:, si, 0:T], ALU.mult, [("ps", b3), "RT"], [("tmp", t_b)])
        self.tt("pool", q32, q32, self.RT[:, ci, 0:T], ALU.mult, [("tmp", t_q), "RT"], [("tmp", t_q)])
        if which == "A":
            self.tt("pool", q32, q32, bb, ALU.add, [("tmp", t_q), ("tmp", t_b)], [("tmp", t_q)])
            self.tt("dve", out_ap, q32, sq, ALU.mult, [("tmp", t_q), ("tmp", t_s)], out_keys)
        else:
            self.tt("dve", out_ap, q32, bb, ALU.add, [("tmp", t_q), ("tmp", t_b)], out_keys)

    def phase1_tile(self, l, seg, ti, meta):
        NR, NB = self.NR, self.NB
        T = NM if meta else 512
        t0 = 0 if meta else ti * 512
        si_ = 0 if seg == "p" else 1
        slot = self._hb_par
        self._hb_par ^= 1
        self.load_hb(self.hsrc(l, seg, t0, T, meta), slot, T, self.hkey(l, seg))
        self.load_rope(seg, 0 if meta else NM + t0, T)
        hbk = [("HB", slot, c) for c in range(KC)]
        hb = self.HB[:, slot]
        wi = self.wi_b[l]
        k1, (x1,) = self.wload([wi[:, :, 1024:1536]])
        k2, (x2,) = self.wload([wi[:, :, 2560:3072]])
        for which, xw, wkey, row in (("A", x1, k1, 0), ("B", x2, k2, 1)):
            for g in range(2):
                bk = self.bank()
                for kc in range(KC):
                    self.mm(self.PS[bk][:, 0:T], xw[:, kc, g * 128:(g + 1) * 128], hb[:, kc, 0:T],
                            kc == 0, kc == KC - 1, [wkey, hbk[kc]], [("ps", bk)])
                if meta:
                    out_ap = self.KM[:, si_, row * 2 + g, 0:NM]
                    okeys = [("KM", seg)]
                else:
                    out_ap = self.KST[:, row, g, :]
                    okeys = [("KST", row)]
                self.head_chain(bk, T, which, self.vec(l, 1), out_ap, okeys)
        ntb = 1 if meta else 4
        for tb in range(ntb):
            bk = self.bank()
            M = NM if meta else 128
            for j, (xw, wkey) in enumerate(((x1, k1), (x2, k2))):
                for kc in range(KC):
                    self.mm(self.PS[bk][0:M, j * 256:(j + 1) * 256], hb[:, kc, tb * 128:tb * 128 + M],
                            xw[:, kc, 256:512], (j == 0 and kc == 0), (j == 1 and kc == KC - 1),
                            [wkey, hbk[kc]], [("ps", bk)])
            if meta:
                self.cp("act", self.VM[0:NM, si_].rearrange("p a b -> p (a b)"), self.PS[bk][0:NM, :], [("ps", bk)], [("VM", seg)])
            else:
                self.cp("act" if tb % 2 else "dve", self.VST[:, tb, :], self.PS[bk][:, :], [("ps", bk)], [("VST", tb)])
        for half in range(2):
            ku, (xu,) = self.wload([wi[:, :, 3072 + half * 512:3072 + (half + 1) * 512]])
            for cc in range(4):
                c = half * 4 + cc
                bk = self.bank()
                for kc in range(KC):
                    self.mm(self.PS[bk][:, 0:T], xu[:, kc, cc * 128:(cc + 1) * 128], hb[:, kc, 0:T],
                            kc == 0, kc == KC - 1, [ku, hbk[kc]], [("ps", bk)])
                self.cp("act" if c % 2 else "dve", self.H32[:, c, 0:T], self.PS[bk][:, 0:T], [("ps", bk)], [("H32", c)])
        h32k = [("H32", c) for c in range(KC)]
        if meta:
            self.dma("pool", self.Um[seg][:, :, 8:8 + NM], self.H32[:, :, 0:NM], h32k, [("Um", seg)], "st_u")
            return
        vst4 = self.VST[:].rearrange("p t (j g d) -> p t j g d", j=2, g=2)
        vstk = [("VST", tb) for tb in range(4)]
        for g in range(2):
            if seg == "p":
                ka_dst = self.KAl[:, g, t0:t0 + 512]
                va_dst = self.VAl[:, g, ti * 4:ti * 4 + 4, :]
                kakey, vakey = "KAl", "VAl"
            else:
                ka_dst = self.CK[l][g].ap()[:, t0:t0 + 512]
                va_dst = self.CV[l][g].ap().rearrange("p (n d) -> p n d", d=128)[:, ti * 4:ti * 4 + 4, :]
                kakey, vakey = ("CK", l, g), ("CV", l, g)
            self.dma("pool", ka_dst, self.KST[:, 0, g, :], [("KST", 0)], [kakey], "st_ka")
            self.dma("pool", va_dst, vst4[:, :, 0, g, :], vstk, [vakey], "st_va")
        self.dma("pool", self.KBl[seg][:, :, 128 + t0:128 + t0 + 512], self.KST[:, 1], [("KST", 1)], [("KBl", seg)], "st_kb")
        self.dma("pool", self.VBl[seg][:, 1 + ti * 4:1 + ti * 4 + 4], vst4[:, :, 1], vstk, [("VBl", seg)], "st_vb")
        self.dma("pool", self.U[seg][:, :, 8 + t0:8 + t0 + 512], self.H32[:, :, :], h32k, [("U", seg)], "st_u")
        if seg == "s":
            o = 0
            C = self.CH[l].ap()
            if ti == 0:
                self.dma("pool", C[:, o:o + 256].rearrange("p (g t) -> p g t", g=2), self.KST[:, 1, :, 0:128],
                         [("KST", 1)], [("CH", l)], "st_kb")
                self.dma("pool", C[:, o + 512:o + 768].rearrange("p (g d) -> p g d", g=2), vst4[:, 0, 1], vstk, [("CH", l)], "st_vb")
                self.cp("dve", self.HO[:, 0:64].rearrange("p (c t) -> p c t", c=8), self.H32[:, :, 0:8], h32k, ["HO"])
                self.dma("pool", C[:, o + 1024:o + 1088], self.HO[:, 0:64], ["HO"], [("CH", l)], "st_ho")
            if ti == self.NT - 1:
                self.dma("pool", C[:, o + 256:o + 512].rearrange("p (g t) -> p g t", g=2), self.KST[:, 1, :, 384:512],
                         [("KST", 1)], [("CH", l)], "st_kb")
                self.dma("pool", C[:, o + 768:o + 1024].rearrange("p (g d) -> p g d", g=2), vst4[:, 3, 1], vstk, [("CH", l)], "st_vb")
                self.cp("dve", self.HO[:, 64:128].rearrange("p (c t) -> p c t", c=8), self.H32[:, :, 504:512], h32k, ["HO"])
                self.dma("pool", C[:, o + 1088:o + 1152], self.HO[:, 64:128], ["HO"], [("CH", l)], "st_ho")

    def phase1(self, l, seg):
        si_ = 0 if seg == "p" else 1
        self.memset("pool", self.KM[:, si_], 0.0, [("KM", seg)])
        self.memset("pool", self.VM[:, si_], 0.0, [("VM", seg)])
        self.phase1_tile(l, seg, 0, True)
        for ti in range(self.NT):
            self.phase1_tile(l, seg, ti, False)

    def allgather(self, l):
        jobs = [(self.CK[l][g], self.GK[l][g], ("CK", l, g), ("GK", l, g)) for g in range(2)]
        jobs += [(self.CV[l][g], self.GV[l][g], ("CV", l, g), ("GV", l, g)) for g in range(2)]
        jobs += [(self.CH[l], self.GH[l], ("CH", l), ("GH", l))]
        for i, (ct, gt, ck, gk) in enumerate(jobs):
            self.S.add("pool", lambda e, ct=ct, gt=gt: e.collective_compute(
                "AllGather", ALU.bypass, replica_groups=[[0, 1, 2, 3], [4, 5, 6, 7]],
                ins=[ct.ap().opt()], outs=[gt.ap().opt()]),
                reads=[ck], writes=[gk], dma=True, sem_key="ag%d_%d" % (l, i), inc=1)

    def prep(self, l, seg):
        NR, NB = self.NR, self.NB
        U, Um = self.U[seg], self.Um[seg]
        z = self.ZT[:].rearrange("p (c t) -> p c t", c=8)
        sfx = "_%d%s" % (l, seg)
        self.dma("pool", Um[:, :, 0:8], z, ["ZT"], [("Um", seg)], "pp0" + sfx)
        if seg == "p":
            self.dma("pool", U[:, :, 0:8], Um[:, :, 16:24], [("Um", seg)], [("U", seg)], "pp1" + sfx)
            self.dma("pool", U[:, :, NR + 8:NR + 16], z, ["ZT"], [("U", seg)], "pp2" + sfx)
            self.dma("pool", Um[:, :, 24:32], U[:, :, 8:16], [("U", seg)], [("Um", seg)], "pp3" + sfx)
            return
        G = self.GH[l].ap()
        o = 0
        gk = ("GH", l)

        def blend(cands, wcol0, width, extra=None):
            acc = self.HA[:, 0:width]
            self.ts("dve", acc, self.HC[:, 0, 0:width], self.BL[:, wcol0:wcol0 + 1], ALU.mult, ["HC", "BL"], ["HA"])
            for r in range(1, 4):
                self.stt(acc, self.HC[:, r, 0:width], self.BL[:, wcol0 + r:wcol0 + r + 1], acc, ALU.mult, ALU.add,
                         ["HC", "BL", "HA"], ["HA"])

        def gload(off, width):
            self.dma("sp", self.HC[:, :, 0:width], G[:, off:off + width].rearrange("(r p) x -> p r x", p=128),
                     [gk], ["HC"], "hc")

        for (off, wc, ext) in ((o + 256, 0, 0), (o, 4, NB + 1)):
            gload(off, 256)
            blend(None, wc, 256)
            self.cp("dve", self.HO[:, 0:256], self.HA[:, 0:256], ["HA"], ["HO"])
            self.dma("pool", self.KBl[seg][:, :, ext * 128:(ext + 1) * 128], self.HO[:, 0:256].rearrange("p (g t) -> p g t", g=2),
                     ["HO"], [("KBl", seg)], "pp1" + sfx)
        for (off, wc, ext) in ((o + 768, 0, 0), (o + 512, 4, NB + 1)):
            gload(off, 256)
            blend(None, wc, 256)
            self.cp("dve", self.HO[:, 0:256], self.HA[:, 0:256], ["HA"], ["HO"])
            self.dma("pool", self.VBl[seg][:, ext], self.HO[:, 0:256].rearrange("p (g d) -> p g d", g=2),
                     ["HO"], [("VBl", seg)], "pp1" + sfx)
        gload(o + 1088, 64)
        blend(None, 0, 64)
        self.dma("sp", self.LD32[:, 0, 0:64].rearrange("p (c t) -> p c t", c=8), Um[:, :, 16:24], [("Um", seg)], ["LD32"], "ld32")
        self.stt(self.HA[:, 0:64], self.LD32[:, 0, 0:64], self.BL[:, 8:9], self.HA[:, 0:64], ALU.mult, ALU.add,
                 ["LD32", "BL", "HA"], ["HA"])
        self.dma("pool", U[:, :, 0:8], self.HA[:, 0:64].rearrange("p (c t) -> p c t", c=8), ["HA"], [("U", seg)], "pp2" + sfx)
        gload(o + 1024, 64)
        blend(None, 4, 64)
        self.dma("pool", U[:, :, NR + 8:NR + 16], self.HA[:, 0:64].rearrange("p (c t) -> p c t", c=8), ["HA"], [("U", seg)], "pp2" + sfx)
        self.dma("sp", self.HC[:, 0, 0:64], G[0:128, o + 1024:o + 1088], [gk], ["HC"], "hc")
        self.cp("dve", self.HA[:, 0:64], self.HC[:, 0, 0:64], ["HC"], ["HA"])
        self.dma("pool", Um[:, :, 24:32], self.HA[:, 0:64].rearrange("p (c t) -> p c t", c=8), ["HA"], [("Um", seg)], "pp3" + sfx)

    def pool_branch(self, l, seg, ti, meta, T):
        NR = self.NR
        src = self.Um[seg] if meta else self.U[seg]
        c0 = 0 if meta else ti * 512
        W = T + 16
        last = (not meta) and ti == self.NT - 1
        for g in range(4):
            w = (2, 4, 8, 16)[g]
            e = self.UT[:, 0, :, 0:W]
            self.dma("pool", e, src[:, 2 * g:2 * g + 2, c0:c0 + W], [("Um", seg) if meta else ("U", seg)], [("UT", 0)], "ut")
            a = self.UT[:, 1, :, :]
            b = self.UT[:, 2, :, :]
            k0, k1, k2 = ("UT", 0), ("UT", 1), ("UT", 2)
            ev = self.UT[:, 0, :, :]
            if w == 2:
                self.tt("pool", a[:, :, 0:T], ev[:, :, 7:7 + T], ev[:, :, 8:8 + T], ALU.add, [k0], [k1])
                s, sk = a, k1
            elif w == 4:
                self.tt("pool", a[:, :, 0:W - 1], ev[:, :, 0:W - 1], ev[:, :, 1:W], ALU.add, [k0], [k1])
                self.tt("pool", b[:, :, 0:T], a[:, :, 6:6 + T], a[:, :, 8:8 + T], ALU.add, [k1], [k2])
                s, sk = b, k2
            elif w == 8:
                self.tt("pool", a[:, :, 0:W - 1], ev[:, :, 0:W - 1], ev[:, :, 1:W], ALU.add, [k0], [k1])
                self.tt("pool", b[:, :, 0:W - 3], a[:, :, 0:W - 3], a[:, :, 2:W - 1], ALU.add, [k1], [k2])
                self.tt("pool", a[:, :, 0:T], b[:, :, 4:4 + T], b[:, :, 8:8 + T], ALU.add, [k2], [k1])
                s, sk = a, k1
            else:
                self.tt("pool", a[:, :, 0:W - 1], ev[:, :, 0:W - 1], ev[:, :, 1:W], ALU.add, [k0], [k1])
                self.tt("pool", b[:, :, 0:W - 3], a[:, :, 0:W - 3], a[:, :, 2:W - 1], ALU.add, [k1], [k2])
                self.tt("pool", a[:, :, 0:W - 7], b[:, :, 0:W - 7], b[:, :, 4:W - 3], ALU.add, [k2], [k1])
                self.tt("pool", b[:, :, 0:T], a[:, :, 0:T], a[:, :, 8:8 + T], ALU.add, [k1], [k2])
                s, sk = b, k2
            dpk = [("DP", 2 * g), ("DP", 2 * g + 1)]
            out = self.DP[:, 2 * g:2 * g + 2, 0:T]
            if meta:
                for cc in range(2):
                    self.tt("dve", s[:, cc, 0:T], s[:, cc, 0:T], self.INVC[:, 0, g, 0:T], ALU.mult, [sk, "INVC"], [sk])
                self.tt("dve", out, s[:, :, 0:T], ev[:, :, 8:8 + T], ALU.subtract, [sk, k0], dpk)
            else:
                self.stt(out, s[:, :, 0:T], 1.0 / w, ev[:, :, 8:8 + T], ALU.mult, ALU.subtract, [sk, k0], dpk)
                if last:
                    ti_ = 1 if seg == "p" else 2
                    for cc in range(2):
                        self.tt("dve", s[:, cc, T - 8:T], s[:, cc, T - 8:T], self.INVC[:, ti_, g, 0:8], ALU.mult, [sk, "INVC"], [sk])
                    self.tt("dve", self.DP[:, 2 * g:2 * g + 2, T - 8:T], s[:, :, T - 8:T], ev[:, :, T:T + 8], ALU.subtract,
                            [sk, k0], dpk)

    def key_sources(self, l, seg):
        NR, NB = self.NR, self.NB
        out = {}
        for g in range(2):
            lst = []
            if seg == "p":
                for c0 in range(0, NB, 16):
                    n = min(16, NB - c0)
                    lst.append((self.KAl[:, g, c0 * 128:(c0 + n) * 128], self.VAl[:, g, c0:c0 + n, :], n, ["KAl", "VAl"]))
            else:
                for r in range(4):
                    Kr = self.GK[l][g].ap()[r * 128:(r + 1) * 128]
                    Vr = self.GV[l][g].ap()[r * 128:(r + 1) * 128].rearrange("p (n d) -> p n d", d=128)
                    for c0 in range(0, NB, 16):
                        n = min(16, NB - c0)
                        lst.append((Kr[:, c0 * 128:(c0 + n) * 128], Vr[:, c0:c0 + n, :], n, [("GK", l, g), ("GV", l, g)]))
            out[g] = lst
        return out

    def attn_global(self, l, seg, T):
        srcs = self.key_sources(l, seg)
        scale = 128.0 ** 0.5
        si_ = 0 if seg == "p" else 1
        SB = (0, 1, 2, 7)
        obs, dbs = (3, 4), (5, 6)
        for g in range(2):
            for pr in range(2):
                heads = (4 * g + 2 * pr, 4 * g + 2 * pr + 1)
                first = [True, True]
                queue = []

                def flush_one():
                    for (hi, pslot, v_ap, ones_ap, rds) in queue.pop(0):
                        p_ap = self.PT[:, pslot, 0:T]
                        self.mm(self.PS[obs[hi]][:, 0:T], v_ap, p_ap, first[hi], False, rds + [("PT", pslot)], [("ps", obs[hi])])
                        self.mm(self.PS[dbs[hi]][:, 0:T], ones_ap, p_ap, first[hi], False, rds + [("PT", pslot)], [("ps", dbs[hi])])
                        first[hi] = False

                def do_tile(kT, v_ap, ones_ap, rds):
                    items = []
                    for hi, h in enumerate(heads):
                        k_ = self._pt_i % 4
                        self._pt_i += 1
                        sb_ = SB[k_]
                        self.mm(self.PS[sb_][:, 0:T], kT, self.XA[:, h, 0:T], True, True, rds + [("XA", h)], [("ps", sb_)])
                        self.act(self.PT[:, k_, 0:T], self.PS[sb_][:, 0:T], AF.Exp, [("ps", sb_)], [("PT", k_)], scale=scale)
                        items.append((hi, k_, v_ap, ones_ap, rds))
                    queue.append(items)
                    if len(queue) > 1:
                        flush_one()

                do_tile(self.KM[:, si_, g, :], self.VM[:, si_, g, :], self.ones16(), [("KM", seg), ("VM", seg), "CB"])
                for (Kap, Vap, n, rd) in srcs[g]:
                    s = self._kv_i % 2
                    self._kv_i += 1
                    kvk = ("KVR", s)
                    self.dma("sp", self.KVR[:, s, 0:n * 128], Kap, rd, [kvk], "kv%d" % s)
                    self.dma("sp", self.KVR[:, s, 2048:2048 + n * 128], Vap.rearrange("p n d -> p (n d)"), rd, [kvk], "kv%d" % s)
                    for j in range(n):
                        do_tile(self.KVR[:, s, j * 128:(j + 1) * 128], self.KVR[:, s, 2048 + j * 128:2048 + (j + 1) * 128],
                                self.onesb(), [kvk, "CB"])
                while queue:
                    flush_one()
                for hi, h in enumerate(heads):
                    t_r = self.tmp()
                    rd_ = self.TMP[:, t_r, 0:T]
                    self.act(rd_, self.PS[dbs[hi]][:, 0:T], AF.Ln, [("ps", dbs[hi])], [("tmp", t_r)])
                    self.act(rd_, rd_, AF.Exp, [("tmp", t_r)], [("tmp", t_r)], scale=-1.0)
                    self.tt("dve", self.XA[:, 16 + h, 0:T], self.PS[obs[hi]][:, 0:T], rd_, ALU.mult,
                            [("ps", obs[hi]), ("tmp", t_r)], [("XA", 16 + h)])

    def attn_window(self, l, seg, ti, meta, T):
        NB = self.NB
        scale = 128.0 ** -0.5
        si_ = 0 if seg == "p" else 1
        SB = (0, 1, 2, 7)
        if meta:
            e0, nblk = 1, 1
        else:
            e0, nblk = ti * 4, 6
        kvbk = "KVB"
        Kv = self.KVB[:, 0:1536].rearrange("p (g t) -> p g t", g=2)
        Vv = self.KVB[:, 1536:3072].rearrange("p (n g d) -> p n g d", g=2, d=128)
        if meta and seg == "s":
            G = self.GH[l].ap()
            o = 0
            self.dma("sp", Kv[:, :, 0:128], G[0:128, o:o + 256].rearrange("p (g t) -> p g t", g=2), [("GH", l)], [kvbk], "kvb")
            self.dma("sp", Vv[:, 0], G[0:128, o + 512:o + 768].rearrange("p (g d) -> p g d", g=2), [("GH", l)], [kvbk], "kvb")
        else:
            self.dma("sp", Kv[:, :, 0:nblk * 128], self.KBl[seg][:, :, e0 * 128:(e0 + nblk) * 128], [("KBl", seg)], [kvbk], "kvb")
            self.dma("sp", Vv[:, 0:nblk], self.VBl[seg][:, e0:e0 + nblk], [("VBl", seg)], [kvbk], "kvb")
        nqb = 1 if meta else 4
        QW = NM if meta else 128
        N4 = 4 * QW
        for qb in range(nqb):
            for g in range(2):
                qv = self.XA[:, 8 + 4 * g:8 + 4 * g + 4, qb * 128:qb * 128 + QW]
                qk = [("XA", 8 + 4 * g + i) for i in range(4)]
                ob, db = 3 + (g % 2), 5 + (g % 2)
                O = self.PS[ob][:, 0:N4].rearrange("p (a b) -> p a b", a=4)
                Dn = self.PS[db][:, 0:N4].rearrange("p (a b) -> p a b", a=4)
                kts = [(self.KM[:, si_, 2 + g, :], self.VM[:, si_, 2 + g, :], self.ones16(), None, [("KM", seg), ("VM", seg), "CB"])]
                if meta:
                    kts.append((Kv[:, g, 0:128], Vv[:, 0, g, :], self.onesb(), self.maskmeta(), [kvbk, "CB"]))
                else:
                    b = ti * 4 + qb
                    for d_, mi in ((0, 0), (1, None), (2, 1)):
                        if seg == "p" and ((b == 0 and d_ == 0) or (b == NB - 1 and d_ == 2)):
                            continue
                        m = None
                        if mi is not None:
                            if seg == "s" and b == 0 and d_ == 0:
                                m = self.mask(2)
                            elif seg == "s" and b == NB - 1 and d_ == 2:
                                m = self.mask(3)
                            else:
                                m = self.mask(mi)
                        j = qb + d_
                        kts.append((Kv[:, g, j * 128:(j + 1) * 128], Vv[:, j, g, :], self.onesb(), m, [kvbk, "CB"]))
                pslots = []
                for (kT, v_ap, ones_ap, m, rds) in kts:
                    k_ = self._pt_i % 4
                    self._pt_i += 1
                    sb_ = SB[k_]
                    Sb = self.PS[sb_][:, 0:N4].rearrange("p (a b) -> p a b", a=4)
                    self.mm(Sb, kT, qv, True, m is None, rds + qk, [("ps", sb_)])
                    if m is not None:
                        self.mm(Sb, self.identb(), m, False, True, ["CB"], [("ps", sb_)])
                    self.act(self.PT[:, k_, 0:N4], self.PS[sb_][:, 0:N4], AF.Exp, [("ps", sb_)], [("PT", k_)], scale=scale)
                    pslots.append((k_, v_ap, ones_ap, rds))
                for i, (sb_, v_ap, ones_ap, rds) in enumerate(pslots):
                    p_ap = self.PT[:, sb_, 0:N4].rearrange("p (a b) -> p a b", a=4)
                    self.mm(O, v_ap, p_ap, i == 0, False, rds + [("PT", sb_)], [("ps", ob)])
                    self.mm(Dn, ones_ap, p_ap, i == 0, False, rds + [("PT", sb_)], [("ps", db)])
                t_r = self.tmp()
                dt = self.TMP[:, t_r, 0:N4].rearrange("p (a b) -> p a b", a=4)
                for hh in range(4):
                    self.act(dt[:, hh, :], Dn[:, hh, :], AF.Ln, [("ps", db), "ESK"], [("tmp", t_r)],
                             bias=self.ESK[:, 4 * g + hh:4 * g + hh + 1])
                self.act(self.TMP[:, t_r, 0:N4], self.TMP[:, t_r, 0:N4], AF.Exp, [("tmp", t_r)], [("tmp", t_r)], scale=-1.0)
                self.tt("dve", self.OB[:, 4 * g:4 * g + 4, qb * 128:qb * 128 + QW], O, dt, ALU.mult,
                        [("ps", ob), ("tmp", t_r)], [("OB", 4 * g + i) for i in range(4)])

    def layer_norm(self, l, T, gcol0, bcol0, hb_slot, store=None):
        b1, b2 = self.bank(), self.bank()
        for c in range(KC):
            self.mm(self.PS[b1][:, 0:T], self.ones32(), self.H32[:, c, 0:T], c == 0, c == KC - 1,
                    ["C32", ("H32", c)], [("ps", b1)])
        for c in range(KC):
            t_ = self.tmp()
            self.act(self.TMP[:, t_, 0:T], self.H32[:, c, 0:T], AF.Square, [("H32", c)], [("tmp", t_)])
            self.mm(self.PS[b2][:, 0:T], self.ones32(), self.TMP[:, t_, 0:T], c == 0, c == KC - 1,
                    ["C32", ("tmp", t_)], [("ps", b2)])
        tm, tv = 5, 6
        m = self.TMP[:, tm, 0:T]
        v = self.TMP[:, tv, 0:T]
        self.ts("dve", m, self.PS[b1][:, 0:T], 1.0 / D, ALU.mult, [("ps", b1)], [("tmp", tm)])
        self.tt("dve", v, m, m, ALU.mult, [("tmp", tm)], [("tmp", tv)])
        self.stt(v, self.PS[b2][:, 0:T], 1.0 / D, v, ALU.mult, ALU.subtract, [("ps", b2), ("tmp", tv)], [("tmp", tv)])
        self.act(v, v, AF.Ln, [("tmp", tv), "VEC"], [("tmp", tv)], bias=self.VEC[:, 2 * NVL + 1:2 * NVL + 2])
        self.act(v, v, AF.Exp, [("tmp", tv)], [("tmp", tv)], scale=-0.5)
        for c in range(KC):
            t_ = self.tmp()
            x = self.TMP[:, t_, 0:T]
            self.tt("dve", x, self.H32[:, c, 0:T], m, ALU.subtract, [("H32", c), ("tmp", tm)], [("tmp", t_)])
            self.tt("dve", x, x, v, ALU.mult, [("tmp", t_), ("tmp", tv)], [("tmp", t_)])
            self.act(self.H32[:, c, 0:T], x, AF.Identity, [("tmp", t_), "VEC"], [("H32", c)],
                     scale=self.vec(l, gcol0 + c), bias=self.vec(l, bcol0 + c))
            if hb_slot is not None:
                self.cp("dve", self.HB[:, hb_slot, c, 0:T], self.H32[:, c, 0:T], [("H32", c)], [("HB", hb_slot, c)])

    def prefetch_hb(self, l, seg, ti, meta):
        T = NM if meta else 512
        t0 = 0 if meta else ti * 512
        slot = self._hb_par
        self._hb_par ^= 1
        self.load_hb(self.hsrc(l, seg, t0, T, meta), slot, T, self.hkey(l, seg))
        self._pref[(l, seg, ti, meta)] = slot

    def phase2_tile(self, l, seg, ti, meta, nxt=None):
        NR = self.NR
        T = NM if meta else 512
        t0 = 0 if meta else ti * 512
        if (l, seg, ti, meta) not in self._pref:
            self.prefetch_hb(l, seg, ti, meta)
        slot = self._pref.pop((l, seg, ti, meta))
        src = self.hsrc(l, seg, t0, T, meta)
        self.load_rope(seg, 0 if meta else NM + t0, T)
        h32k = [("H32", c) for c in range(KC)]
        hbk = [("HB", slot, c) for c in range(KC)]
        hb = self.HB[:, slot]
        wi = self.wi_b[l]
        for which, col0, xbase, gcol in (("A", 0, 0, self.vec(l, 0)), ("B", 1536, 8, None)):
            for half in range(2):
                kw, (xw,) = self.wload([wi[:, :, col0 + half * 512:col0 + (half + 1) * 512]])
                for hh in range(4):
                    h = half * 4 + hh
                    bk = self.bank()
                    for kc in range(KC):
                        self.mm(self.PS[bk][:, 0:T], xw[:, kc, hh * 128:(hh + 1) * 128], hb[:, kc, 0:T],
                                kc == 0, kc == KC - 1, [kw, hbk[kc]], [("ps", bk)])
                    self.head_chain(bk, T, which, gcol, self.XA[:, xbase + h, 0:T], [("XA", xbase + h)])
        self.pool_branch(l, seg, ti, meta, T)
        self.attn_global(l, seg, T)
        self.attn_window(l, seg, ti, meta, T)
        self.dma("sp", self.H32[:, :, 0:T], src, self.hkey(l, seg), h32k, "h32")
        if nxt is not None:
            self.prefetch_hb(l, seg, nxt[0], nxt[1])
        for c in range(KC):
            kg, (xgf,) = self.wload([self.wg_b[l][:, c].rearrange("p i k n -> p (i k n)")])
            xg4 = xgf.rearrange("p (i k n) -> p i k n", i=3, k=KC)
            xgs = [xg4[:, i] for i in range(3)]
            kb_, (xabf,) = self.wload([self.wab_b[l][:, c].rearrange("p i k n -> p (i k n)")])
            xab4 = xabf.rearrange("p (i k n) -> p i k n", i=2, k=KC)
            xa_, xb_ = xab4[:, 0], xab4[:, 1]
            gts = []
            for i, (xw, wk) in enumerate(((xgs[0], kg), (xgs[1], kg), (xgs[2], kg))):
                bk = self.bank()
                for kc in range(KC):
                    self.mm(self.PS[bk][:, 0:T], xw[:, kc, :], hb[:, kc, 0:T], kc == 0, kc == KC - 1,
                            [wk, hbk[kc]], [("ps", bk)])
                t_ = self.tmp()
                self.act(self.TMP[:, t_, 0:T], self.PS[bk][:, 0:T], AF.Sigmoid, [("ps", bk)], [("tmp", t_)])
                gts.append(t_)
            ba, bb_, bc = self.bank(), self.bank(), self.bank()
            for kc in range(KC):
                self.mm(self.PS[ba][:, 0:T], xa_[:, kc, :], self.XA[:, 16 + kc, 0:T], kc == 0, kc == KC - 1,
                        [kb_, ("XA", 16 + kc)], [("ps", ba)])
            for kc in range(KC):
                self.mm(self.PS[bb_][:, 0:T], xb_[:, kc, :], self.OB[:, kc, 0:T], kc == 0, kc == KC - 1,
                        [kb_, ("OB", kc)], [("ps", bb_)])
            g_ = c // 2
            e_ = c % 2
            for kc in range(2):
                self.mm(self.PS[bc][:, 0:T], self.PW[:, g_, kc, e_ * 128:(e_ + 1) * 128], self.DP[:, 2 * g_ + kc, 0:T],
                        kc == 0, kc == 1, ["PW", ("DP", 2 * g_ + kc)], [("ps", bc)])
            ta, tb_, tc = gts
            A_ = self.TMP[:, ta, 0:T]
            B_ = self.TMP[:, tb_, 0:T]
            C_ = self.TMP[:, tc, 0:T]
            self.tt("dve", A_, self.PS[ba][:, 0:T], A_, ALU.mult, [("ps", ba), ("tmp", ta)], [("tmp", ta)])
            self.tt("dve", B_, self.PS[bb_][:, 0:T], B_, ALU.mult, [("ps", bb_), ("tmp", tb_)], [("tmp", tb_)])
            self.stt(C_, self.PS[bc][:, 0:T], self.vec(l, 2 + c), C_, ALU.mult, ALU.mult, [("ps", bc), ("tmp", tc), "VEC"], [("tmp", tc)])
            self.tt("pool", A_, A_, B_, ALU.add, [("tmp", ta), ("tmp", tb_)], [("tmp", ta)])
            self.tt("dve", self.XA[:, c, 0:T], A_, C_, ALU.add, [("tmp", ta), ("tmp", tc)], [("XA", c)])
        for half in range(2):
            kw, (xw,) = self.wload([self.wo_b[l][:, :, half * 512:(half + 1) * 512]])
            for cc in range(4):
                c = half * 4 + cc
                bk = self.bank()
                for kc in range(KC):
                    self.mm(self.PS[bk][:, 0:T], xw[:, kc, cc * 128:(cc + 1) * 128], self.XA[:, kc, 0:T], kc == 0, kc == KC - 1,
                            [kw, ("XA", kc)], [("ps", bk)])
                self.stt(self.H32[:, c, 0:T], self.H32[:, c, 0:T], ALPHA, self.PS[bk][:, 0:T], ALU.mult, ALU.add,
                         [("H32", c), ("ps", bk)], [("H32", c)])
        self.layer_norm(l, T, 10, 18, slot)
        wu = self.wu_b[l]
        for j0 in range(0, FC, 2):
            kw, (xuf,) = self.wload([self.wu2_b[l][:, j0 // 2].rearrange("p i k n -> p (i k n)")])
            xu4 = xuf.rearrange("p (i k n) -> p i k n", i=2, k=KC)
            xg, xu = xu4[:, 0], xu4[:, 1]
            for jj in range(2):
                j = j0 + jj
                bg, bu = self.bank(), self.bank()
                for kc in range(KC):
                    self.mm(self.PS[bg][:, 0:T], xg[:, kc, jj * 128:(jj + 1) * 128], hb[:, kc, 0:T], kc == 0, kc == KC - 1,
                            [kw, hbk[kc]], [("ps", bg)])
                for kc in range(KC):
                    self.mm(self.PS[bu][:, 0:T], xu[:, kc, jj * 128:(jj + 1) * 128], hb[:, kc, 0:T], kc == 0, kc == KC - 1,
                            [kw, hbk[kc]], [("ps", bu)])
                t_ = self.tmp()
                self.act(self.TMP[:, t_, 0:T], self.PS[bg][:, 0:T], AF.Silu, [("ps", bg)], [("tmp", t_)])
                self.tt("dve", self.XA[:, j, 0:T], self.TMP[:, t_, 0:T], self.PS[bu][:, 0:T], ALU.mult,
                        [("tmp", t_), ("ps", bu)], [("XA", j)])
        for c in range(KC):
            kw, (xwf,) = self.wload([self.wd_b[l][:, c].rearrange("p k n -> p (k n)")])
            xw = xwf.rearrange("p (k n) -> p k n", k=FC)
            bk = self.bank()
            for j in range(FC):
                self.mm(self.PS[bk][:, 0:T], xw[:, j, :], self.XA[:, j, 0:T], j == 0, j == FC - 1,
                        [kw, ("XA", j)], [("ps", bk)])
            self.stt(self.H32[:, c, 0:T], self.H32[:, c, 0:T], ALPHA, self.PS[bk][:, 0:T], ALU.mult, ALU.add,
                     [("H32", c), ("ps", bk)], [("H32", c)])
        self.layer_norm(l, T, 26, 34, None)
        if l == 0:
            dst = self.H1[seg][:, :, 0:NM] if meta else self.H1[seg][:, :, NM + t0:NM + t0 + T]
            self.dma("pool", dst, self.H32[:, :, 0:T], h32k, [("H1", seg)], "st_h")
            if self.debug:
                dd = self.dbg[seg][:, :, 0:NM] if meta else self.dbg[seg][:, :, NM + t0:NM + t0 + T]
                self.dma("pool", dd, self.H32[:, :, 0:T], h32k, ["out"], "st_d")
        else:
            self.dma("pool", self.y[seg][:, :, t0:t0 + T], self.H32[:, :, 0:T], h32k, ["out"], "st_h")

    def phase2(self, l, seg):
        tiles = ([(0, True)] if l == 0 else []) + [(ti, False) for ti in range(self.NT)]
        for i, (ti, meta) in enumerate(tiles):
            self.phase2_tile(l, seg, ti, meta, tiles[i + 1] if i + 1 < len(tiles) else None)

    def layer_consts(self, l):
        self.dma("sp", self.PW[:], self.pw_b[l], ["wts"], ["PW"], "pw")
        self.act(self.ESK[:], self.VEC[:, l * NVL + 42:l * NVL + 50], AF.Exp, ["VEC"], ["ESK"])

    def build(self):
        with contextlib.ExitStack() as st:
            self.declare(st)
            self._hb_par = 0
            self._pref = {}
            self.load_consts()
            import os
            if os.environ.get("KNOCONV", "0") != "1":
                self.convert_weights()
            import os
            stop = int(os.environ.get("KSTOP", "99"))
            for l in range(DEPTH):
                if stop <= 0:
                    break
                self.layer_consts(l)
                self.phase1(l, "s")
                if stop <= 1:
                    break
                self.allgather(l)
                if stop <= 2:
                    break
                self.phase1(l, "p")
                self.prep(l, "p")
                if stop <= 3:
                    break
                self.phase2(l, "p")
                if stop <= 4:
                    break
                self.prep(l, "s")
                if stop <= 5:
                    break
                self.phase2(l, "s")
                if stop <= 6:
                    break
            self.S.add("sp", lambda e: None, reads=["out"])
            self.S.emit(self.nc, st)
        return self.nc


def _rope_tables(NR, q):
    theta = np.float32(10000.0)
    s = (np.arange(NR, dtype=np.int64) + q * NR)
    row = np.concatenate([-np.ones(NM, np.int64), s // 64]).astype(np.float32)
    col = np.concatenate([np.arange(NM, dtype=np.int64), s % 64]).astype(np.float32)
    pos = np.concatenate([np.arange(NM, dtype=np.int64), NM + s]).astype(np.float32)
    inv32 = (theta ** (-np.arange(0, 64, 2, dtype=np.float32) / np.float32(64))).astype(np.float32)
    inv64 = (theta ** (-np.arange(0, 128, 2, dtype=np.float32) / np.float32(128))).astype(np.float32)
    out = np.zeros((128, 4, NM + NR), np.float32)
    for base, p in ((0, row), (64, col)):
        ang = (p[None, :] * inv32[:, None]).astype(np.float32)
        c, sn = np.cos(ang).astype(np.float32), np.sin(ang).astype(np.float32)
        out[base:base + 32, 0] = c
        out[base + 32:base + 64, 0] = c
        out[base:base + 32, 1] = -sn
        out[base + 32:base + 64, 1] = sn
    ang = (pos[None, :] * inv64[:, None]).astype(np.float32)
    c, sn = np.cos(ang).astype(np.float32), np.sin(ang).astype(np.float32)
    out[0:64, 2] = c
    out[64:128, 2] = c
    out[0:64, 3] = -sn
    out[64:128, 3] = sn
    return out


def _consts():
    c32 = np.zeros((128, 3, 128), np.float32)
    c32[:, 0, :] = 1.0
    for d in range(128):
        srcA = d + 32 if (d % 64) < 32 else d - 32
        srcB = d + 64 if d < 64 else d - 64
        c32[srcA, 1, d] = 1.0
        c32[srcB, 2, d] = 1.0
    return c32


def _cb16(q, is_first_valid, is_last_valid):
    NCB = 3 * 128 + 4 * 512 + 64
    cb = np.zeros((128, NCB), np.float32)
    cb[:, 0:128] = np.eye(128, dtype=np.float32)
    cb[:, 128:256] = 1.0
    cb[0:NM, 256:384] = 1.0
    k = np.arange(128)[:, None]
    qq = np.arange(128)[None, :]
    mprev = np.where(k >= qq, 0.0, NEGM).astype(np.float32)
    mnext = np.where(k <= qq, 0.0, NEGM).astype(np.float32)
    mpf = mprev if is_first_valid else np.full((128, 128), NEGM, np.float32)
    mnl = mnext if is_last_valid else np.full((128, 128), NEGM, np.float32)
    for i, m in enumerate((mprev, mnext, mpf, mnl)):
        cb[:, 384 + i * 512:384 + (i + 1) * 512] = np.tile(m, (1, 4))
    qm = np.arange(NM)[None, :]
    mm = np.where(k <= 112 + qm, 0.0, NEGM).astype(np.float32)
    cb[:, 384 + 4 * 512:384 + 4 * 512 + 64] = np.tile(mm, (1, 4))
    return cb.astype(ml_dtypes.bfloat16)


def _invc(NR, L_s, q):
    out = np.zeros((3, 4, 16), np.float32)
    wins = (2, 4, 8, 16)
    Lp = NM + NR
    for g, w in enumerate(wins):
        for t in range(NM):
            out[0, g, t] = 1.0 / (min(Lp, t + w // 2) - max(0, t - w // 2))
        for i in range(8):
            tp = Lp - 8 + i
            out[1, g, i] = 1.0 / (min(Lp, tp + w // 2) - max(0, tp - w // 2))
            ts_ = NM + (q + 1) * NR - 8 + i
            out[2, g, i] = 1.0 / (min(L_s, ts_ + w // 2) - max(0, ts_ - w // 2))
    return np.broadcast_to(out[None], (128, 3, 4, 16)).copy()


def _fm(a):
    t = a.shape[0]
    return np.ascontiguousarray(a.reshape(t, KC, 128).transpose(2, 1, 0))


def _vecs(inp):
    v = np.zeros((128, 2 * NVL + 2), np.float32)
    for l in range(DEPTH):
        b = l * NVL
        v[:, b + 0] = inp["q_norm_g"][l]
        v[:, b + 1] = inp["k_norm_g"][l]
        for name, c0 in (("pool_scale", 2), ("ln1_g", 10), ("ln1_b", 18), ("ln2_g", 26), ("ln2_b", 34)):
            v[:, b + c0:b + c0 + 8] = inp[name][l].reshape(KC, 128).T
        v[:, b + 42:b + 50] = inp["sink_logit"][l][None, :]
    v[:, 2 * NVL] = 128.0 * 1e-6
    v[:, 2 * NVL + 1] = 1e-5
    return v


_NC_CACHE = {}


def _run(inp, debug=False):
    xp = np.asarray(inp["x_prompt"], np.float32)
    xs = np.asarray(inp["x_sample"], np.float32)
    NR = xp.shape[1]
    assert xp.shape[0] == 8 and xs.shape[0] == 2 and xs.shape[1] == 4 * NR
    key = (NR, debug)
    if key not in _NC_CACHE:
        _NC_CACHE[key] = Builder(NR, debug).build()
    nc = _NC_CACHE[key]
    f = lambda n: np.ascontiguousarray(np.asarray(inp[n], np.float32))
    shared = {
        "xm": _fm(np.asarray(inp["meta_tokens"], np.float32)),
        "w_in": f("w_in"), "w_ba": f("w_branch_a"), "w_bb": f("w_branch_b"), "w_out": f("w_out"),
        "w_up": f("w_up"), "w_down": f("w_down"), "pool_w": f("pool_w"),
        "vecs": _vecs({k: np.asarray(v, np.float32) for k, v in inp.items()}),
        "rope_p": _rope_tables(NR, 0), "c32": _consts(),
    }
    L_s = NM + 4 * NR
    in_maps = []
    for c in range(8):
        q = c % 4
        bl = np.zeros((128, 16), np.float32)
        if q > 0:
            bl[:, q - 1] = 1.0
        else:
            bl[:, 8] = 1.0
        if q < 3:
            bl[:, 4 + q + 1] = 1.0
        m = dict(shared)
        m["xp"] = _fm(xp[c])
        m["xs"] = _fm(xs[c // 4, q * NR:(q + 1) * NR])
        m["rope_s"] = _rope_tables(NR, q)
        m["cb16"] = _cb16(q, q > 0, q < 3)
        m["blend"] = bl
        m["invc"] = _invc(NR, L_s, q)
        in_maps.append(m)
    res = run_bass_kernel_spmd(nc, in_maps, core_ids=list(range(8)))
    r = res.results

    def unfm(a):
        return np.ascontiguousarray(a.transpose(2, 1, 0).reshape(a.shape[2], D))

    y_p = np.stack([unfm(r[c]["yp"]) for c in range(8)], 0).astype(np.float32)
    y_s = np.stack([np.concatenate([unfm(r[4 * b + q]["ys"]) for q in range(4)], 0) for b in range(2)], 0).astype(np.float32)
    if debug:
        return (y_p, y_s), r
    return (y_p, y_s)


def kernel(**inputs):
    return _run(inputs)
```

```python
import contextlib
import numpy as np
import ml_dtypes
import concourse.bass as bass
import concourse.mybir as mybir
from concourse.bass_utils import run_bass_kernel_spmd

F32 = mybir.dt.float32
BF16 = mybir.dt.bfloat16
AF = mybir.ActivationFunctionType
ALU = mybir.AluOpType

D = 1024
KC = 8
NM = 16
DFF = 2816
FC = 22
INW = 7168
DEPTH = 2
ALPHA = (2 * DEPTH) ** 0.25
NVL = 50
NEGM = -30000.0


class Op:
    __slots__ = ("eng", "fn", "deps", "dma", "sem_key", "signal", "sig_idx", "dma_waits", "inc")

    def __init__(self, eng, fn, dma, sem_key, inc):
        self.eng = eng
        self.fn = fn
        self.deps = []
        self.dma = dma
        self.sem_key = sem_key
        self.signal = False
        self.sig_idx = 0
        self.dma_waits = {}
        self.inc = inc


class Sched:
    ENGS = ("pe", "act", "dve", "pool", "sp")
    EPOCH = 30000

    def __init__(self):
        self.ops = {e: [] for e in self.ENGS}
        self.last_writer = {}
        self.readers = {}
        self.dma_count = {}
        self.nops = 0

    def add(self, eng, fn, reads=(), writes=(), dma=False, sem_key=None, inc=16):
        op = Op(eng, fn, dma, sem_key, inc)
        deps = {}
        for k in reads:
            w = self.last_writer.get(k)
            if w is not None:
                deps[id(w)] = w
        for k in writes:
            w = self.last_writer.get(k)
            if w is not None:
                if not (w.eng == eng and not w.dma and not dma):
                    deps[id(w)] = w
            last_by_eng = {}
            for r in self.readers.get(k, ()):
                if r.eng == eng and not r.dma and not dma:
                    continue
                if r.dma or r.eng == "pool":
                    deps[id(r)] = r
                else:
                    last_by_eng[r.eng] = r
            for r in last_by_eng.values():
                deps[id(r)] = r
        if eng == "pe":
            deps = {i: d for i, d in deps.items() if d.eng != "pe" or d.dma}
        if dma:
            deps = {i: d for i, d in deps.items() if not (d.dma and d.sem_key == sem_key)}
        op.deps = list(deps.values())
        for d in op.deps:
            if d.dma:
                c = self.dma_count[d.sem_key]
                if op.dma_waits.get(d.sem_key, 0) < c:
                    op.dma_waits[d.sem_key] = c
        if dma:
            self.dma_count[sem_key] = self.dma_count.get(sem_key, 0) + inc
        for k in reads:
            self.readers.setdefault(k, []).append(op)
        for k in writes:
            self.last_writer[k] = op
            self.readers[k] = []
        self.ops[eng].append(op)
        self.nops += 1
        return op

    def emit(self, nc, stack):
        for e in self.ENGS:
            for op in self.ops[e]:
                for d in op.deps:
                    if not d.dma:
                        d.signal = True
        nsig = {}
        for e in self.ENGS:
            c = 0
            for op in self.ops[e]:
                if op.signal and not op.dma:
                    c += 1
                    op.sig_idx = c
            nsig[e] = c
        esem = {}
        for e in self.ENGS:
            n_ep = max(1, (nsig[e] + self.EPOCH - 1) // self.EPOCH)
            esem[e] = [stack.enter_context(nc.semaphore(f"s_{e}_{i}")) for i in range(n_ep)]
        dsem = {}
        for i, k in enumerate(self.dma_count):
            dsem[k] = stack.enter_context(nc.semaphore(f"d{i}"))
        self.n_sems = sum(len(v) for v in esem.values()) + len(dsem)
        block = stack.enter_context(nc.Block())
        EP = self.EPOCH

        def run(e, engine):
            waited = {}
            dwaited = {}
            for op in self.ops[e]:
                need = {}
                for d in op.deps:
                    if d.dma:
                        continue
                    if need.get(d.eng, 0) < d.sig_idx:
                        need[d.eng] = d.sig_idx
                for se, idx in need.items():
                    if waited.get(se, 0) >= idx:
                        continue
                    waited[se] = idx
                    ep = (idx - 1) // EP
                    engine.wait_ge(esem[se][ep], idx - ep * EP)
                for sk, cnt in op.dma_waits.items():
                    if dwaited.get(sk, 0) >= cnt:
                        continue
                    dwaited[sk] = cnt
                    engine.wait_ge(dsem[sk], cnt)
                ins = op.fn(engine)
                if ins is None:
                    if e == "sp":
                        for sk, cnt in self.dma_count.items():
                            engine.wait_ge(dsem[sk], cnt)
                        for se in self.ENGS:
                            if nsig[se] > 0:
                                ep = (nsig[se] - 1) // EP
                                engine.wait_ge(esem[se][ep], nsig[se] - ep * EP)
                    continue
                if op.dma:
                    ins.then_inc(dsem[op.sem_key], op.inc)
                elif op.signal:
                    ep = (op.sig_idx - 1) // EP
                    ins.then_inc(esem[e][ep], 1)

        @block.tensor
        def _(t):
            run("pe", t)

        @block.scalar
        def _(t):
            run("act", t)

        @block.vector
        def _(t):
            run("dve", t)

        @block.gpsimd
        def _(t):
            run("pool", t)

        @block.sync
        def _(t):
            run("sp", t)


class Builder:
    def __init__(self, NR, debug=False):
        assert NR % 512 == 0
        self.NR = NR
        self.NT = NR // 512
        self.NB = NR // 128
        self.XW = 4 * NR + 1152
        self.debug = debug
        self.S = Sched()
        self.nc = bass.Bass("TRN2", target_bir_lowering=False)
        self._tmp_i = 0
        self._ps_i = 0
        self._wr_i = 0
        self._kv_i = 0
        self._pt_i = 0

    def mm(self, out, lhsT, rhs, start, stop, reads, writes):
        self.S.add("pe", lambda e, o=out, l=lhsT, r=rhs, a=start, b=stop:
                   e.matmul(o, lhsT=l, rhs=r, start=a, stop=b, skip_group_check=True),
                   reads=reads, writes=writes)

    def act(self, out, in_, func, reads, writes, scale=None, bias=None):
        kw = {}
        if scale is not None:
            kw["scale"] = scale
        if bias is not None:
            kw["bias"] = bias
        self.S.add("act", lambda e, o=out, i=in_, f=func, kw=kw: e.activation(out=o, in_=i, func=f, **kw),
                   reads=reads, writes=writes)

    def tt(self, eng, out, in0, in1, op, reads, writes):
        self.S.add(eng, lambda e, o=out, a=in0, b=in1, p=op: e.tensor_tensor(out=o, in0=a, in1=b, op=p),
                   reads=reads, writes=writes)

    def ts(self, eng, out, in0, s1, op0, reads, writes, s2=None, op1=None):
        if op1 is None:
            self.S.add(eng, lambda e, o=out, a=in0, s=s1, p=op0: e.tensor_scalar(out=o, in0=a, scalar1=s, scalar2=None, op0=p),
                       reads=reads, writes=writes)
        else:
            self.S.add(eng, lambda e, o=out, a=in0, s=s1, p=op0, t=s2, q=op1:
                       e.tensor_scalar(out=o, in0=a, scalar1=s, scalar2=t, op0=p, op1=q),
                       reads=reads, writes=writes)

    def stt(self, out, in0, scalar, in1, op0, op1, reads, writes):
        self.S.add("dve", lambda e, o=out, a=in0, s=scalar, b=in1, p=op0, q=op1:
                   e.scalar_tensor_tensor(out=o, in0=a, scalar=s, in1=b, op0=p, op1=q),
                   reads=reads, writes=writes)

    def cp(self, eng, out, in_, reads, writes):
        if eng == "act":
            self.S.add("act", lambda e, o=out, i=in_: e.copy(out=o, in_=i), reads=reads, writes=writes)
        else:
            self.S.add(eng, lambda e, o=out, i=in_: e.tensor_copy(out=o, in_=i), reads=reads, writes=writes)

    def recip(self, out, in_, reads, writes):
        self.S.add("dve", lambda e, o=out, i=in_: e.reciprocal(out=o, in_=i), reads=reads, writes=writes)

    def memset(self, eng, ap, val, writes):
        self.S.add(eng, lambda e, a=ap, v=val: e.memset(a, v), writes=writes)

    def dma(self, q, out, in_, reads, writes, sem_key):
        self.S.add(q, lambda e, o=out, i=in_: e.dma_start(out=o, in_=i), reads=reads, writes=writes,
                   dma=True, sem_key=sem_key)

    def tmp(self):
        i = self._tmp_i % 5
        self._tmp_i += 1
        return i

    def bank(self):
        i = self._ps_i % 8
        self._ps_i += 1
        return i

    def declare(self, st):
        nc, NR = self.nc, self.NR

        def din(name, shape, dt=F32):
            return nc.dram_tensor(name, shape, dt, kind="ExternalInput").ap()

        def dout(name, shape, dt=F32):
            return nc.dram_tensor(name, shape, dt, kind="ExternalOutput").ap()

        def dscr(name, shape, dt):
            return nc.dram_tensor(name, shape, dt)

        self.x = {"p": din("xp", [128, KC, NR]), "s": din("xs", [128, KC, NR])}
        self.xm = din("xm", [128, KC, NM])
        self.w_in = din("w_in", [DEPTH, D, INW])
        self.w_ba = din("w_ba", [DEPTH, D, D])
        self.w_bb = din("w_bb", [DEPTH, D, D])
        self.w_out = din("w_out", [DEPTH, D, D])
        self.w_up = din("w_up", [DEPTH, D, 2 * DFF])
        self.w_down = din("w_down", [DEPTH, DFF, D])
        self.pool_w = din("pool_w", [DEPTH, 4, 256, 256])
        self.vecs_d = din("vecs", [128, 2 * NVL + 2])
        self.rope = {"p": din("rope_p", [128, 4, NM + NR]), "s": din("rope_s", [128, 4, NM + NR])}
        self.c32_d = din("c32", [128, 3, 128])
        self.NCB = 3 * 128 + 4 * 512 + 64
        self.cb16_d = din("cb16", [128, self.NCB], BF16)
        self.blend_d = din("blend", [128, 16])
        self.invc_d = din("invc", [128, 3, 4, 16])
        self.y = {"p": dout("yp", [128, KC, NR]), "s": dout("ys", [128, KC, NR])}
        self.wi_b = dscr("wi_b", [DEPTH, 128, KC, INW], BF16).ap()
        self.wba_b = dscr("wba_b", [DEPTH, 128, KC, D], BF16).ap()
        self.wbb_b = dscr("wbb_b", [DEPTH, 128, KC, D], BF16).ap()
        self.wo_b = dscr("wo_b", [DEPTH, 128, KC, D], BF16).ap()
        self.wu_b = dscr("wu_b", [DEPTH, 128, KC, 2 * DFF], BF16).ap()
        self.wd_b = dscr("wd_b", [DEPTH, 128, KC, FC, 128], BF16).ap()
        self.pw_b = dscr("pw_b", [DEPTH, 128, 4, 2, 256], BF16).ap()
        self.wg_b = dscr("wg_b", [DEPTH, 128, KC, 3, KC, 128], BF16).ap()
        self.wab_b = dscr("wab_b", [DEPTH, 128, KC, 2, KC, 128], BF16).ap()
        self.wu2_b = dscr("wu2_b", [DEPTH, 128, FC // 2, 2, KC, 256], BF16).ap()
        self.H1 = {s: dscr("h1_" + s, [128, KC, NM + NR], F32).ap() for s in "ps"}
        self.KAl = dscr("kal", [128, 2, NR], BF16).ap()
        self.VAl = dscr("val", [128, 2, self.NB, 128], BF16).ap()
        self.KBl = {s: dscr("kbl_" + s, [128, 2, (self.NB + 2) * 128], BF16).ap() for s in "ps"}
        self.VBl = {s: dscr("vbl_" + s, [128, self.NB + 2, 2, 128], BF16).ap() for s in "ps"}
        self.U = {s: dscr("u_" + s, [128, KC, NR + 16], F32).ap() for s in "ps"}
        self.Um = {s: dscr("um_" + s, [128, KC, 32], F32).ap() for s in "ps"}
        self.CK = [[dscr(f"ck{l}{g}", [128, NR], BF16) for g in range(2)] for l in range(DEPTH)]
        self.CV = [[dscr(f"cv{l}{g}", [128, NR], BF16) for g in range(2)] for l in range(DEPTH)]
        self.CH = [dscr(f"ch{l}", [128, 1152], BF16) for l in range(DEPTH)]
        self.GK = [[dscr(f"gk{l}{g}", [512, NR], BF16) for g in range(2)] for l in range(DEPTH)]
        self.GV = [[dscr(f"gv{l}{g}", [512, NR], BF16) for g in range(2)] for l in range(DEPTH)]
        self.GH = [dscr(f"gh{l}", [512, 1152], BF16) for l in range(DEPTH)]
        if self.debug:
            self.dbg = {s: dout("dbg_" + s, [128, KC, NM + NR]) for s in "ps"}

        def sb(name, shape, dt):
            return st.enter_context(nc.sbuf_tensor(name, shape, dt))

        self.WR = sb("WR", [128, 4, 4096], BF16)
        self.KVR = sb("KVR", [128, 2, 4096], BF16)
        self.KVB = sb("KVB", [128, 3072], BF16)
        self.H32 = sb("H32", [128, KC, 512], F32)
        self.LD32 = sb("LD32", [128, 4, 512], F32)
        self.HB = sb("HB", [128, 2, KC, 512], BF16)
        self.XA = sb("XA", [128, 24, 512], BF16)
        self.OB = sb("OB", [128, 8, 512], BF16)
        self.DP = sb("DP", [128, 8, 512], BF16)
        self.TMP = sb("TMP", [128, 8, 512], F32)
        self.RT = sb("RT", [128, 4, 512], F32)
        self.PT = sb("PT", [128, 4, 512], BF16)
        self.UT = sb("UT", [128, 3, 2, 528], F32)
        self.KST = sb("KST", [128, 2, 2, 512], BF16)
        self.VST = sb("VST", [128, 4, 512], BF16)
        self.C32 = sb("C32", [128, 3, 128], F32)
        self.CB = sb("CB", [128, self.NCB], BF16)
        self.VEC = sb("VEC", [128, 2 * NVL + 2], F32)
        self.BL = sb("BL", [128, 16], F32)
        self.INVC = sb("INVC", [128, 3, 4, 16], F32)
        self.PW = sb("PW", [128, 4, 2, 256], BF16)
        self.KM = sb("KM", [128, 2, 4, 128], BF16)
        self.VM = sb("VM", [128, 2, 4, 128], BF16)
        self.ESK = sb("ESK", [128, 8], F32)
        self.HC = sb("HC", [128, 4, 256], BF16)
        self.HA = sb("HA", [128, 256], F32)
        self.HO = sb("HO", [128, 256], BF16)
        self.ZT = sb("ZT", [128, 64], F32)
        self.PS = [st.enter_context(nc.psum_tensor(f"ps{i}", [128, 512], F32)) for i in range(8)]

    def ones32(self):
        return self.C32[:, 0, :]

    def perm(self, which):
        return self.C32[:, 1 if which == "A" else 2, :]

    def identb(self):
        return self.CB[:, 0:128]

    def onesb(self):
        return self.CB[:, 128:256]

    def ones16(self):
        return self.CB[:, 256:384]

    def mask(self, i):
        o = 384 + i * 512
        return self.CB[:, o:o + 512].rearrange("p (a b) -> p a b", a=4)

    def maskmeta(self):
        o = 384 + 4 * 512
        return self.CB[:, o:o + 64].rearrange("p (a b) -> p a b", a=4)

    def vec(self, l, col):
        return self.VEC[:, l * NVL + col: l * NVL + col + 1]

    def load_consts(self):
        for name, sbt, dr in (("C32", self.C32, self.c32_d), ("CB", self.CB, self.cb16_d),
                              ("VEC", self.VEC, self.vecs_d), ("BL", self.BL, self.blend_d),
                              ("INVC", self.INVC, self.invc_d)):
            self.dma("sp", sbt[:], dr, [], [name], "c_" + name)
        self.memset("pool", self.ZT[:], 0.0, ["ZT"])

    def convert_weights(self):
        jobs = []
        for l in range(DEPTH):
            def blocks(src, dst, n):
                for c0 in range(0, n, 512):
                    jobs.append((src[:, c0:c0 + 512].rearrange("(k p) n -> p k n", p=128),
                                 [(dst[:, :, c0:c0 + 512], None)], KC, 512))
            blocks(self.w_in[l][:, 0:4096], self.wi_b[l], 4096)
            for i in range(3):
                for c0 in range(0, D, 512):
                    src = self.w_in[l][:, 4096 + i * D + c0:4096 + i * D + c0 + 512].rearrange("(k p) n -> p k n", p=128)
                    jobs.append((src, [(self.wg_b[l][:, c0 // 128 + cc, i], cc) for cc in range(4)], KC, 512))
            for j, wsrc in enumerate((self.w_ba[l], self.w_bb[l])):
                for c0 in range(0, D, 512):
                    src = wsrc[:, c0:c0 + 512].rearrange("(k p) n -> p k n", p=128)
                    jobs.append((src, [(self.wab_b[l][:, c0 // 128 + cc, j], cc) for cc in range(4)], KC, 512))
            blocks(self.w_out[l], self.wo_b[l], D)
            for part in range(2):
                for gi in range(FC // 2):
                    src = self.w_up[l][:, part * DFF + gi * 256:part * DFF + (gi + 1) * 256].rearrange("(k p) n -> p k n", p=128)
                    jobs.append((src, [(self.wu2_b[l][:, gi, part], None)], KC, 256))
            for k0 in range(0, FC, 8):
                kn = min(8, FC - k0)
                for c0 in range(0, D, 512):
                    src = self.w_down[l][k0 * 128:(k0 + kn) * 128, c0:c0 + 512].rearrange("(k p) n -> p k n", p=128)
                    dsts = [(self.wd_b[l][:, c0 // 128 + cc, k0:k0 + kn, :], cc) for cc in range(4)]
                    jobs.append((src, dsts, kn, 512))
            for g in range(4):
                jobs.append((self.pool_w[l, g].rearrange("(k p) n -> p k n", p=128),
                             [(self.pw_b[l][:, g, :, :], None)], 2, 256))
        stg = [(self.H32, "H32"), (self.TMP, "tmp")]
        outb = [(self.OB, "OB"), (self.DP, "DP")]
        engs = ["dve", "act", "pool"]
        for i, (src, dsts, kc, bw) in enumerate(jobs):
            s32, k32 = stg[i % 2]
            s16, k16 = outb[i % 2]
            n = kc * bw
            v32 = s32[:].rearrange("p a b -> p (a b)")[:, 0:n].rearrange("p (k n) -> p k n", k=kc)
            v16 = s16[:].rearrange("p a b -> p (a b)")[:, 0:n].rearrange("p (k n) -> p k n", k=kc)
            rk = [(k32, c) for c in range(8)]
            wk = [(k16, c) for c in range(8)]
            self.dma("sp", v32, src, [], rk, "cv32_%d" % (i % 2))
            self.cp(engs[i % 3], v16, v32, rk, wk)
            for (dst, cc) in dsts:
                sv = v16 if cc is None else v16[:, :, cc * 128:(cc + 1) * 128]
                self.dma("pool", dst, sv, wk, ["wts"], "cv16_%d" % (i % 2))

    def wload(self, parts):
        s = self._wr_i % 4
        self._wr_i += 1
        key = ("WR", s)
        views = []
        off = 0
        for ap in parts:
            if len(ap.shape) == 3:
                k, n = ap.shape[1], ap.shape[2]
                v = self.WR[:, s, off:off + k * n].rearrange("p (k n) -> p k n", k=k)
                m = k * n
            else:
                m = ap.shape[1]
                v = self.WR[:, s, off:off + m]
            self.dma("sp", v, ap, ["wts"], [key], "wr%d" % s)
            views.append(v)
            off += m
        assert off <= 4096
        return key, views

    def hkey(self, l, seg):
        return [] if l == 0 else [("H1", seg)]

    def load_hb(self, src, slot, T, rk):
        for half in range(2):
            self.dma("sp", self.LD32[:, :, 0:T], src[:, half * 4:half * 4 + 4, :], rk, ["LD32"], "ld32")
            self.cp("pool", self.HB[:, slot, half * 4:half * 4 + 4, 0:T], self.LD32[:, :, 0:T], ["LD32"],
                    [("HB", slot, c) for c in range(half * 4, half * 4 + 4)])

    def hsrc(self, l, seg, t0, T, meta):
        if l == 0:
            return self.xm if meta else self.x[seg][:, :, t0:t0 + T]
        return self.H1[seg][:, :, 0:NM] if meta else self.H1[seg][:, :, NM + t0:NM + t0 + T]

    def load_rope(self, seg, col0, T):
        self.dma("sp", self.RT[:, :, 0:T], self.rope[seg][:, :, col0:col0 + T], [], ["RT"], "rt")

    def head_chain(self, bk, T, which, gcol, out_ap, out_keys):
        psk = ("ps", bk)
        ps = self.PS[bk][:, 0:T]
        ci, si = (0, 1) if which == "A" else (2, 3)
        t_q = self.tmp()
        q32 = self.TMP[:, t_q, 0:T]
        if which == "A":
            self.act(q32, ps, AF.Identity, [psk, "VEC"], [("tmp", t_q)], scale=gcol)
            t_s = self.tmp()
            sq = self.TMP[:, t_s, 0:T]
            self.act(sq, ps, AF.Square, [psk], [("tmp", t_s)])
            b2 = self.bank()
            self.mm(self.PS[b2][:, 0:T], self.ones32(), sq, True, True, ["C32", ("tmp", t_s)], [("ps", b2)])
            self.act(sq, self.PS[b2][:, 0:T], AF.Ln, [("ps", b2), "VEC"], [("tmp", t_s)],
                     bias=self.VEC[:, 2 * NVL:2 * NVL + 1])
            self.act(sq, sq, AF.Exp, [("tmp", t_s)], [("tmp", t_s)], scale=-0.5)
        else:
            self.cp("dve", q32, ps, [psk], [("tmp", t_q)])
        b3 = self.bank()
        self.mm(self.PS[b3][:, 0:T], self.perm(which), q32, True, True, ["C32", ("tmp", t_q)], [("ps", b3)])
        t_b = self.tmp()
        bb = self.TMP[:, t_b, 0:T]
        self.tt("dve", bb, self.PS[b3][:, 0:T], self.RT[:, si, 0:T], ALU.mult, [("ps", b3), "RT"], [("tmp", t_b)])
        self.tt("pool", q32, q32, self.RT[:, ci, 0:T], ALU.mult, [("tmp", t_q), "RT"], [("tmp", t_q)])
        if which == "A":
            self.tt("pool", q32, q32, bb, ALU.add, [("tmp", t_q), ("tmp", t_b)], [("tmp", t_q)])
            self.tt("dve", out_ap, q32, sq, ALU.mult, [("tmp", t_q), ("tmp", t_s)], out_keys)
        else:
            self.tt("dve", out_ap, q32, bb, ALU.add, [("tmp", t_q), ("tmp", t_b)], out_keys)

    def phase1_tile(self, l, seg, ti, meta):
        NR, NB = self.NR, self.NB
        T = NM if meta else 512
        t0 = 0 if meta else ti * 512
        si_ = 0 if seg == "p" else 1
        slot = self._hb_par
        self._hb_par ^= 1
        self.load_hb(self.hsrc(l, seg, t0, T, meta), slot, T, self.hkey(l, seg))
        self.load_rope(seg, 0 if meta else NM + t0, T)
        hbk = [("HB", slot, c) for c in range(KC)]
        hb = self.HB[:, slot]
        wi = self.wi_b[l]
        k1, (x1,) = self.wload([wi[:, :, 1024:1536]])
        k2, (x2,) = self.wload([wi[:, :, 2560:3072]])
        for which, xw, wkey, row in (("A", x1, k1, 0), ("B", x2, k2, 1)):
            for g in range(2):
                bk = self.bank()
                for kc in range(KC):
                    self.mm(self.PS[bk][:, 0:T], xw[:, kc, g * 128:(g + 1) * 128], hb[:, kc, 0:T],
                            kc == 0, kc == KC - 1, [wkey, hbk[kc]], [("ps", bk)])
                if meta:
                    out_ap = self.KM[:, si_, row * 2 + g, 0:NM]
                    okeys = [("KM", seg)]
                else:
                    out_ap = self.KST[:, row, g, :]
                    okeys = [("KST", row)]
                self.head_chain(bk, T, which, self.vec(l, 1), out_ap, okeys)
        ntb = 1 if meta else 4
        for tb in range(ntb):
            bk = self.bank()
            M = NM if meta else 128
            for j, (xw, wkey) in enumerate(((x1, k1), (x2, k2))):
                for kc in range(KC):
                    self.mm(self.PS[bk][0:M, j * 256:(j + 1) * 256], hb[:, kc, tb * 128:tb * 128 + M],
                            xw[:, kc, 256:512], (j == 0 and kc == 0), (j == 1 and kc == KC - 1),
                            [wkey, hbk[kc]], [("ps", bk)])
            if meta:
                self.cp("act", self.VM[0:NM, si_].rearrange("p a b -> p (a b)"), self.PS[bk][0:NM, :], [("ps", bk)], [("VM", seg)])
            else:
                self.cp("act" if tb % 2 else "dve", self.VST[:, tb, :], self.PS[bk][:, :], [("ps", bk)], [("VST", tb)])
        for half in range(2):
            ku, (xu,) = self.wload([wi[:, :, 3072 + half * 512:3072 + (half + 1) * 512]])
            for cc in range(4):
                c = half * 4 + cc
                bk = self.bank()
                for kc in range(KC):
                    self.mm(self.PS[bk][:, 0:T], xu[:, kc, cc * 128:(cc + 1) * 128], hb[:, kc, 0:T],
                            kc == 0, kc == KC - 1, [ku, hbk[kc]], [("ps", bk)])
                self.cp("act" if c % 2 else "dve", self.H32[:, c, 0:T], self.PS[bk][:, 0:T], [("ps", bk)], [("H32", c)])
        h32k = [("H32", c) for c in range(KC)]
        if meta:
            self.dma("pool", self.Um[seg][:, :, 8:8 + NM], self.H32[:, :, 0:NM], h32k, [("Um", seg)], "st_u")
            return
        vst4 = self.VST[:].rearrange("p t (j g d) -> p t j g d", j=2, g=2)
        vstk = [("VST", tb) for tb in range(4)]
        for g in range(2):
            if seg == "p":
                ka_dst = self.KAl[:, g, t0:t0 + 512]
                va_dst = self.VAl[:, g, ti * 4:ti * 4 + 4, :]
                kakey, vakey = "KAl", "VAl"
            else:
                ka_dst = self.CK[l][g].ap()[:, t0:t0 + 512]
                va_dst = self.CV[l][g].ap().rearrange("p (n d) -> p n d", d=128)[:, ti * 4:ti * 4 + 4, :]
                kakey, vakey = ("CK", l, g), ("CV", l, g)
            self.dma("pool", ka_dst, self.KST[:, 0, g, :], [("KST", 0)], [kakey], "st_ka")
            self.dma("pool", va_dst, vst4[:, :, 0, g, :], vstk, [vakey], "st_va")
        self.dma("pool", self.KBl[seg][:, :, 128 + t0:128 + t0 + 512], self.KST[:, 1], [("KST", 1)], [("KBl", seg)], "st_kb")
        self.dma("pool", self.VBl[seg][:, 1 + ti * 4:1 + ti * 4 + 4], vst4[:, :, 1], vstk, [("VBl", seg)], "st_vb")
        self.dma("pool", self.U[seg][:, :, 8 + t0:8 + t0 + 512], self.H32[:, :, :], h32k, [("U", seg)], "st_u")
        if seg == "s":
            o = 0
            C = self.CH[l].ap()
            if ti == 0:
                self.dma("pool", C[:, o:o + 256].rearrange("p (g t) -> p g t", g=2), self.KST[:, 1, :, 0:128],
                         [("KST", 1)], [("CH", l)], "st_kb")
                self.dma("pool", C[:, o + 512:o + 768].rearrange("p (g d) -> p g d", g=2), vst4[:, 0, 1], vstk, [("CH", l)], "st_vb")
                self.cp("dve", self.HO[:, 0:64].rearrange("p (c t) -> p c t", c=8), self.H32[:, :, 0:8], h32k, ["HO"])
                self.dma("pool", C[:, o + 1024:o + 1088], self.HO[:, 0:64], ["HO"], [("CH", l)], "st_ho")
            if ti == self.NT - 1:
                self.dma("pool", C[:, o + 256:o + 512].rearrange("p (g t) -> p g t", g=2), self.KST[:, 1, :, 384:512],
                         [("KST", 1)], [("CH", l)], "st_kb")
                self.dma("pool", C[:, o + 768:o + 1024].rearrange("p (g d) -> p g d", g=2), vst4[:, 3, 1], vstk, [("CH", l)], "st_vb")
                self.cp("dve", self.HO[:, 64:128].rearrange("p (c t) -> p c t", c=8), self.H32[:, :, 504:512], h32k, ["HO"])
                self.dma("pool", C[:, o + 1088:o + 1152], self.HO[:, 64:128], ["HO"], [("CH", l)], "st_ho")

    def phase1(self, l, seg):
        si_ = 0 if seg == "p" else 1
        self.memset("pool", self.KM[:, si_], 0.0, [("KM", seg)])
        self.memset("pool", self.VM[:, si_], 0.0, [("VM", seg)])
        self.phase1_tile(l, seg, 0, True)
        for ti in range(self.NT):
            self.phase1_tile(l, seg, ti, False)

    def allgather(self, l):
        jobs = [(self.CK[l][g], self.GK[l][g], ("CK", l, g), ("GK", l, g)) for g in range(2)]
        jobs += [(self.CV[l][g], self.GV[l][g], ("CV", l, g), ("GV", l, g)) for g in range(2)]
        jobs += [(self.CH[l], self.GH[l], ("CH", l), ("GH", l))]
        for i, (ct, gt, ck, gk) in enumerate(jobs):
            self.S.add("pool", lambda e, ct=ct, gt=gt: e.collective_compute(
                "AllGather", ALU.bypass, replica_groups=[[0, 1, 2, 3], [4, 5, 6, 7]],
                ins=[ct.ap().opt()], outs=[gt.ap().opt()]),
                reads=[ck], writes=[gk], dma=True, sem_key="ag%d_%d" % (l, i), inc=1)

    def prep(self, l, seg):
        NR, NB = self.NR, self.NB
        U, Um = self.U[seg], self.Um[seg]
        z = self.ZT[:].rearrange("p (c t) -> p c t", c=8)
        sfx = "_%d%s" % (l, seg)
        self.dma("pool", Um[:, :, 0:8], z, ["ZT"], [("Um", seg)], "pp0" + sfx)
        if seg == "p":
            self.dma("pool", U[:, :, 0:8], Um[:, :, 16:24], [("Um", seg)], [("U", seg)], "pp1" + sfx)
            self.dma("pool", U[:, :, NR + 8:NR + 16], z, ["ZT"], [("U", seg)], "pp2" + sfx)
            self.dma("pool", Um[:, :, 24:32], U[:, :, 8:16], [("U", seg)], [("Um", seg)], "pp3" + sfx)
            return
        G = self.GH[l].ap()
        o = 0
        gk = ("GH", l)

        def blend(cands, wcol0, width, extra=None):
            acc = self.HA[:, 0:width]
            self.ts("dve", acc, self.HC[:, 0, 0:width], self.BL[:, wcol0:wcol0 + 1], ALU.mult, ["HC", "BL"], ["HA"])
            for r in range(1, 4):
                self.stt(acc, self.HC[:, r, 0:width], self.BL[:, wcol0 + r:wcol0 + r + 1], acc, ALU.mult, ALU.add,
                         ["HC", "BL", "HA"], ["HA"])

        def gload(off, width):
            self.dma("sp", self.HC[:, :, 0:width], G[:, off:off + width].rearrange("(r p) x -> p r x", p=128),
                     [gk], ["HC"], "hc")

        for (off, wc, ext) in ((o + 256, 0, 0), (o, 4, NB + 1)):
            gload(off, 256)
            blend(None, wc, 256)
            self.cp("dve", self.HO[:, 0:256], self.HA[:, 0:256], ["HA"], ["HO"])
            self.dma("pool", self.KBl[seg][:, :, ext * 128:(ext + 1) * 128], self.HO[:, 0:256].rearrange("p (g t) -> p g t", g=2),
                     ["HO"], [("KBl", seg)], "pp1" + sfx)
        for (off, wc, ext) in ((o + 768, 0, 0), (o + 512, 4, NB + 1)):
            gload(off, 256)
            blend(None, wc, 256)
            self.cp("dve", self.HO[:, 0:256], self.HA[:, 0:256], ["HA"], ["HO"])
            self.dma("pool", self.VBl[seg][:, ext], self.HO[:, 0:256].rearrange("p (g d) -> p g d", g=2),
                     ["HO"], [("VBl", seg)], "pp1" + sfx)
        gload(o + 1088, 64)
        blend(None, 0, 64)
        self.dma("sp", self.LD32[:, 0, 0:64].rearrange("p (c t) -> p c t", c=8), Um[:, :, 16:24], [("Um", seg)], ["LD32"], "ld32")
        self.stt(self.HA[:, 0:64], self.LD32[:, 0, 0:64], self.BL[:, 8:9], self.HA[:, 0:64], ALU.mult, ALU.add,
                 ["LD32", "BL", "HA"], ["HA"])
        self.dma("pool", U[:, :, 0:8], self.HA[:, 0:64].rearrange("p (c t) -> p c t", c=8), ["HA"], [("U", seg)], "pp2" + sfx)
        gload(o + 1024, 64)
        blend(None, 4, 64)
        self.dma("pool", U[:, :, NR + 8:NR + 16], self.HA[:, 0:64].rearrange("p (c t) -> p c t", c=8), ["HA"], [("U", seg)], "pp2" + sfx)
        self.dma("sp", self.HC[:, 0, 0:64], G[0:128, o + 1024:o + 1088], [gk], ["HC"], "hc")
        self.cp("dve", self.HA[:, 0:64], self.HC[:, 0, 0:64], ["HC"], ["HA"])
        self.dma("pool", Um[:, :, 24:32], self.HA[:, 0:64].rearrange("p (c t) -> p c t", c=8), ["HA"], [("Um", seg)], "pp3" + sfx)

    def pool_branch(self, l, seg, ti, meta, T):
        NR = self.NR
        src = self.Um[seg] if meta else self.U[seg]
        c0 = 0 if meta else ti * 512
        W = T + 16
        last = (not meta) and ti == self.NT - 1
        for g in range(4):
            w = (2, 4, 8, 16)[g]
            e = self.UT[:, 0, :, 0:W]
            self.dma("pool", e, src[:, 2 * g:2 * g + 2, c0:c0 + W], [("Um", seg) if meta else ("U", seg)], [("UT", 0)], "ut")
            a = self.UT[:, 1, :, :]
            b = self.UT[:, 2, :, :]
            k0, k1, k2 = ("UT", 0), ("UT", 1), ("UT", 2)
            ev = self.UT[:, 0, :, :]
            if w == 2:
                self.tt("pool", a[:, :, 0:T], ev[:, :, 7:7 + T], ev[:, :, 8:8 + T], ALU.add, [k0], [k1])
                s, sk = a, k1
            elif w == 4:
                self.tt("pool", a[:, :, 0:W - 1], ev[:, :, 0:W - 1], ev[:, :, 1:W], ALU.add, [k0], [k1])
                self.tt("pool", b[:, :, 0:T], a[:, :, 6:6 + T], a[:, :, 8:8 + T], ALU.add, [k1], [k2])
                s, sk = b, k2
            elif w == 8:
                self.tt("pool", a[:, :, 0:W - 1], ev[:, :, 0:W - 1], ev[:, :, 1:W], ALU.add, [k0], [k1])
                self.tt("pool", b[:, :, 0:W - 3], a[:, :, 0:W - 3], a[:, :, 2:W - 1], ALU.add, [k1], [k2])
                self.tt("pool", a[:, :, 0:T], b[:, :, 4:4 + T], b[:, :, 8:8 + T], ALU.add, [k2], [k1])
                s, sk = a, k1
            else:
                self.tt("pool", a[:, :, 0:W - 1], ev[:, :, 0:W - 1], ev[:, :, 1:W], ALU.add, [k0], [k1])
                self.tt("pool", b[:, :, 0:W - 3], a[:, :, 0:W - 3], a[:, :, 2:W - 1], ALU.add, [k1], [k2])
                self.tt("pool", a[:, :, 0:W - 7], b[:, :, 0:W - 7], b[:, :, 4:W - 3], ALU.add, [k2], [k1])
                self.tt("pool", b[:, :, 0:T], a[:, :, 0:T], a[:, :, 8:8 + T], ALU.add, [k1], [k2])
                s, sk = b, k2
            dpk = [("DP", 2 * g), ("DP", 2 * g + 1)]
            out = self.DP[:, 2 * g:2 * g + 2, 0:T]
            if meta:
                for cc in range(2):
                    self.tt("dve", s[:, cc, 0:T], s[:, cc, 0:T], self.INVC[:, 0, g, 0:T], ALU.mult, [sk, "INVC"], [sk])
                self.tt("dve", out, s[:, :, 0:T], ev[:, :, 8:8 + T], ALU.subtract, [sk, k0], dpk)
            else:
                self.stt(out, s[:, :, 0:T], 1.0 / w, ev[:, :, 8:8 + T], ALU.mult, ALU.subtract, [sk, k0], dpk)
                if last:
                    ti_ = 1 if seg == "p" else 2
                    for cc in range(2):
                        self.tt("dve", s[:, cc, T - 8:T], s[:, cc, T - 8:T], self.INVC[:, ti_, g, 0:8], ALU.mult, [sk, "INVC"], [sk])
                    self.tt("dve", self.DP[:, 2 * g:2 * g + 2, T - 8:T], s[:, :, T - 8:T], ev[:, :, T:T + 8], ALU.subtract,
                            [sk, k0], dpk)

    def key_sources(self, l, seg):
        NR, NB = self.NR, self.NB
        out = {}
        for g in range(2):
            lst = []
            if seg == "p":
                for c0 in range(0, NB, 16):
                    n = min(16, NB - c0)
                    lst.append((self.KAl[:, g, c0 * 128:(c0 + n) * 128], self.VAl[:, g, c0:c0 + n, :], n, ["KAl", "VAl"]))
            else:
                for r in range(4):
                    Kr = self.GK[l][g].ap()[r * 128:(r + 1) * 128]
                    Vr = self.GV[l][g].ap()[r * 128:(r + 1) * 128].rearrange("p (n d) -> p n d", d=128)
                    for c0 in range(0, NB, 16):
                        n = min(16, NB - c0)
                        lst.append((Kr[:, c0 * 128:(c0 + n) * 128], Vr[:, c0:c0 + n, :], n, [("GK", l, g), ("GV", l, g)]))
            out[g] = lst
        return out

    def attn_global(self, l, seg, T):
        srcs = self.key_sources(l, seg)
        scale = 128.0 ** 0.5
        si_ = 0 if seg == "p" else 1
        SB = (0, 1, 2, 7)
        obs, dbs = (3, 4), (5, 6)
        for g in range(2):
            for pr in range(2):
                heads = (4 * g + 2 * pr, 4 * g + 2 * pr + 1)
                first = [True, True]
                queue = []

                def flush_one():
                    for (hi, pslot, v_ap, ones_ap, rds) in queue.pop(0):
                        p_ap = self.PT[:, pslot, 0:T]
                        self.mm(self.PS[obs[hi]][:, 0:T], v_ap, p_ap, first[hi], False, rds + [("PT", pslot)], [("ps", obs[hi])])
                        self.mm(self.PS[dbs[hi]][:, 0:T], ones_ap, p_ap, first[hi], False, rds + [("PT", pslot)], [("ps", dbs[hi])])
                        first[hi] = False

                def do_tile(kT, v_ap, ones_ap, rds):
                    items = []
                    for hi, h in enumerate(heads):
                        k_ = self._pt_i % 4
                        self._pt_i += 1
                        sb_ = SB[k_]
                        self.mm(self.PS[sb_][:, 0:T], kT, self.XA[:, h, 0:T], True, True, rds + [("XA", h)], [("ps", sb_)])
                        self.act(self.PT[:, k_, 0:T], self.PS[sb_][:, 0:T], AF.Exp, [("ps", sb_)], [("PT", k_)], scale=scale)
                        items.append((hi, k_, v_ap, ones_ap, rds))
                    queue.append(items)
                    if len(queue) > 1:
                        flush_one()

                do_tile(self.KM[:, si_, g, :], self.VM[:, si_, g, :], self.ones16(), [("KM", seg), ("VM", seg), "CB"])
                for (Kap, Vap, n, rd) in srcs[g]:
                    s = self._kv_i % 2
                    self._kv_i += 1
                    kvk = ("KVR", s)
                    self.dma("sp", self.KVR[:, s, 0:n * 128], Kap, rd, [kvk], "kv%d" % s)
                    self.dma("sp", self.KVR[:, s, 2048:2048 + n * 128], Vap.rearrange("p n d -> p (n d)"), rd, [kvk], "kv%d" % s)
                    for j in range(n):
                        do_tile(self.KVR[:, s, j * 128:(j + 1) * 128], self.KVR[:, s, 2048 + j * 128:2048 + (j + 1) * 128],
                                self.onesb(), [kvk, "CB"])
                while queue:
                    flush_one()
                for hi, h in enumerate(heads):
                    t_r = self.tmp()
                    rd_ = self.TMP[:, t_r, 0:T]
                    self.act(rd_, self.PS[dbs[hi]][:, 0:T], AF.Ln, [("ps", dbs[hi])], [("tmp", t_r)])
                    self.act(rd_, rd_, AF.Exp, [("tmp", t_r)], [("tmp", t_r)], scale=-1.0)
                    self.tt("dve", self.XA[:, 16 + h, 0:T], self.PS[obs[hi]][:, 0:T], rd_, ALU.mult,
                            [("ps", obs[hi]), ("tmp", t_r)], [("XA", 16 + h)])

    def attn_window(self, l, seg, ti, meta, T):
        NB = self.NB
        scale = 128.0 ** -0.5
        si_ = 0 if seg == "p" else 1
        SB = (0, 1, 2, 7)
        if meta:
            e0, nblk = 1, 1
        else:
            e0, nblk = ti * 4, 6
        kvbk = "KVB"
        Kv = self.KVB[:, 0:1536].rearrange("p (g t) -> p g t", g=2)
        Vv = self.KVB[:, 1536:3072].rearrange("p (n g d) -> p n g d", g=2, d=128)
        if meta and seg == "s":
            G = self.GH[l].ap()
            o = 0
            self.dma("sp", Kv[:, :, 0:128], G[0:128, o:o + 256].rearrange("p (g t) -> p g t", g=2), [("GH", l)], [kvbk], "kvb")
            self.dma("sp", Vv[:, 0], G[0:128, o + 512:o + 768].rearrange("p (g d) -> p g d", g=2), [("GH", l)], [kvbk], "kvb")
        else:
            self.dma("sp", Kv[:, :, 0:nblk * 128], self.KBl[seg][:, :, e0 * 128:(e0 + nblk) * 128], [("KBl", seg)], [kvbk], "kvb")
            self.dma("sp", Vv[:, 0:nblk], self.VBl[seg][:, e0:e0 + nblk], [("VBl", seg)], [kvbk], "kvb")
        nqb = 1 if meta else 4
        QW = NM if meta else 128
        N4 = 4 * QW
        for qb in range(nqb):
            for g in range(2):
                qv = self.XA[:, 8 + 4 * g:8 + 4 * g + 4, qb * 128:qb * 128 + QW]
                qk = [("XA", 8 + 4 * g + i) for i in range(4)]
                ob, db = 3 + (g % 2), 5 + (g % 2)
                O = self.PS[ob][:, 0:N4].rearrange("p (a b) -> p a b", a=4)
                Dn = self.PS[db][:, 0:N4].rearrange("p (a b) -> p a b", a=4)
                kts = [(self.KM[:, si_, 2 + g, :], self.VM[:, si_, 2 + g, :], self.ones16(), None, [("KM", seg), ("VM", seg), "CB"])]
                if meta:
                    kts.append((Kv[:, g, 0:128], Vv[:, 0, g, :], self.onesb(), self.maskmeta(), [kvbk, "CB"]))
                else:
                    b = ti * 4 + qb
                    for d_, mi in ((0, 0), (1, None), (2, 1)):
                        if seg == "p" and ((b == 0 and d_ == 0) or (b == NB - 1 and d_ == 2)):
                            continue
                        m = None
                        if mi is not None:
                            if seg == "s" and b == 0 and d_ == 0:
                                m = self.mask(2)
                            elif seg == "s" and b == NB - 1 and d_ == 2:
                                m = self.mask(3)
                            else:
                                m = self.mask(mi)
                        j = qb + d_
                        kts.append((Kv[:, g, j * 128:(j + 1) * 128], Vv[:, j, g, :], self.onesb(), m, [kvbk, "CB"]))
                pslots = []
                for (kT, v_ap, ones_ap, m, rds) in kts:
                    k_ = self._pt_i % 4
                    self._pt_i += 1
                    sb_ = SB[k_]
                    Sb = self.PS[sb_][:, 0:N4].rearrange("p (a b) -> p a b", a=4)
                    self.mm(Sb, kT, qv, True, m is None, rds + qk, [("ps", sb_)])
                    if m is not None:
                        self.mm(Sb, self.identb(), m, False, True, ["CB"], [("ps", sb_)])
                    self.act(self.PT[:, k_, 0:N4], self.PS[sb_][:, 0:N4], AF.Exp, [("ps", sb_)], [("PT", k_)], scale=scale)
                    pslots.append((k_, v_ap, ones_ap, rds))
                for i, (sb_, v_ap, ones_ap, rds) in enumerate(pslots):
                    p_ap = self.PT[:, sb_, 0:N4].rearrange("p (a b) -> p a b", a=4)
                    self.mm(O, v_ap, p_ap, i == 0, False, rds + [("PT", sb_)], [("ps", ob)])
                    self.mm(Dn, ones_ap, p_ap, i == 0, False, rds + [("PT", sb_)], [("ps", db)])
                t_r = self.tmp()
                dt = self.TMP[:, t_r, 0:N4].rearrange("p (a b) -> p a b", a=4)
                for hh in range(4):
                    self.act(dt[:, hh, :], Dn[:, hh, :], AF.Ln, [("ps", db), "ESK"], [("tmp", t_r)],
                             bias=self.ESK[:, 4 * g + hh:4 * g + hh + 1])
                self.act(self.TMP[:, t_r, 0:N4], self.TMP[:, t_r, 0:N4], AF.Exp, [("tmp", t_r)], [("tmp", t_r)], scale=-1.0)
                self.tt("dve", self.OB[:, 4 * g:4 * g + 4, qb * 128:qb * 128 + QW], O, dt, ALU.mult,
                        [("ps", ob), ("tmp", t_r)], [("OB", 4 * g + i) for i in range(4)])

    def layer_norm(self, l, T, gcol0, bcol0, hb_slot, store=None):
        b1, b2 = self.bank(), self.bank()
        for p_ in range(4):
            c0, c1 = 2 * p_, 2 * p_ + 1
            t_ = self.tmp()
            self.tt("dve", self.TMP[:, t_, 0:T], self.H32[:, c0, 0:T], self.H32[:, c1, 0:T], ALU.add,
                    [("H32", c0), ("H32", c1)], [("tmp", t_)])
            self.mm(self.PS[b1][:, 0:T], self.ones32(), self.TMP[:, t_, 0:T], p_ == 0, p_ == 3,
                    ["C32", ("tmp", t_)], [("ps", b1)])
        for p_ in range(4):
            c0, c1 = 2 * p_, 2 * p_ + 1
            ta, tb = self.tmp(), self.tmp()
            self.act(self.TMP[:, ta, 0:T], self.H32[:, c0, 0:T], AF.Square, [("H32", c0)], [("tmp", ta)])
            self.act(self.TMP[:, tb, 0:T], self.H32[:, c1, 0:T], AF.Square, [("H32", c1)], [("tmp", tb)])
            self.tt("pool", self.TMP[:, ta, 0:T], self.TMP[:, ta, 0:T], self.TMP[:, tb, 0:T], ALU.add,
                    [("tmp", ta), ("tmp", tb)], [("tmp", ta)])
            self.mm(self.PS[b2][:, 0:T], self.ones32(), self.TMP[:, ta, 0:T], p_ == 0, p_ == 3,
                    ["C32", ("tmp", ta)], [("ps", b2)])
        tm, tv = 5, 6
        m = self.TMP[:, tm, 0:T]
        v = self.TMP[:, tv, 0:T]
        self.ts("dve", m, self.PS[b1][:, 0:T], 1.0 / D, ALU.mult, [("ps", b1)], [("tmp", tm)])
        self.tt("dve", v, m, m, ALU.mult, [("tmp", tm)], [("tmp", tv)])
        self.stt(v, self.PS[b2][:, 0:T], 1.0 / D, v, ALU.mult, ALU.subtract, [("ps", b2), ("tmp", tv)], [("tmp", tv)])
        self.act(v, v, AF.Ln, [("tmp", tv), "VEC"], [("tmp", tv)], bias=self.VEC[:, 2 * NVL + 1:2 * NVL + 2])
        self.act(v, v, AF.Exp, [("tmp", tv)], [("tmp", tv)], scale=-0.5)
        for c in range(KC):
            t_ = self.tmp()
            x = self.TMP[:, t_, 0:T]
            self.tt("dve", x, self.H32[:, c, 0:T], m, ALU.subtract, [("H32", c), ("tmp", tm)], [("tmp", t_)])
            self.tt("dve", x, x, v, ALU.mult, [("tmp", t_), ("tmp", tv)], [("tmp", t_)])
            self.act(self.H32[:, c, 0:T], x, AF.Identity, [("tmp", t_), "VEC"], [("H32", c)],
                     scale=self.vec(l, gcol0 + c), bias=self.vec(l, bcol0 + c))
            if hb_slot is not None:
                self.cp("dve", self.HB[:, hb_slot, c, 0:T], self.H32[:, c, 0:T], [("H32", c)], [("HB", hb_slot, c)])

    def prefetch_hb(self, l, seg, ti, meta):
        T = NM if meta else 512
        t0 = 0 if meta else ti * 512
        slot = self._hb_par
        self._hb_par ^= 1
        self.load_hb(self.hsrc(l, seg, t0, T, meta), slot, T, self.hkey(l, seg))
        self._pref[(l, seg, ti, meta)] = slot

    def phase2_tile(self, l, seg, ti, meta, nxt=None):
        NR = self.NR
        T = NM if meta else 512
        t0 = 0 if meta else ti * 512
        if (l, seg, ti, meta) not in self._pref:
            self.prefetch_hb(l, seg, ti, meta)
        slot = self._pref.pop((l, seg, ti, meta))
        src = self.hsrc(l, seg, t0, T, meta)
        self.load_rope(seg, 0 if meta else NM + t0, T)
        h32k = [("H32", c) for c in range(KC)]
        hbk = [("HB", slot, c) for c in range(KC)]
        hb = self.HB[:, slot]
        wi = self.wi_b[l]
        for which, col0, xbase, gcol in (("A", 0, 0, self.vec(l, 0)), ("B", 1536, 8, None)):
            for half in range(2):
                kw, (xw,) = self.wload([wi[:, :, col0 + half * 512:col0 + (half + 1) * 512]])
                for hh in range(4):
                    h = half * 4 + hh
                    bk = self.bank()
                    for kc in range(KC):
                        self.mm(self.PS[bk][:, 0:T], xw[:, kc, hh * 128:(hh + 1) * 128], hb[:, kc, 0:T],
                                kc == 0, kc == KC - 1, [kw, hbk[kc]], [("ps", bk)])
                    self.head_chain(bk, T, which, gcol, self.XA[:, xbase + h, 0:T], [("XA", xbase + h)])
        self.pool_branch(l, seg, ti, meta, T)
        self.attn_global(l, seg, T)
        self.attn_window(l, seg, ti, meta, T)
        self.dma("sp", self.H32[:, :, 0:T], src, self.hkey(l, seg), h32k, "h32")
        if nxt is not None:
            self.prefetch_hb(l, seg, nxt[0], nxt[1])
        for c in range(KC):
            kg, (xgf,) = self.wload([self.wg_b[l][:, c].rearrange("p i k n -> p (i k n)")])
            xg4 = xgf.rearrange("p (i k n) -> p i k n", i=3, k=KC)
            xgs = [xg4[:, i] for i in range(3)]
            kb_, (xabf,) = self.wload([self.wab_b[l][:, c].rearrange("p i k n -> p (i k n)")])
            xab4 = xabf.rearrange("p (i k n) -> p i k n", i=2, k=KC)
            xa_, xb_ = xab4[:, 0], xab4[:, 1]
            gts = []
            for i, (xw, wk) in enumerate(((xgs[0], kg), (xgs[1], kg), (xgs[2], kg))):
                bk = self.bank()
                for kc in range(KC):
                    self.mm(self.PS[bk][:, 0:T], xw[:, kc, :], hb[:, kc, 0:T], kc == 0, kc == KC - 1,
                            [wk, hbk[kc]], [("ps", bk)])
                t_ = self.tmp()
                self.act(self.TMP[:, t_, 0:T], self.PS[bk][:, 0:T], AF.Sigmoid, [("ps", bk)], [("tmp", t_)])
                gts.append(t_)
            ba, bb_, bc = self.bank(), self.bank(), self.bank()
            for kc in range(KC):
                self.mm(self.PS[ba][:, 0:T], xa_[:, kc, :], self.XA[:, 16 + kc, 0:T], kc == 0, kc == KC - 1,
                        [kb_, ("XA", 16 + kc)], [("ps", ba)])
            for kc in range(KC):
                self.mm(self.PS[bb_][:, 0:T], xb_[:, kc, :], self.OB[:, kc, 0:T], kc == 0, kc == KC - 1,
                        [kb_, ("OB", kc)], [("ps", bb_)])
            g_ = c // 2
            e_ = c % 2
            for kc in range(2):
                self.mm(self.PS[bc][:, 0:T], self.PW[:, g_, kc, e_ * 128:(e_ + 1) * 128], self.DP[:, 2 * g_ + kc, 0:T],
                        kc == 0, kc == 1, ["PW", ("DP", 2 * g_ + kc)], [("ps", bc)])
            ta, tb_, tc = gts
            A_ = self.TMP[:, ta, 0:T]
            B_ = self.TMP[:, tb_, 0:T]
            C_ = self.TMP[:, tc, 0:T]
            self.tt("dve", A_, self.PS[ba][:, 0:T], A_, ALU.mult, [("ps", ba), ("tmp", ta)], [("tmp", ta)])
            self.tt("dve", B_, self.PS[bb_][:, 0:T], B_, ALU.mult, [("ps", bb_), ("tmp", tb_)], [("tmp", tb_)])
            self.stt(C_, self.PS[bc][:, 0:T], self.vec(l, 2 + c), C_, ALU.mult, ALU.mult, [("ps", bc), ("tmp", tc), "VEC"], [("tmp", tc)])
            self.tt("pool", A_, A_, B_, ALU.add, [("tmp", ta), ("tmp", tb_)], [("tmp", ta)])
            self.tt("dve", self.XA[:, c, 0:T], A_, C_, ALU.add, [("tmp", ta), ("tmp", tc)], [("XA", c)])
        for half in range(2):
            kw, (xw,) = self.wload([self.wo_b[l][:, :, half * 512:(half + 1) * 512]])
            for cc in range(4):
                c = half * 4 + cc
                bk = self.bank()
                for kc in range(KC):
                    self.mm(self.PS[bk][:, 0:T], xw[:, kc, cc * 128:(cc + 1) * 128], self.XA[:, kc, 0:T], kc == 0, kc == KC - 1,
                            [kw, ("XA", kc)], [("ps", bk)])
                self.stt(self.H32[:, c, 0:T], self.H32[:, c, 0:T], ALPHA, self.PS[bk][:, 0:T], ALU.mult, ALU.add,
                         [("H32", c), ("ps", bk)], [("H32", c)])
        self.layer_norm(l, T, 10, 18, slot)
        wu = self.wu_b[l]
        for j0 in range(0, FC, 2):
            kw, (xuf,) = self.wload([self.wu2_b[l][:, j0 // 2].rearrange("p i k n -> p (i k n)")])
            xu4 = xuf.rearrange("p (i k n) -> p i k n", i=2, k=KC)
            xg, xu = xu4[:, 0], xu4[:, 1]
            for jj in range(2):
                j = j0 + jj
                bg, bu = self.bank(), self.bank()
                for kc in range(KC):
                    self.mm(self.PS[bg][:, 0:T], xg[:, kc, jj * 128:(jj + 1) * 128], hb[:, kc, 0:T], kc == 0, kc == KC - 1,
                            [kw, hbk[kc]], [("ps", bg)])
                for kc in range(KC):
                    self.mm(self.PS[bu][:, 0:T], xu[:, kc, jj * 128:(jj + 1) * 128], hb[:, kc, 0:T], kc == 0, kc == KC - 1,
                            [kw, hbk[kc]], [("ps", bu)])
                t_ = self.tmp()
                self.act(self.TMP[:, t_, 0:T], self.PS[bg][:, 0:T], AF.Silu, [("ps", bg)], [("tmp", t_)])
                self.tt("dve", self.XA[:, j, 0:T], self.TMP[:, t_, 0:T], self.PS[bu][:, 0:T], ALU.mult,
                        [("tmp", t_), ("ps", bu)], [("XA", j)])
        for c in range(KC):
            kw, (xwf,) = self.wload([self.wd_b[l][:, c].rearrange("p k n -> p (k n)")])
            xw = xwf.rearrange("p (k n) -> p k n", k=FC)
            bk = self.bank()
            for j in range(FC):
                self.mm(self.PS[bk][:, 0:T], xw[:, j, :], self.XA[:, j, 0:T], j == 0, j == FC - 1,
                        [kw, ("XA", j)], [("ps", bk)])
            self.stt(self.H32[:, c, 0:T], self.H32[:, c, 0:T], ALPHA, self.PS[bk][:, 0:T], ALU.mult, ALU.add,
                     [("H32", c), ("ps", bk)], [("H32", c)])
        self.layer_norm(l, T, 26, 34, None)
        if l == 0:
            dst = self.H1[seg][:, :, 0:NM] if meta else self.H1[seg][:, :, NM + t0:NM + t0 + T]
            self.dma("pool", dst, self.H32[:, :, 0:T], h32k, [("H1", seg)], "st_h")
            if self.debug:
                dd = self.dbg[seg][:, :, 0:NM] if meta else self.dbg[seg][:, :, NM + t0:NM + t0 + T]
                self.dma("pool", dd, self.H32[:, :, 0:T], h32k, ["out"], "st_d")
        else:
            self.dma("pool", self.y[seg][:, :, t0:t0 + T], self.H32[:, :, 0:T], h32k, ["out"], "st_h")

    def phase2(self, l, seg):
        tiles = ([(0, True)] if l == 0 else []) + [(ti, False) for ti in range(self.NT)]
        for i, (ti, meta) in enumerate(tiles):
            self.phase2_tile(l, seg, ti, meta, tiles[i + 1] if i + 1 < len(tiles) else None)

    def layer_consts(self, l):
        self.dma("sp", self.PW[:], self.pw_b[l], ["wts"], ["PW"], "pw")
        self.act(self.ESK[:], self.VEC[:, l * NVL + 42:l * NVL + 50], AF.Exp, ["VEC"], ["ESK"])

    def build(self):
        with contextlib.ExitStack() as st:
            self.declare(st)
            self._hb_par = 0
            self._pref = {}
            self.load_consts()
            import os
            if os.environ.get("KNOCONV", "0") != "1":
                self.convert_weights()
            import os
            stop = int(os.environ.get("KSTOP", "99"))
            for l in range(DEPTH):
                if stop <= 0:
                    break
                self.layer_consts(l)
                self.phase1(l, "s")
                if stop <= 1:
                    break
                self.allgather(l)
                if stop <= 2:
                    break
                self.phase1(l, "p")
                self.prep(l, "p")
                if stop <= 3:
                    break
                self.phase2(l, "p")
                if stop <= 4:
                    break
                self.prep(l, "s")
                if stop <= 5:
                    break
                self.phase2(l, "s")
                if stop <= 6:
                    break
            self.S.add("sp", lambda e: None, reads=["out"])
            self.S.emit(self.nc, st)
        return self.nc


def _rope_tables(NR, q):
    theta = np.float32(10000.0)
    s = (np.arange(NR, dtype=np.int64) + q * NR)
    row = np.concatenate([-np.ones(NM, np.int64), s // 64]).astype(np.float32)
    col = np.concatenate([np.arange(NM, dtype=np.int64), s % 64]).astype(np.float32)
    pos = np.concatenate([np.arange(NM, dtype=np.int64), NM + s]).astype(np.float32)
    inv32 = (theta ** (-np.arange(0, 64, 2, dtype=np.float32) / np.float32(64))).astype(np.float32)
    inv64 = (theta ** (-np.arange(0, 128, 2, dtype=np.float32) / np.float32(128))).astype(np.float32)
    out = np.zeros((128, 4, NM + NR), np.float32)
    for base, p in ((0, row), (64, col)):
        ang = (p[None, :] * inv32[:, None]).astype(np.float32)
        c, sn = np.cos(ang).astype(np.float32), np.sin(ang).astype(np.float32)
        out[base:base + 32, 0] = c
        out[base + 32:base + 64, 0] = c
        out[base:base + 32, 1] = -sn
        out[base + 32:base + 64, 1] = sn
    ang = (pos[None, :] * inv64[:, None]).astype(np.float32)
    c, sn = np.cos(ang).astype(np.float32), np.sin(ang).astype(np.float32)
    out[0:64, 2] = c
    out[64:128, 2] = c
    out[0:64, 3] = -sn
    out[64:128, 3] = sn
    return out


def _consts():
    c32 = np.zeros((128, 3, 128), np.float32)
    c32[:, 0, :] = 1.0
    for d in range(128):
        srcA = d + 32 if (d % 64) < 32 else d - 32
        srcB = d + 64 if d < 64 else d - 64
        c32[srcA, 1, d] = 1.0
        c32[srcB, 2, d] = 1.0
    return c32


def _cb16(q, is_first_valid, is_last_valid):
    NCB = 3 * 128 + 4 * 512 + 64
    cb = np.zeros((128, NCB), np.float32)
    cb[:, 0:128] = np.eye(128, dtype=np.float32)
    cb[:, 128:256] = 1.0
    cb[0:NM, 256:384] = 1.0
    k = np.arange(128)[:, None]
    qq = np.arange(128)[None, :]
    mprev = np.where(k >= qq, 0.0, NEGM).astype(np.float32)
    mnext = np.where(k <= qq, 0.0, NEGM).astype(np.float32)
    mpf = mprev if is_first_valid else np.full((128, 128), NEGM, np.float32)
    mnl = mnext if is_last_valid else np.full((128, 128), NEGM, np.float32)
    for i, m in enumerate((mprev, mnext, mpf, mnl)):
        cb[:, 384 + i * 512:384 + (i + 1) * 512] = np.tile(m, (1, 4))
    qm = np.arange(NM)[None, :]
    mm = np.where(k <= 112 + qm, 0.0, NEGM).astype(np.float32)
    cb[:, 384 + 4 * 512:384 + 4 * 512 + 64] = np.tile(mm, (1, 4))
    return cb.astype(ml_dtypes.bfloat16)


def _invc(NR, L_s, q):
    out = np.zeros((3, 4, 16), np.float32)
    wins = (2, 4, 8, 16)
    Lp = NM + NR
    for g, w in enumerate(wins):
        for t in range(NM):
            out[0, g, t] = 1.0 / (min(Lp, t + w // 2) - max(0, t - w // 2))
        for i in range(8):
            tp = Lp - 8 + i
            out[1, g, i] = 1.0 / (min(Lp, tp + w // 2) - max(0, tp - w // 2))
            ts_ = NM + (q + 1) * NR - 8 + i
            out[2, g, i] = 1.0 / (min(L_s, ts_ + w // 2) - max(0, ts_ - w // 2))
    return np.broadcast_to(out[None], (128, 3, 4, 16)).copy()


def _fm(a):
    t = a.shape[0]
    return np.ascontiguousarray(a.reshape(t, KC, 128).transpose(2, 1, 0))


def _vecs(inp):
    v = np.zeros((128, 2 * NVL + 2), np.float32)
    for l in range(DEPTH):
        b = l * NVL
        v[:, b + 0] = inp["q_norm_g"][l]
        v[:, b + 1] = inp["k_norm_g"][l]
        for name, c0 in (("pool_scale", 2), ("ln1_g", 10), ("ln1_b", 18), ("ln2_g", 26), ("ln2_b", 34)):
            v[:, b + c0:b + c0 + 8] = inp[name][l].reshape(KC, 128).T
        v[:, b + 42:b + 50] = inp["sink_logit"][l][None, :]
    v[:, 2 * NVL] = 128.0 * 1e-6
    v[:, 2 * NVL + 1] = 1e-5
    return v


_NC_CACHE = {}


def _run(inp, debug=False):
    xp = np.asarray(inp["x_prompt"], np.float32)
    xs = np.asarray(inp["x_sample"], np.float32)
    NR = xp.shape[1]
    assert xp.shape[0] == 8 and xs.shape[0] == 2 and xs.shape[1] == 4 * NR
    key = (NR, debug)
    if key not in _NC_CACHE:
        _NC_CACHE[key] = Builder(NR, debug).build()
    nc = _NC_CACHE[key]
    f = lambda n: np.ascontiguousarray(np.asarray(inp[n], np.float32))
    shared = {
        "xm": _fm(np.asarray(inp["meta_tokens"], np.float32)),
        "w_in": f("w_in"), "w_ba": f("w_branch_a"), "w_bb": f("w_branch_b"), "w_out": f("w_out"),
        "w_up": f("w_up"), "w_down": f("w_down"), "pool_w": f("pool_w"),
        "vecs": _vecs({k: np.asarray(v, np.float32) for k, v in inp.items()}),
        "rope_p": _rope_tables(NR, 0), "c32": _consts(),
    }
    L_s = NM + 4 * NR
    in_maps = []
    for c in range(8):
        q = c % 4
        bl = np.zeros((128, 16), np.float32)
        if q > 0:
            bl[:, q - 1] = 1.0
        else:
            bl[:, 8] = 1.0
        if q < 3:
            bl[:, 4 + q + 1] = 1.0
        m = dict(shared)
        m["xp"] = _fm(xp[c])
        m["xs"] = _fm(xs[c // 4, q * NR:(q + 1) * NR])
        m["rope_s"] = _rope_tables(NR, q)
        m["cb16"] = _cb16(q, q > 0, q < 3)
        m["blend"] = bl
        m["invc"] = _invc(NR, L_s, q)
        in_maps.append(m)
    res = run_bass_kernel_spmd(nc, in_maps, core_ids=list(range(8)))
    r = res.results

    def unfm(a):
        return np.ascontiguousarray(a.transpose(2, 1, 0).reshape(a.shape[2], D))

    y_p = np.stack([unfm(r[c]["yp"]) for c in range(8)], 0).astype(np.float32)
    y_s = np.stack([np.concatenate([unfm(r[4 * b + q]["ys"]) for q in range(4)], 0) for b in range(2)], 0).astype(np.float32)
    if debug:
        return (y_p, y_s), r
    return (y_p, y_s)


def kernel(**inputs):
    return _run(inputs)
```

```python
import contextlib
import numpy as np
import ml_dtypes
import concourse.bass as bass
import concourse.mybir as mybir
from concourse.bass_utils import run_bass_kernel_spmd

F32 = mybir.dt.float32
BF16 = mybir.dt.bfloat16
AF = mybir.ActivationFunctionType
ALU = mybir.AluOpType

D = 1024
KC = 8
NM = 16
DFF = 2816
FC = 22
INW = 7168
DEPTH = 2
ALPHA = (2 * DEPTH) ** 0.25
NVL = 50
NEGM = -30000.0


class Op:
    __slots__ = ("eng", "fn", "deps", "dma", "sem_key", "signal", "sig_idx", "dma_waits", "inc")

    def __init__(self, eng, fn, dma, sem_key, inc):
        self.eng = eng
        self.fn = fn
        self.deps = []
        self.dma = dma
        self.sem_key = sem_key
        self.signal = False
        self.sig_idx = 0
        self.dma_waits = {}
        self.inc = inc


class Sched:
    ENGS = ("pe", "act", "dve", "pool", "sp")
    EPOCH = 30000

    def __init__(self):
        self.ops = {e: [] for e in self.ENGS}
        self.last_writer = {}
        self.readers = {}
        self.dma_count = {}
        self.nops = 0

    def add(self, eng, fn, reads=(), writes=(), dma=False, sem_key=None, inc=16):
        op = Op(eng, fn, dma, sem_key, inc)
        deps = {}
        for k in reads:
            w = self.last_writer.get(k)
            if w is not None:
                deps[id(w)] = w
        for k in writes:
            w = self.last_writer.get(k)
            if w is not None:
                if not (w.eng == eng and not w.dma and not dma):
                    deps[id(w)] = w
            last_by_eng = {}
            for r in self.readers.get(k, ()):
                if r.eng == eng and not r.dma and not dma:
                    continue
                if r.dma or r.eng == "pool":
                    deps[id(r)] = r
                else:
                    last_by_eng[r.eng] = r
            for r in last_by_eng.values():
                deps[id(r)] = r
        if eng == "pe":
            deps = {i: d for i, d in deps.items() if d.eng != "pe" or d.dma}
        if dma:
            deps = {i: d for i, d in deps.items() if not (d.dma and d.sem_key == sem_key)}
        op.deps = list(deps.values())
        for d in op.deps:
            if d.dma:
                c = self.dma_count[d.sem_key]
                if op.dma_waits.get(d.sem_key, 0) < c:
                    op.dma_waits[d.sem_key] = c
        if dma:
            self.dma_count[sem_key] = self.dma_count.get(sem_key, 0) + inc
        for k in reads:
            self.readers.setdefault(k, []).append(op)
        for k in writes:
            self.last_writer[k] = op
            self.readers[k] = []
        self.ops[eng].append(op)
        self.nops += 1
        return op

    def emit(self, nc, stack):
        for e in self.ENGS:
            for op in self.ops[e]:
                for d in op.deps:
                    if not d.dma:
                        d.signal = True
        nsig = {}
        for e in self.ENGS:
            c = 0
            for op in self.ops[e]:
                if op.signal and not op.dma:
                    c += 1
                    op.sig_idx = c
            nsig[e] = c
        esem = {}
        for e in self.ENGS:
            n_ep = max(1, (nsig[e] + self.EPOCH - 1) // self.EPOCH)
            esem[e] = [stack.enter_context(nc.semaphore(f"s_{e}_{i}")) for i in range(n_ep)]
        dsem = {}
        for i, k in enumerate(self.dma_count):
            dsem[k] = stack.enter_context(nc.semaphore(f"d{i}"))
        self.n_sems = sum(len(v) for v in esem.values()) + len(dsem)
        block = stack.enter_context(nc.Block())
        EP = self.EPOCH

        def run(e, engine):
            waited = {}
            dwaited = {}
            for op in self.ops[e]:
                need = {}
                for d in op.deps:
                    if d.dma:
                        continue
                    if need.get(d.eng, 0) < d.sig_idx:
                        need[d.eng] = d.sig_idx
                for se, idx in need.items():
                    if waited.get(se, 0) >= idx:
                        continue
                    waited[se] = idx
                    ep = (idx - 1) // EP
                    engine.wait_ge(esem[se][ep], idx - ep * EP)
                for sk, cnt in op.dma_waits.items():
                    if dwaited.get(sk, 0) >= cnt:
                        continue
                    dwaited[sk] = cnt
                    engine.wait_ge(dsem[sk], cnt)
                ins = op.fn(engine)
                if ins is None:
                    if e == "sp":
                        for sk, cnt in self.dma_count.items():
                            engine.wait_ge(dsem[sk], cnt)
                        for se in self.ENGS:
                            if nsig[se] > 0:
                                ep = (nsig[se] - 1) // EP
                                engine.wait_ge(esem[se][ep], nsig[se] - ep * EP)
                    continue
                if op.dma:
                    ins.then_inc(dsem[op.sem_key], op.inc)
                elif op.signal:
                    ep = (op.sig_idx - 1) // EP
                    ins.then_inc(esem[e][ep], 1)

        @block.tensor
        def _(t):
            run("pe", t)

        @block.scalar
        def _(t):
            run("act", t)

        @block.vector
        def _(t):
            run("dve", t)

        @block.gpsimd
        def _(t):
            run("pool", t)

        @block.sync
        def _(t):
            run("sp", t)


class Builder:
    def __init__(self, NR, debug=False):
        assert NR % 512 == 0
        self.NR = NR
        self.NT = NR // 512
        self.NB = NR // 128
        self.XW = 4 * NR + 1152
        self.debug = debug
        self.S = Sched()
        self.nc = bass.Bass("TRN2", target_bir_lowering=False)
        self._tmp_i = 0
        self._ps_i = 0
        self._wr_i = 0
        self._kv_i = 0
        self._pt_i = 0

    def mm(self, out, lhsT, rhs, start, stop, reads, writes):
        self.S.add("pe", lambda e, o=out, l=lhsT, r=rhs, a=start, b=stop:
                   e.matmul(o, lhsT=l, rhs=r, start=a, stop=b, skip_group_check=True),
                   reads=reads, writes=writes)

    def act(self, out, in_, func, reads, writes, scale=None, bias=None):
        kw = {}
        if scale is not None:
            kw["scale"] = scale
        if bias is not None:
            kw["bias"] = bias
        self.S.add("act", lambda e, o=out, i=in_, f=func, kw=kw: e.activation(out=o, in_=i, func=f, **kw),
                   reads=reads, writes=writes)

    def tt(self, eng, out, in0, in1, op, reads, writes):
        self.S.add(eng, lambda e, o=out, a=in0, b=in1, p=op: e.tensor_tensor(out=o, in0=a, in1=b, op=p),
                   reads=reads, writes=writes)

    def ts(self, eng, out, in0, s1, op0, reads, writes, s2=None, op1=None):
        if op1 is None:
            self.S.add(eng, lambda e, o=out, a=in0, s=s1, p=op0: e.tensor_scalar(out=o, in0=a, scalar1=s, scalar2=None, op0=p),
                       reads=reads, writes=writes)
        else:
            self.S.add(eng, lambda e, o=out, a=in0, s=s1, p=op0, t=s2, q=op1:
                       e.tensor_scalar(out=o, in0=a, scalar1=s, scalar2=t, op0=p, op1=q),
                       reads=reads, writes=writes)

    def stt(self, out, in0, scalar, in1, op0, op1, reads, writes):
        self.S.add("dve", lambda e, o=out, a=in0, s=scalar, b=in1, p=op0, q=op1:
                   e.scalar_tensor_tensor(out=o, in0=a, scalar=s, in1=b, op0=p, op1=q),
                   reads=reads, writes=writes)

    def cp(self, eng, out, in_, reads, writes):
        if eng == "act":
            self.S.add("act", lambda e, o=out, i=in_: e.copy(out=o, in_=i), reads=reads, writes=writes)
        else:
            self.S.add(eng, lambda e, o=out, i=in_: e.tensor_copy(out=o, in_=i), reads=reads, writes=writes)

    def recip(self, out, in_, reads, writes):
        self.S.add("dve", lambda e, o=out, i=in_: e.reciprocal(out=o, in_=i), reads=reads, writes=writes)

    def memset(self, eng, ap, val, writes):
        self.S.add(eng, lambda e, a=ap, v=val: e.memset(a, v), writes=writes)

    def dma(self, q, out, in_, reads, writes, sem_key):
        self.S.add(q, lambda e, o=out, i=in_: e.dma_start(out=o, in_=i), reads=reads, writes=writes,
                   dma=True, sem_key=sem_key)

    def tmp(self):
        i = self._tmp_i % 5
        self._tmp_i += 1
        return i

    def bank(self):
        i = self._ps_i % 8
        self._ps_i += 1
        return i

    def declare(self, st):
        nc, NR = self.nc, self.NR

        def din(name, shape, dt=F32):
            return nc.dram_tensor(name, shape, dt, kind="ExternalInput").ap()

        def dout(name, shape, dt=F32):
            return nc.dram_tensor(name, shape, dt, kind="ExternalOutput").ap()

        def dscr(name, shape, dt):
            return nc.dram_tensor(name, shape, dt)

        self.x = {"p": din("xp", [128, KC, NR]), "s": din("xs", [128, KC, NR])}
        self.xm = din("xm", [128, KC, NM])
        self.w_in = din("w_in", [DEPTH, D, INW])
        self.w_ba = din("w_ba", [DEPTH, D, D])
        self.w_bb = din("w_bb", [DEPTH, D, D])
        self.w_out = din("w_out", [DEPTH, D, D])
        self.w_up = din("w_up", [DEPTH, D, 2 * DFF])
        self.w_down = din("w_down", [DEPTH, DFF, D])
        self.pool_w = din("pool_w", [DEPTH, 4, 256, 256])
        self.vecs_d = din("vecs", [128, 2 * NVL + 2])
        self.rope = {"p": din("rope_p", [128, 4, NM + NR]), "s": din("rope_s", [128, 4, NM + NR])}
        self.c32_d = din("c32", [128, 3, 128])
        self.NCB = 3 * 128 + 4 * 512 + 64
        self.cb16_d = din("cb16", [128, self.NCB], BF16)
        self.blend_d = din("blend", [128, 16])
        self.invc_d = din("invc", [128, 3, 4, 16])
        self.y = {"p": dout("yp", [128, KC, NR]), "s": dout("ys", [128, KC, NR])}
        self.wi_b = dscr("wi_b", [DEPTH, 128, KC, INW], BF16).ap()
        self.wba_b = dscr("wba_b", [DEPTH, 128, KC, D], BF16).ap()
        self.wbb_b = dscr("wbb_b", [DEPTH, 128, KC, D], BF16).ap()
        self.wo_b = dscr("wo_b", [DEPTH, 128, KC, D], BF16).ap()
        self.wu_b = dscr("wu_b", [DEPTH, 128, KC, 2 * DFF], BF16).ap()
        self.wd_b = dscr("wd_b", [DEPTH, 128, KC, FC, 128], BF16).ap()
        self.pw_b = dscr("pw_b", [DEPTH, 128, 4, 2, 256], BF16).ap()
        self.wg_b = dscr("wg_b", [DEPTH, 128, KC, 3, KC, 128], BF16).ap()
        self.wab_b = dscr("wab_b", [DEPTH, 128, KC, 2, KC, 128], BF16).ap()
        self.wu2_b = dscr("wu2_b", [DEPTH, 128, FC // 2, 2, KC, 256], BF16).ap()
        self.H1 = {s: dscr("h1_" + s, [128, KC, NM + NR], F32).ap() for s in "ps"}
        self.KAl = dscr("kal", [128, 2, NR], BF16).ap()
        self.VAl = dscr("val", [128, 2, self.NB, 128], BF16).ap()
        self.KBl = {s: dscr("kbl_" + s, [128, 2, (self.NB + 2) * 128], BF16).ap() for s in "ps"}
        self.VBl = {s: dscr("vbl_" + s, [128, self.NB + 2, 2, 128], BF16).ap() for s in "ps"}
        self.U = {s: dscr("u_" + s, [128, KC, NR + 16], F32).ap() for s in "ps"}
        self.Um = {s: dscr("um_" + s, [128, KC, 32], F32).ap() for s in "ps"}
        self.CK = [[dscr(f"ck{l}{g}", [128, NR], BF16) for g in range(2)] for l in range(DEPTH)]
        self.CV = [[dscr(f"cv{l}{g}", [128, NR], BF16) for g in range(2)] for l in range(DEPTH)]
        self.CH = [dscr(f"ch{l}", [128, 1152], BF16) for l in range(DEPTH)]
        self.GK = [[dscr(f"gk{l}{g}", [512, NR], BF16) for g in range(2)] for l in range(DEPTH)]
        self.GV = [[dscr(f"gv{l}{g}", [512, NR], BF16) for g in range(2)] for l in range(DEPTH)]
        self.GH = [dscr(f"gh{l}", [512, 1152], BF16) for l in range(DEPTH)]
        if self.debug:
            self.dbg = {s: dout("dbg_" + s, [128, KC, NM + NR]) for s in "ps"}

        def sb(name, shape, dt):
            return st.enter_context(nc.sbuf_tensor(name, shape, dt))

        self.WR = sb("WR", [128, 4, 4096], BF16)
        self.KVR = sb("KVR", [128, 2, 4096], BF16)
        self.KVB = sb("KVB", [128, 3072], BF16)
        self.H32 = sb("H32", [128, KC, 512], F32)
        self.LD32 = sb("LD32", [128, 4, 512], F32)
        self.HB = sb("HB", [128, 2, KC, 512], BF16)
        self.XA = sb("XA", [128, 24, 512], BF16)
        self.OB = sb("OB", [128, 8, 512], BF16)
        self.DP = sb("DP", [128, 8, 512], BF16)
        self.TMP = sb("TMP", [128, 8, 512], F32)
        self.RT = sb("RT", [128, 4, 512], F32)
        self.PT = sb("PT", [128, 4, 512], BF16)
        self.UT = sb("UT", [128, 3, 2, 528], F32)
        self.KST = sb("KST", [128, 2, 2, 512], BF16)
        self.VST = sb("VST", [128, 4, 512], BF16)
        self.C32 = sb("C32", [128, 3, 128], F32)
        self.CB = sb("CB", [128, self.NCB], BF16)
        self.VEC = sb("VEC", [128, 2 * NVL + 2], F32)
        self.BL = sb("BL", [128, 16], F32)
        self.INVC = sb("INVC", [128, 3, 4, 16], F32)
        self.PW = sb("PW", [128, 4, 2, 256], BF16)
        self.KM = sb("KM", [128, 2, 4, 128], BF16)
        self.VM = sb("VM", [128, 2, 4, 128], BF16)
        self.ESK = sb("ESK", [128, 8], F32)
        self.HC = sb("HC", [128, 4, 256], BF16)
        self.HA = sb("HA", [128, 256], F32)
        self.HO = sb("HO", [128, 256], BF16)
        self.ZT = sb("ZT", [128, 64], F32)
        self.PS = [st.enter_context(nc.psum_tensor(f"ps{i}", [128, 512], F32)) for i in range(8)]

    def ones32(self):
        return self.C32[:, 0, :]

    def perm(self, which):
        return self.C32[:, 1 if which == "A" else 2, :]

    def identb(self):
        return self.CB[:, 0:128]

    def onesb(self):
        return self.CB[:, 128:256]

    def ones16(self):
        return self.CB[:, 256:384]

    def mask(self, i):
        o = 384 + i * 512
        return self.CB[:, o:o + 512].rearrange("p (a b) -> p a b", a=4)

    def maskmeta(self):
        o = 384 + 4 * 512
        return self.CB[:, o:o + 64].rearrange("p (a b) -> p a b", a=4)

    def vec(self, l, col):
        return self.VEC[:, l * NVL + col: l * NVL + col + 1]

    def load_consts(self):
        for name, sbt, dr in (("C32", self.C32, self.c32_d), ("CB", self.CB, self.cb16_d),
                              ("VEC", self.VEC, self.vecs_d), ("BL", self.BL, self.blend_d),
                              ("INVC", self.INVC, self.invc_d)):
            self.dma("sp", sbt[:], dr, [], [name], "c_" + name)
        self.memset("pool", self.ZT[:], 0.0, ["ZT"])

    def convert_weights(self):
        jobs = []
        for l in range(DEPTH):
            def blocks(src, dst, n):
                for c0 in range(0, n, 512):
                    jobs.append((src[:, c0:c0 + 512].rearrange("(k p) n -> p k n", p=128),
                                 [(dst[:, :, c0:c0 + 512], None)], KC, 512))
            blocks(self.w_in[l][:, 0:4096], self.wi_b[l], 4096)
            for i in range(3):
                for c0 in range(0, D, 512):
                    src = self.w_in[l][:, 4096 + i * D + c0:4096 + i * D + c0 + 512].rearrange("(k p) n -> p k n", p=128)
                    jobs.append((src, [(self.wg_b[l][:, c0 // 128 + cc, i], cc) for cc in range(4)], KC, 512))
            for j, wsrc in enumerate((self.w_ba[l], self.w_bb[l])):
                for c0 in range(0, D, 512):
                    src = wsrc[:, c0:c0 + 512].rearrange("(k p) n -> p k n", p=128)
                    jobs.append((src, [(self.wab_b[l][:, c0 // 128 + cc, j], cc) for cc in range(4)], KC, 512))
            blocks(self.w_out[l], self.wo_b[l], D)
            for part in range(2):
                for gi in range(FC // 2):
                    src = self.w_up[l][:, part * DFF + gi * 256:part * DFF + (gi + 1) * 256].rearrange("(k p) n -> p k n", p=128)
                    jobs.append((src, [(self.wu2_b[l][:, gi, part], None)], KC, 256))
            for k0 in range(0, FC, 8):
                kn = min(8, FC - k0)
                for c0 in range(0, D, 512):
                    src = self.w_down[l][k0 * 128:(k0 + kn) * 128, c0:c0 + 512].rearrange("(k p) n -> p k n", p=128)
                    dsts = [(self.wd_b[l][:, c0 // 128 + cc, k0:k0 + kn, :], cc) for cc in range(4)]
                    jobs.append((src, dsts, kn, 512))
            for g in range(4):
                jobs.append((self.pool_w[l, g].rearrange("(k p) n -> p k n", p=128),
                             [(self.pw_b[l][:, g, :, :], None)], 2, 256))
        stg = [(self.H32, "H32"), (self.TMP, "tmp")]
        outb = [(self.OB, "OB"), (self.DP, "DP")]
        engs = ["dve", "act", "pool"]
        for i, (src, dsts, kc, bw) in enumerate(jobs):
            s32, k32 = stg[i % 2]
            s16, k16 = outb[i % 2]
            n = kc * bw
            v32 = s32[:].rearrange("p a b -> p (a b)")[:, 0:n].rearrange("p (k n) -> p k n", k=kc)
            v16 = s16[:].rearrange("p a b -> p (a b)")[:, 0:n].rearrange("p (k n) -> p k n", k=kc)
            rk = [(k32, c) for c in range(8)]
            wk = [(k16, c) for c in range(8)]
            self.dma("sp", v32, src, [], rk, "cv32_%d" % (i % 2))
            self.cp(engs[i % 3], v16, v32, rk, wk)
            for (dst, cc) in dsts:
                sv = v16 if cc is None else v16[:, :, cc * 128:(cc + 1) * 128]
                self.dma("pool", dst, sv, wk, ["wts"], "cv16_%d" % (i % 2))

    def wload(self, parts):
        s = self._wr_i % 4
        self._wr_i += 1
        key = ("WR", s)
        views = []
        off = 0
        for ap in parts:
            if len(ap.shape) == 3:
                k, n = ap.shape[1], ap.shape[2]
                v = self.WR[:, s, off:off + k * n].rearrange("p (k n) -> p k n", k=k)
                m = k * n
            else:
                m = ap.shape[1]
                v = self.WR[:, s, off:off + m]
            self.dma("sp", v, ap, ["wts"], [key], "wr%d" % s)
            views.append(v)
            off += m
        assert off <= 4096
        return key, views

    def hkey(self, l, seg):
        return [] if l == 0 else [("H1", seg)]

    def load_hb(self, src, slot, T, rk):
        for half in range(2):
            self.dma("sp", self.LD32[:, :, 0:T], src[:, half * 4:half * 4 + 4, :], rk, ["LD32"], "ld32")
            self.cp("pool", self.HB[:, slot, half * 4:half * 4 + 4, 0:T], self.LD32[:, :, 0:T], ["LD32"],
                    [("HB", slot, c) for c in range(half * 4, half * 4 + 4)])

    def hsrc(self, l, seg, t0, T, meta):
        if l == 0:
            return self.xm if meta else self.x[seg][:, :, t0:t0 + T]
        return self.H1[seg][:, :, 0:NM] if meta else self.H1[seg][:, :, NM + t0:NM + t0 + T]

    def load_rope(self, seg, col0, T):
        self.dma("sp", self.RT[:, :, 0:T], self.rope[seg][:, :, col0:col0 + T], [], ["RT"], "rt")

    def head_chain(self, bk, T, which, gcol, out_ap, out_keys):
        psk = ("ps", bk)
        ps = self.PS[bk][:, 0:T]
        ci, si = (0, 1) if which == "A" else (2, 3)
        t_q = self.tmp()
        q32 = self.TMP[:, t_q, 0:T]
        if which == "A":
            self.act(q32, ps, AF.Identity, [psk, "VEC"], [("tmp", t_q)], scale=gcol)
            t_s = self.tmp()
            sq = self.TMP[:, t_s, 0:T]
            self.act(sq, ps, AF.Square, [psk], [("tmp", t_s)])
            b2 = self.bank()
            self.mm(self.PS[b2][:, 0:T], self.ones32(), sq, True, True, ["C32", ("tmp", t_s)], [("ps", b2)])
            self.act(sq, self.PS[b2][:, 0:T], AF.Ln, [("ps", b2), "VEC"], [("tmp", t_s)],
                     bias=self.VEC[:, 2 * NVL:2 * NVL + 1])
            self.act(sq, sq, AF.Exp, [("tmp", t_s)], [("tmp", t_s)], scale=-0.5)
        else:
            self.cp("dve", q32, ps, [psk], [("tmp", t_q)])
        b3 = self.bank()
        self.mm(self.PS[b3][:, 0:T], self.perm(which), q32, True, True, ["C32", ("tmp", t_q)], [("ps", b3)])
        t_b = self.tmp()
        bb = self.TMP[:, t_b, 0:T]
        self.tt("dve", bb, self.PS[b3][:, 0:T], self.RT[:, si, 0:T], ALU.mult, [("ps", b3), "RT"], [("tmp", t_b)])
        self.tt("pool", q32, q32, self.RT[:, ci, 0:T], ALU.mult, [("tmp", t_q), "RT"], [("tmp", t_q)])
        if which == "A":
            self.tt("pool", q32, q32, bb, ALU.add, [("tmp", t_q), ("tmp", t_b)], [("tmp", t_q)])
            self.tt("dve", out_ap, q32, sq, ALU.mult, [("tmp", t_q), ("tmp", t_s)], out_keys)
        else:
            self.tt("dve", out_ap, q32, bb, ALU.add, [("tmp", t_q), ("tmp", t_b)], out_keys)

    def phase1_tile(self, l, seg, ti, meta):
        NR, NB = self.NR, self.NB
        T = NM if meta else 512
        t0 = 0 if meta else ti * 512
        si_ = 0 if seg == "p" else 1
        slot = self._hb_par
        self._hb_par ^= 1
        self.load_hb(self.hsrc(l, seg, t0, T, meta), slot, T, self.hkey(l, seg))
        self.load_rope(seg, 0 if meta else NM + t0, T)
        hbk = [("HB", slot, c) for c in range(KC)]
        hb = self.HB[:, slot]
        wi = self.wi_b[l]
        k1, (x1,) = self.wload([wi[:, :, 1024:1536]])
        k2, (x2,) = self.wload([wi[:, :, 2560:3072]])
        for which, xw, wkey, row in (("A", x1, k1, 0), ("B", x2, k2, 1)):
            for g in range(2):
                bk = self.bank()
                for kc in range(KC):
                    self.mm(self.PS[bk][:, 0:T], xw[:, kc, g * 128:(g + 1) * 128], hb[:, kc, 0:T],
                            kc == 0, kc == KC - 1, [wkey, hbk[kc]], [("ps", bk)])
                if meta:
                    out_ap = self.KM[:, si_, row * 2 + g, 0:NM]
                    okeys = [("KM", seg)]
                else:
                    out_ap = self.KST[:, row, g, :]
                    okeys = [("KST", row)]
                self.head_chain(bk, T, which, self.vec(l, 1), out_ap, okeys)
        ntb = 1 if meta else 4
        for tb in range(ntb):
            bk = self.bank()
            M = NM if meta else 128
            for j, (xw, wkey) in enumerate(((x1, k1), (x2, k2))):
                for kc in range(KC):
                    self.mm(self.PS[bk][0:M, j * 256:(j + 1) * 256], hb[:, kc, tb * 128:tb * 128 + M],
                            xw[:, kc, 256:512], (j == 0 and kc == 0), (j == 1 and kc == KC - 1),
                            [wkey, hbk[kc]], [("ps", bk)])
            if meta:
                self.cp("act", self.VM[0:NM, si_].rearrange("p a b -> p (a b)"), self.PS[bk][0:NM, :], [("ps", bk)], [("VM", seg)])
            else:
                self.cp("act" if tb % 2 else "dve", self.VST[:, tb, :], self.PS[bk][:, :], [("ps", bk)], [("VST", tb)])
        for half in range(2):
            ku, (xu,) = self.wload([wi[:, :, 3072 + half * 512:3072 + (half + 1) * 512]])
            for cc in range(4):
                c = half * 4 + cc
                bk = self.bank()
                for kc in range(KC):
                    self.mm(self.PS[bk][:, 0:T], xu[:, kc, cc * 128:(cc + 1) * 128], hb[:, kc, 0:T],
                            kc == 0, kc == KC - 1, [ku, hbk[kc]], [("ps", bk)])
                self.cp("act" if c % 2 else "dve", self.H32[:, c, 0:T], self.PS[bk][:, 0:T], [("ps", bk)], [("H32", c)])
        h32k = [("H32", c) for c in range(KC)]
        if meta:
            self.dma("pool", self.Um[seg][:, :, 8:8 + NM], self.H32[:, :, 0:NM], h32k, [("Um", seg)], "st_u")
            return
        vst4 = self.VST[:].rearrange("p t (j g d) -> p t j g d", j=2, g=2)
        vstk = [("VST", tb) for tb in range(4)]
        for g in range(2):
            if seg == "p":
                ka_dst = self.KAl[:, g, t0:t0 + 512]
                va_dst = self.VAl[:, g, ti * 4:ti * 4 + 4, :]
                kakey, vakey = "KAl", "VAl"
            else:
                ka_dst = self.CK[l][g].ap()[:, t0:t0 + 512]
                va_dst = self.CV[l][g].ap().rearrange("p (n d) -> p n d", d=128)[:, ti * 4:ti * 4 + 4, :]
                kakey, vakey = ("CK", l, g), ("CV", l, g)
            self.dma("pool", ka_dst, self.KST[:, 0, g, :], [("KST", 0)], [kakey], "st_ka")
            self.dma("pool", va_dst, vst4[:, :, 0, g, :], vstk, [vakey], "st_va")
        self.dma("pool", self.KBl[seg][:, :, 128 + t0:128 + t0 + 512], self.KST[:, 1], [("KST", 1)], [("KBl", seg)], "st_kb")
        self.dma("pool", self.VBl[seg][:, 1 + ti * 4:1 + ti * 4 + 4], vst4[:, :, 1], vstk, [("VBl", seg)], "st_vb")
        self.dma("pool", self.U[seg][:, :, 8 + t0:8 + t0 + 512], self.H32[:, :, :], h32k, [("U", seg)], "st_u")
        if seg == "s":
            o = 0
            C = self.CH[l].ap()
            if ti == 0:
                self.dma("pool", C[:, o:o + 256].rearrange("p (g t) -> p g t", g=2), self.KST[:, 1, :, 0:128],
                         [("KST", 1)], [("CH", l)], "st_kb")
                self.dma("pool", C[:, o + 512:o + 768].rearrange("p (g d) -> p g d", g=2), vst4[:, 0, 1], vstk, [("CH", l)], "st_vb")
                self.cp("dve", self.HO[:, 0:64].rearrange("p (c t) -> p c t", c=8), self.H32[:, :, 0:8], h32k, ["HO"])
                self.dma("pool", C[:, o + 1024:o + 1088], self.HO[:, 0:64], ["HO"], [("CH", l)], "st_ho")
            if ti == self.NT - 1:
                self.dma("pool", C[:, o + 256:o + 512].rearrange("p (g t) -> p g t", g=2), self.KST[:, 1, :, 384:512],
                         [("KST", 1)], [("CH", l)], "st_kb")
                self.dma("pool", C[:, o + 768:o + 1024].rearrange("p (g d) -> p g d", g=2), vst4[:, 3, 1], vstk, [("CH", l)], "st_vb")
                self.cp("dve", self.HO[:, 64:128].rearrange("p (c t) -> p c t", c=8), self.H32[:, :, 504:512], h32k, ["HO"])
                self.dma("pool", C[:, o + 1088:o + 1152], self.HO[:, 64:128], ["HO"], [("CH", l)], "st_ho")

    def phase1(self, l, seg):
        si_ = 0 if seg == "p" else 1
        self.memset("pool", self.KM[:, si_], 0.0, [("KM", seg)])
        self.memset("pool", self.VM[:, si_], 0.0, [("VM", seg)])
        self.phase1_tile(l, seg, 0, True)
        for ti in range(self.NT):
            self.phase1_tile(l, seg, ti, False)

    def allgather(self, l):
        jobs = [(self.CK[l][g], self.GK[l][g], ("CK", l, g), ("GK", l, g)) for g in range(2)]
        jobs += [(self.CV[l][g], self.GV[l][g], ("CV", l, g), ("GV", l, g)) for g in range(2)]
        jobs += [(self.CH[l], self.GH[l], ("CH", l), ("GH", l))]
        for i, (ct, gt, ck, gk) in enumerate(jobs):
            self.S.add("pool", lambda e, ct=ct, gt=gt: e.collective_compute(
                "AllGather", ALU.bypass, replica_groups=[[0, 1, 2, 3], [4, 5, 6, 7]],
                ins=[ct.ap().opt()], outs=[gt.ap().opt()]),
                reads=[ck], writes=[gk], dma=True, sem_key="ag%d_%d" % (l, i), inc=1)

    def prep(self, l, seg):
        NR, NB = self.NR, self.NB
        U, Um = self.U[seg], self.Um[seg]
        z = self.ZT[:].rearrange("p (c t) -> p c t", c=8)
        sfx = "_%d%s" % (l, seg)
        self.dma("pool", Um[:, :, 0:8], z, ["ZT"], [("Um", seg)], "pp0" + sfx)
        if seg == "p":
            self.dma("pool", U[:, :, 0:8], Um[:, :, 16:24], [("Um", seg)], [("U", seg)], "pp1" + sfx)
            self.dma("pool", U[:, :, NR + 8:NR + 16], z, ["ZT"], [("U", seg)], "pp2" + sfx)
            self.dma("pool", Um[:, :, 24:32], U[:, :, 8:16], [("U", seg)], [("Um", seg)], "pp3" + sfx)
            return
        G = self.GH[l].ap()
        o = 0
        gk = ("GH", l)

        def blend(cands, wcol0, width, extra=None):
            acc = self.HA[:, 0:width]
            self.ts("dve", acc, self.HC[:, 0, 0:width], self.BL[:, wcol0:wcol0 + 1], ALU.mult, ["HC", "BL"], ["HA"])
            for r in range(1, 4):
                self.stt(acc, self.HC[:, r, 0:width], self.BL[:, wcol0 + r:wcol0 + r + 1], acc, ALU.mult, ALU.add,
                         ["HC", "BL", "HA"], ["HA"])

        def gload(off, width):
            self.dma("sp", self.HC[:, :, 0:width], G[:, off:off + width].rearrange("(r p) x -> p r x", p=128),
                     [gk], ["HC"], "hc")

        for (off, wc, ext) in ((o + 256, 0, 0), (o, 4, NB + 1)):
            gload(off, 256)
            blend(None, wc, 256)
            self.cp("dve", self.HO[:, 0:256], self.HA[:, 0:256], ["HA"], ["HO"])
            self.dma("pool", self.KBl[seg][:, :, ext * 128:(ext + 1) * 128], self.HO[:, 0:256].rearrange("p (g t) -> p g t", g=2),
                     ["HO"], [("KBl", seg)], "pp1" + sfx)
        for (off, wc, ext) in ((o + 768, 0, 0), (o + 512, 4, NB + 1)):
            gload(off, 256)
            blend(None, wc, 256)
            self.cp("dve", self.HO[:, 0:256], self.HA[:, 0:256], ["HA"], ["HO"])
            self.dma("pool", self.VBl[seg][:, ext], self.HO[:, 0:256].rearrange("p (g d) -> p g d", g=2),
                     ["HO"], [("VBl", seg)], "pp1" + sfx)
        gload(o + 1088, 64)
        blend(None, 0, 64)
        self.dma("sp", self.LD32[:, 0, 0:64].rearrange("p (c t) -> p c t", c=8), Um[:, :, 16:24], [("Um", seg)], ["LD32"], "ld32")
        self.stt(self.HA[:, 0:64], self.LD32[:, 0, 0:64], self.BL[:, 8:9], self.HA[:, 0:64], ALU.mult, ALU.add,
                 ["LD32", "BL", "HA"], ["HA"])
        self.dma("pool", U[:, :, 0:8], self.HA[:, 0:64].rearrange("p (c t) -> p c t", c=8), ["HA"], [("U", seg)], "pp2" + sfx)
        gload(o + 1024, 64)
        blend(None, 4, 64)
        self.dma("pool", U[:, :, NR + 8:NR + 16], self.HA[:, 0:64].rearrange("p (c t) -> p c t", c=8), ["HA"], [("U", seg)], "pp2" + sfx)
        self.dma("sp", self.HC[:, 0, 0:64], G[0:128, o + 1024:o + 1088], [gk], ["HC"], "hc")
        self.cp("dve", self.HA[:, 0:64], self.HC[:, 0, 0:64], ["HC"], ["HA"])
        self.dma("pool", Um[:, :, 24:32], self.HA[:, 0:64].rearrange("p (c t) -> p c t", c=8), ["HA"], [("Um", seg)], "pp3" + sfx)

    def pool_branch(self, l, seg, ti, meta, T):
        NR = self.NR
        src = self.Um[seg] if meta else self.U[seg]
        c0 = 0 if meta else ti * 512
        W = T + 16
        last = (not meta) and ti == self.NT - 1
        for g in range(4):
            w = (2, 4, 8, 16)[g]
            e = self.UT[:, 0, :, 0:W]
            self.dma("pool", e, src[:, 2 * g:2 * g + 2, c0:c0 + W], [("Um", seg) if meta else ("U", seg)], [("UT", 0)], "ut")
            a = self.UT[:, 1, :, :]
            b = self.UT[:, 2, :, :]
            k0, k1, k2 = ("UT", 0), ("UT", 1), ("UT", 2)
            ev = self.UT[:, 0, :, :]
            if w == 2:
                self.tt("pool", a[:, :, 0:T], ev[:, :, 7:7 + T], ev[:, :, 8:8 + T], ALU.add, [k0], [k1])
                s, sk = a, k1
            elif w == 4:
                self.tt("pool", a[:, :, 0:W - 1], ev[:, :, 0:W - 1], ev[:, :, 1:W], ALU.add, [k0], [k1])
                self.tt("pool", b[:, :, 0:T], a[:, :, 6:6 + T], a[:, :, 8:8 + T], ALU.add, [k1], [k2])
                s, sk = b, k2
            elif w == 8:
                self.tt("pool", a[:, :, 0:W - 1], ev[:, :, 0:W - 1], ev[:, :, 1:W], ALU.add, [k0], [k1])
                self.tt("pool", b[:, :, 0:W - 3], a[:, :, 0:W - 3], a[:, :, 2:W - 1], ALU.add, [k1], [k2])
                self.tt("pool", a[:, :, 0:T], b[:, :, 4:4 + T], b[:, :, 8:8 + T], ALU.add, [k2], [k1])
                s, sk = a, k1
            else:
                self.tt("pool", a[:, :, 0:W - 1], ev[:, :, 0:W - 1], ev[:, :, 1:W], ALU.add, [k0], [k1])
                self.tt("pool", b[:, :, 0:W - 3], a[:, :, 0:W - 3], a[:, :, 2:W - 1], ALU.add, [k1], [k2])
                self.tt("pool", a[:, :, 0:W - 7], b[:, :, 0:W - 7], b[:, :, 4:W - 3], ALU.add, [k2], [k1])
                self.tt("pool", b[:, :, 0:T], a[:, :, 0:T], a[:, :, 8:8 + T], ALU.add, [k1], [k2])
                s, sk = b, k2
            dpk = [("DP", 2 * g), ("DP", 2 * g + 1)]
            out = self.DP[:, 2 * g:2 * g + 2, 0:T]
            if meta:
                for cc in range(2):
                    self.tt("dve", s[:, cc, 0:T], s[:, cc, 0:T], self.INVC[:, 0, g, 0:T], ALU.mult, [sk, "INVC"], [sk])
                self.tt("dve", out, s[:, :, 0:T], ev[:, :, 8:8 + T], ALU.subtract, [sk, k0], dpk)
            else:
                self.stt(out, s[:, :, 0:T], 1.0 / w, ev[:, :, 8:8 + T], ALU.mult, ALU.subtract, [sk, k0], dpk)
                if last:
                    ti_ = 1 if seg == "p" else 2
                    for cc in range(2):
                        self.tt("dve", s[:, cc, T - 8:T], s[:, cc, T - 8:T], self.INVC[:, ti_, g, 0:8], ALU.mult, [sk, "INVC"], [sk])
                    self.tt("dve", self.DP[:, 2 * g:2 * g + 2, T - 8:T], s[:, :, T - 8:T], ev[:, :, T:T + 8], ALU.subtract,
                            [sk, k0], dpk)

    def key_sources(self, l, seg):
        NR, NB = self.NR, self.NB
        out = {}
        for g in range(2):
            lst = []
            if seg == "p":
                for c0 in range(0, NB, 16):
                    n = min(16, NB - c0)
                    lst.append((self.KAl[:, g, c0 * 128:(c0 + n) * 128], self.VAl[:, g, c0:c0 + n, :], n, ["KAl", "VAl"]))
            else:
                for r in range(4):
                    Kr = self.GK[l][g].ap()[r * 128:(r + 1) * 128]
                    Vr = self.GV[l][g].ap()[r * 128:(r + 1) * 128].rearrange("p (n d) -> p n d", d=128)
                    for c0 in range(0, NB, 16):
                        n = min(16, NB - c0)
                        lst.append((Kr[:, c0 * 128:(c0 + n) * 128], Vr[:, c0:c0 + n, :], n, [("GK", l, g), ("GV", l, g)]))
            out[g] = lst
        return out

    def attn_global(self, l, seg, T):
        srcs = self.key_sources(l, seg)
        scale = 128.0 ** 0.5
        si_ = 0 if seg == "p" else 1
        SB = (0, 1, 2, 7)
        obs, dbs = (3, 4), (5, 6)
        for g in range(2):
            for pr in range(2):
                heads = (4 * g + 2 * pr, 4 * g + 2 * pr + 1)
                first = [True, True]
                queue = []

                def flush_one():
                    for (hi, pslot, v_ap, ones_ap, rds) in queue.pop(0):
                        p_ap = self.PT[:, pslot, 0:T]
                        self.mm(self.PS[obs[hi]][:, 0:T], v_ap, p_ap, first[hi], False, rds + [("PT", pslot)], [("ps", obs[hi])])
                        self.mm(self.PS[dbs[hi]][:, 0:T], ones_ap, p_ap, first[hi], False, rds + [("PT", pslot)], [("ps", dbs[hi])])
                        first[hi] = False

                def do_tile(kT, v_ap, ones_ap, rds):
                    items = []
                    for hi, h in enumerate(heads):
                        k_ = self._pt_i % 4
                        self._pt_i += 1
                        sb_ = SB[k_]
                        self.mm(self.PS[sb_][:, 0:T], kT, self.XA[:, h, 0:T], True, True, rds + [("XA", h)], [("ps", sb_)])
                        self.act(self.PT[:, k_, 0:T], self.PS[sb_][:, 0:T], AF.Exp, [("ps", sb_)], [("PT", k_)], scale=scale)
                        items.append((hi, k_, v_ap, ones_ap, rds))
                    queue.append(items)
                    if len(queue) > 1:
                        flush_one()

                do_tile(self.KM[:, si_, g, :], self.VM[:, si_, g, :], self.ones16(), [("KM", seg), ("VM", seg), "CB"])
                for (Kap, Vap, n, rd) in srcs[g]:
                    s = self._kv_i % 2
                    self._kv_i += 1
                    kvk = ("KVR", s)
                    self.dma("sp", self.KVR[:, s, 0:n * 128], Kap, rd, [kvk], "kv%d" % s)
                    self.dma("sp", self.KVR[:, s, 2048:2048 + n * 128], Vap.rearrange("p n d -> p (n d)"), rd, [kvk], "kv%d" % s)
                    for j in range(n):
                        do_tile(self.KVR[:, s, j * 128:(j + 1) * 128], self.KVR[:, s, 2048 + j * 128:2048 + (j + 1) * 128],
                                self.onesb(), [kvk, "CB"])
                while queue:
                    flush_one()
                for hi, h in enumerate(heads):
                    t_r = self.tmp()
                    rd_ = self.TMP[:, t_r, 0:T]
                    self.act(rd_, self.PS[dbs[hi]][:, 0:T], AF.Ln, [("ps", dbs[hi])], [("tmp", t_r)])
                    self.act(rd_, rd_, AF.Exp, [("tmp", t_r)], [("tmp", t_r)], scale=-1.0)
                    self.tt("dve", self.XA[:, 16 + h, 0:T], self.PS[obs[hi]][:, 0:T], rd_, ALU.mult,
                            [("ps", obs[hi]), ("tmp", t_r)], [("XA", 16 + h)])

    def attn_window(self, l, seg, ti, meta, T):
        NB = self.NB
        scale = 128.0 ** -0.5
        si_ = 0 if seg == "p" else 1
        SB = (0, 1, 2, 7)
        if meta:
            e0, nblk = 1, 1
        else:
            e0, nblk = ti * 4, 6
        kvbk = "KVB"
        Kv = self.KVB[:, 0:1536].rearrange("p (g t) -> p g t", g=2)
        Vv = self.KVB[:, 1536:3072].rearrange("p (n g d) -> p n g d", g=2, d=128)
        if meta and seg == "s":
            G = self.GH[l].ap()
            o = 0
            self.dma("sp", Kv[:, :, 0:128], G[0:128, o:o + 256].rearrange("p (g t) -> p g t", g=2), [("GH", l)], [kvbk], "kvb")
            self.dma("sp", Vv[:, 0], G[0:128, o + 512:o + 768].rearrange("p (g d) -> p g d", g=2), [("GH", l)], [kvbk], "kvb")
        else:
            self.dma("sp", Kv[:, :, 0:nblk * 128], self.KBl[seg][:, :, e0 * 128:(e0 + nblk) * 128], [("KBl", seg)], [kvbk], "kvb")
            self.dma("sp", Vv[:, 0:nblk], self.VBl[seg][:, e0:e0 + nblk], [("VBl", seg)], [kvbk], "kvb")
        nqb = 1 if meta else 4
        QW = NM if meta else 128
        N4 = 4 * QW
        for qb in range(nqb):
            for g in range(2):
                qv = self.XA[:, 8 + 4 * g:8 + 4 * g + 4, qb * 128:qb * 128 + QW]
                qk = [("XA", 8 + 4 * g + i) for i in range(4)]
                ob, db = 3 + (g % 2), 5 + (g % 2)
                O = self.PS[ob][:, 0:N4].rearrange("p (a b) -> p a b", a=4)
                Dn = self.PS[db][:, 0:N4].rearrange("p (a b) -> p a b", a=4)
                kts = [(self.KM[:, si_, 2 + g, :], self.VM[:, si_, 2 + g, :], self.ones16(), None, [("KM", seg), ("VM", seg), "CB"])]
                if meta:
                    kts.append((Kv[:, g, 0:128], Vv[:, 0, g, :], self.onesb(), self.maskmeta(), [kvbk, "CB"]))
                else:
                    b = ti * 4 + qb
                    for d_, mi in ((0, 0), (1, None), (2, 1)):
                        if seg == "p" and ((b == 0 and d_ == 0) or (b == NB - 1 and d_ == 2)):
                            continue
                        m = None
                        if mi is not None:
                            if seg == "s" and b == 0 and d_ == 0:
                                m = self.mask(2)
                            elif seg == "s" and b == NB - 1 and d_ == 2:
                                m = self.mask(3)
                            else:
                                m = self.mask(mi)
                        j = qb + d_
                        kts.append((Kv[:, g, j * 128:(j + 1) * 128], Vv[:, j, g, :], self.onesb(), m, [kvbk, "CB"]))
                pslots = []
                for (kT, v_ap, ones_ap, m, rds) in kts:
                    k_ = self._pt_i % 4
                    self._pt_i += 1
                    sb_ = SB[k_]
                    Sb = self.PS[sb_][:, 0:N4].rearrange("p (a b) -> p a b", a=4)
                    self.mm(Sb, kT, qv, True, m is None, rds + qk, [("ps", sb_)])
                    if m is not None:
                        self.mm(Sb, self.identb(), m, False, True, ["CB"], [("ps", sb_)])
                    self.act(self.PT[:, k_, 0:N4], self.PS[sb_][:, 0:N4], AF.Exp, [("ps", sb_)], [("PT", k_)], scale=scale)
                    pslots.append((k_, v_ap, ones_ap, rds))
                for i, (sb_, v_ap, ones_ap, rds) in enumerate(pslots):
                    p_ap = self.PT[:, sb_, 0:N4].rearrange("p (a b) -> p a b", a=4)
                    self.mm(O, v_ap, p_ap, i == 0, False, rds + [("PT", sb_)], [("ps", ob)])
                    self.mm(Dn, ones_ap, p_ap, i == 0, False, rds + [("PT", sb_)], [("ps", db)])
                t_r = self.tmp()
                dt = self.TMP[:, t_r, 0:N4].rearrange("p (a b) -> p a b", a=4)
                for hh in range(4):
                    self.act(dt[:, hh, :], Dn[:, hh, :], AF.Ln, [("ps", db), "ESK"], [("tmp", t_r)],
                             bias=self.ESK[:, 4 * g + hh:4 * g + hh + 1])
                self.act(self.TMP[:, t_r, 0:N4], self.TMP[:, t_r, 0:N4], AF.Exp, [("tmp", t_r)], [("tmp", t_r)], scale=-1.0)
                self.tt("dve", self.OB[:, 4 * g:4 * g + 4, qb * 128:qb * 128 + QW], O, dt, ALU.mult,
                        [("ps", ob), ("tmp", t_r)], [("OB", 4 * g + i) for i in range(4)])

    def layer_norm(self, l, T, gcol0, bcol0, hb_slot, store=None):
        b1, b2 = self.bank(), self.bank()
        for p_ in range(4):
            c0, c1 = 2 * p_, 2 * p_ + 1
            t_ = self.tmp()
            self.tt("dve", self.TMP[:, t_, 0:T], self.H32[:, c0, 0:T], self.H32[:, c1, 0:T], ALU.add,
                    [("H32", c0), ("H32", c1)], [("tmp", t_)])
            self.mm(self.PS[b1][:, 0:T], self.ones32(), self.TMP[:, t_, 0:T], p_ == 0, p_ == 3,
                    ["C32", ("tmp", t_)], [("ps", b1)])
        for p_ in range(4):
            c0, c1 = 2 * p_, 2 * p_ + 1
            ta, tb = self.tmp(), self.tmp()
            self.act(self.TMP[:, ta, 0:T], self.H32[:, c0, 0:T], AF.Square, [("H32", c0)], [("tmp", ta)])
            self.act(self.TMP[:, tb, 0:T], self.H32[:, c1, 0:T], AF.Square, [("H32", c1)], [("tmp", tb)])
            self.tt("pool", self.TMP[:, ta, 0:T], self.TMP[:, ta, 0:T], self.TMP[:, tb, 0:T], ALU.add,
                    [("tmp", ta), ("tmp", tb)], [("tmp", ta)])
            self.mm(self.PS[b2][:, 0:T], self.ones32(), self.TMP[:, ta, 0:T], p_ == 0, p_ == 3,
                    ["C32", ("tmp", ta)], [("ps", b2)])
        tm, tv = 5, 6
        m = self.TMP[:, tm, 0:T]
        v = self.TMP[:, tv, 0:T]
        self.ts("dve", m, self.PS[b1][:, 0:T], 1.0 / D, ALU.mult, [("ps", b1)], [("tmp", tm)])
        self.tt("dve", v, m, m, ALU.mult, [("tmp", tm)], [("tmp", tv)])
        self.stt(v, self.PS[b2][:, 0:T], 1.0 / D, v, ALU.mult, ALU.subtract, [("ps", b2), ("tmp", tv)], [("tmp", tv)])
        self.act(v, v, AF.Ln, [("tmp", tv), "VEC"], [("tmp", tv)], bias=self.VEC[:, 2 * NVL + 1:2 * NVL + 2])
        self.act(v, v, AF.Exp, [("tmp", tv)], [("tmp", tv)], scale=-0.5)
        for c in range(KC):
            t_ = self.tmp()
            x = self.TMP[:, t_, 0:T]
            self.tt("pool", x, self.H32[:, c, 0:T], m, ALU.subtract, [("H32", c), ("tmp", tm)], [("tmp", t_)])
            self.tt("dve", x, x, v, ALU.mult, [("tmp", t_), ("tmp", tv)], [("tmp", t_)])
            self.act(self.H32[:, c, 0:T], x, AF.Identity, [("tmp", t_), "VEC"], [("H32", c)],
                     scale=self.vec(l, gcol0 + c), bias=self.vec(l, bcol0 + c))
            if hb_slot is not None:
                self.cp("dve", self.HB[:, hb_slot, c, 0:T], self.H32[:, c, 0:T], [("H32", c)], [("HB", hb_slot, c)])

    def prefetch_hb(self, l, seg, ti, meta):
        T = NM if meta else 512
        t0 = 0 if meta else ti * 512
        slot = self._hb_par
        self._hb_par ^= 1
        self.load_hb(self.hsrc(l, seg, t0, T, meta), slot, T, self.hkey(l, seg))
        self._pref[(l, seg, ti, meta)] = slot

    def phase2_tile(self, l, seg, ti, meta, nxt=None):
        NR = self.NR
        T = NM if meta else 512
        t0 = 0 if meta else ti * 512
        if (l, seg, ti, meta) not in self._pref:
            self.prefetch_hb(l, seg, ti, meta)
        slot = self._pref.pop((l, seg, ti, meta))
        src = self.hsrc(l, seg, t0, T, meta)
        self.load_rope(seg, 0 if meta else NM + t0, T)
        h32k = [("H32", c) for c in range(KC)]
        hbk = [("HB", slot, c) for c in range(KC)]
        hb = self.HB[:, slot]
        wi = self.wi_b[l]
        for which, col0, xbase, gcol in (("A", 0, 0, self.vec(l, 0)), ("B", 1536, 8, None)):
            for half in range(2):
                kw, (xw,) = self.wload([wi[:, :, col0 + half * 512:col0 + (half + 1) * 512]])
                for hh in range(4):
                    h = half * 4 + hh
                    bk = self.bank()
                    for kc in range(KC):
                        self.mm(self.PS[bk][:, 0:T], xw[:, kc, hh * 128:(hh + 1) * 128], hb[:, kc, 0:T],
                                kc == 0, kc == KC - 1, [kw, hbk[kc]], [("ps", bk)])
                    self.head_chain(bk, T, which, gcol, self.XA[:, xbase + h, 0:T], [("XA", xbase + h)])
        self.pool_branch(l, seg, ti, meta, T)
        self.attn_global(l, seg, T)
        self.attn_window(l, seg, ti, meta, T)
        self.dma("sp", self.H32[:, :, 0:T], src, self.hkey(l, seg), h32k, "h32")
        if nxt is not None:
            self.prefetch_hb(l, seg, nxt[0], nxt[1])
        for c in range(KC):
            kg, (xgf,) = self.wload([self.wg_b[l][:, c].rearrange("p i k n -> p (i k n)")])
            xg4 = xgf.rearrange("p (i k n) -> p i k n", i=3, k=KC)
            xgs = [xg4[:, i] for i in range(3)]
            kb_, (xabf,) = self.wload([self.wab_b[l][:, c].rearrange("p i k n -> p (i k n)")])
            xab4 = xabf.rearrange("p (i k n) -> p i k n", i=2, k=KC)
            xa_, xb_ = xab4[:, 0], xab4[:, 1]
            gts = []
            for i, (xw, wk) in enumerate(((xgs[0], kg), (xgs[1], kg), (xgs[2], kg))):
                bk = self.bank()
                for kc in range(KC):
                    self.mm(self.PS[bk][:, 0:T], xw[:, kc, :], hb[:, kc, 0:T], kc == 0, kc == KC - 1,
                            [wk, hbk[kc]], [("ps", bk)])
                t_ = self.tmp()
                self.act(self.TMP[:, t_, 0:T], self.PS[bk][:, 0:T], AF.Sigmoid, [("ps", bk)], [("tmp", t_)])
                gts.append(t_)
            ba, bb_, bc = self.bank(), self.bank(), self.bank()
            for kc in range(KC):
                self.mm(self.PS[ba][:, 0:T], xa_[:, kc, :], self.XA[:, 16 + kc, 0:T], kc == 0, kc == KC - 1,
                        [kb_, ("XA", 16 + kc)], [("ps", ba)])
            for kc in range(KC):
                self.mm(self.PS[bb_][:, 0:T], xb_[:, kc, :], self.OB[:, kc, 0:T], kc == 0, kc == KC - 1,
                        [kb_, ("OB", kc)], [("ps", bb_)])
            g_ = c // 2
            e_ = c % 2
            for kc in range(2):
                self.mm(self.PS[bc][:, 0:T], self.PW[:, g_, kc, e_ * 128:(e_ + 1) * 128], self.DP[:, 2 * g_ + kc, 0:T],
                        kc == 0, kc == 1, ["PW", ("DP", 2 * g_ + kc)], [("ps", bc)])
            ta, tb_, tc = gts
            A_ = self.TMP[:, ta, 0:T]
            B_ = self.TMP[:, tb_, 0:T]
            C_ = self.TMP[:, tc, 0:T]
            self.tt("dve", A_, self.PS[ba][:, 0:T], A_, ALU.mult, [("ps", ba), ("tmp", ta)], [("tmp", ta)])
            self.tt("dve", B_, self.PS[bb_][:, 0:T], B_, ALU.mult, [("ps", bb_), ("tmp", tb_)], [("tmp", tb_)])
            self.stt(C_, self.PS[bc][:, 0:T], self.vec(l, 2 + c), C_, ALU.mult, ALU.mult, [("ps", bc), ("tmp", tc), "VEC"], [("tmp", tc)])
            self.tt("pool", A_, A_, B_, ALU.add, [("tmp", ta), ("tmp", tb_)], [("tmp", ta)])
            self.tt("dve", self.XA[:, c, 0:T], A_, C_, ALU.add, [("tmp", ta), ("tmp", tc)], [("XA", c)])
        for half in range(2):
            kw, (xw,) = self.wload([self.wo_b[l][:, :, half * 512:(half + 1) * 512]])
            for cc in range(4):
                c = half * 4 + cc
                bk = self.bank()
                for kc in range(KC):
                    self.mm(self.PS[bk][:, 0:T], xw[:, kc, cc * 128:(cc + 1) * 128], self.XA[:, kc, 0:T], kc == 0, kc == KC - 1,
                            [kw, ("XA", kc)], [("ps", bk)])
                self.stt(self.H32[:, c, 0:T], self.H32[:, c, 0:T], ALPHA, self.PS[bk][:, 0:T], ALU.mult, ALU.add,
                         [("H32", c), ("ps", bk)], [("H32", c)])
        self.layer_norm(l, T, 10, 18, slot)
        wu = self.wu_b[l]
        for j0 in range(0, FC, 2):
            kw, (xuf,) = self.wload([self.wu2_b[l][:, j0 // 2].rearrange("p i k n -> p (i k n)")])
            xu4 = xuf.rearrange("p (i k n) -> p i k n", i=2, k=KC)
            xg, xu = xu4[:, 0], xu4[:, 1]
            for jj in range(2):
                j = j0 + jj
                bg, bu = self.bank(), self.bank()
                for kc in range(KC):
                    self.mm(self.PS[bg][:, 0:T], xg[:, kc, jj * 128:(jj + 1) * 128], hb[:, kc, 0:T], kc == 0, kc == KC - 1,
                            [kw, hbk[kc]], [("ps", bg)])
                for kc in range(KC):
                    self.mm(self.PS[bu][:, 0:T], xu[:, kc, jj * 128:(jj + 1) * 128], hb[:, kc, 0:T], kc == 0, kc == KC - 1,
                            [kw, hbk[kc]], [("ps", bu)])
                t_ = self.tmp()
                self.act(self.TMP[:, t_, 0:T], self.PS[bg][:, 0:T], AF.Silu, [("ps", bg)], [("tmp", t_)])
                self.tt("dve", self.XA[:, j, 0:T], self.TMP[:, t_, 0:T], self.PS[bu][:, 0:T], ALU.mult,
                        [("tmp", t_), ("ps", bu)], [("XA", j)])
        for c in range(KC):
            kw, (xwf,) = self.wload([self.wd_b[l][:, c].rearrange("p k n -> p (k n)")])
            xw = xwf.rearrange("p (k n) -> p k n", k=FC)
            bk = self.bank()
            for j in range(FC):
                self.mm(self.PS[bk][:, 0:T], xw[:, j, :], self.XA[:, j, 0:T], j == 0, j == FC - 1,
                        [kw, ("XA", j)], [("ps", bk)])
            self.stt(self.H32[:, c, 0:T], self.H32[:, c, 0:T], ALPHA, self.PS[bk][:, 0:T], ALU.mult, ALU.add,
                     [("H32", c), ("ps", bk)], [("H32", c)])
        self.layer_norm(l, T, 26, 34, None)
        if l == 0:
            dst = self.H1[seg][:, :, 0:NM] if meta else self.H1[seg][:, :, NM + t0:NM + t0 + T]
            self.dma("pool", dst, self.H32[:, :, 0:T], h32k, [("H1", seg)], "st_h")
            if self.debug:
                dd = self.dbg[seg][:, :, 0:NM] if meta else self.dbg[seg][:, :, NM + t0:NM + t0 + T]
                self.dma("pool", dd, self.H32[:, :, 0:T], h32k, ["out"], "st_d")
        else:
            self.dma("pool", self.y[seg][:, :, t0:t0 + T], self.H32[:, :, 0:T], h32k, ["out"], "st_h")

    def phase2(self, l, seg):
        tiles = ([(0, True)] if l == 0 else []) + [(ti, False) for ti in range(self.NT)]
        for i, (ti, meta) in enumerate(tiles):
            self.phase2_tile(l, seg, ti, meta, tiles[i + 1] if i + 1 < len(tiles) else None)

    def layer_consts(self, l):
        self.dma("sp", self.PW[:], self.pw_b[l], ["wts"], ["PW"], "pw")
        self.act(self.ESK[:], self.VEC[:, l * NVL + 42:l * NVL + 50], AF.Exp, ["VEC"], ["ESK"])

    def build(self):
        with contextlib.ExitStack() as st:
            self.declare(st)
            self._hb_par = 0
            self._pref = {}
            self.load_consts()
            import os
            if os.environ.get("KNOCONV", "0") != "1":
                self.convert_weights()
            import os
            stop = int(os.environ.get("KSTOP", "99"))
            for l in range(DEPTH):
                if stop <= 0:
                    break
                self.layer_consts(l)
                self.phase1(l, "s")
                if stop <= 1:
                    break
                self.allgather(l)
                if stop <= 2:
                    break
                self.phase1(l, "p")
                self.prep(l, "p")
                if stop <= 3:
                    break
                self.phase2(l, "p")
                if stop <= 4:
                    break
                self.prep(l, "s")
                if stop <= 5:
                    break
                self.phase2(l, "s")
                if stop <= 6:
                    break
            self.S.add("sp", lambda e: None, reads=["out"])
            self.S.emit(self.nc, st)
        return self.nc


def _rope_tables(NR, q):
    theta = np.float32(10000.0)
    s = (np.arange(NR, dtype=np.int64) + q * NR)
    row = np.concatenate([-np.ones(NM, np.int64), s // 64]).astype(np.float32)
    col = np.concatenate([np.arange(NM, dtype=np.int64), s % 64]).astype(np.float32)
    pos = np.concatenate([np.arange(NM, dtype=np.int64), NM + s]).astype(np.float32)
    inv32 = (theta ** (-np.arange(0, 64, 2, dtype=np.float32) / np.float32(64))).astype(np.float32)
    inv64 = (theta ** (-np.arange(0, 128, 2, dtype=np.float32) / np.float32(128))).astype(np.float32)
    out = np.zeros((128, 4, NM + NR), np.float32)
    for base, p in ((0, row), (64, col)):
        ang = (p[None, :] * inv32[:, None]).astype(np.float32)
        c, sn = np.cos(ang).astype(np.float32), np.sin(ang).astype(np.float32)
        out[base:base + 32, 0] = c
        out[base + 32:base + 64, 0] = c
        out[base:base + 32, 1] = -sn
        out[base + 32:base + 64, 1] = sn
    ang = (pos[None, :] * inv64[:, None]).astype(np.float32)
    c, sn = np.cos(ang).astype(np.float32), np.sin(ang).astype(np.float32)
    out[0:64, 2] = c
    out[64:128, 2] = c
    out[0:64, 3] = -sn
    out[64:128, 3] = sn
    return out


def _consts():
    c32 = np.zeros((128, 3, 128), np.float32)
    c32[:, 0, :] = 1.0
    for d in range(128):
        srcA = d + 32 if (d % 64) < 32 else d - 32
        srcB = d + 64 if d < 64 else d - 64
        c32[srcA, 1, d] = 1.0
        c32[srcB, 2, d] = 1.0
    return c32


def _cb16(q, is_first_valid, is_last_valid):
    NCB = 3 * 128 + 4 * 512 + 64
    cb = np.zeros((128, NCB), np.float32)
    cb[:, 0:128] = np.eye(128, dtype=np.float32)
    cb[:, 128:256] = 1.0
    cb[0:NM, 256:384] = 1.0
    k = np.arange(128)[:, None]
    qq = np.arange(128)[None, :]
    mprev = np.where(k >= qq, 0.0, NEGM).astype(np.float32)
    mnext = np.where(k <= qq, 0.0, NEGM).astype(np.float32)
    mpf = mprev if is_first_valid else np.full((128, 128), NEGM, np.float32)
    mnl = mnext if is_last_valid else np.full((128, 128), NEGM, np.float32)
    for i, m in enumerate((mprev, mnext, mpf, mnl)):
        cb[:, 384 + i * 512:384 + (i + 1) * 512] = np.tile(m, (1, 4))
    qm = np.arange(NM)[None, :]
    mm = np.where(k <= 112 + qm, 0.0, NEGM).astype(np.float32)
    cb[:, 384 + 4 * 512:384 + 4 * 512 + 64] = np.tile(mm, (1, 4))
    return cb.astype(ml_dtypes.bfloat16)


def _invc(NR, L_s, q):
    out = np.zeros((3, 4, 16), np.float32)
    wins = (2, 4, 8, 16)
    Lp = NM + NR
    for g, w in enumerate(wins):
        for t in range(NM):
            out[0, g, t] = 1.0 / (min(Lp, t + w // 2) - max(0, t - w // 2))
        for i in range(8):
            tp = Lp - 8 + i
            out[1, g, i] = 1.0 / (min(Lp, tp + w // 2) - max(0, tp - w // 2))
            ts_ = NM + (q + 1) * NR - 8 + i
            out[2, g, i] = 1.0 / (min(L_s, ts_ + w // 2) - max(0, ts_ - w // 2))
    return np.broadcast_to(out[None], (128, 3, 4, 16)).copy()


def _fm(a):
    t = a.shape[0]
    return np.ascontiguousarray(a.reshape(t, KC, 128).transpose(2, 1, 0))


def _vecs(inp):
    v = np.zeros((128, 2 * NVL + 2), np.float32)
    for l in range(DEPTH):
        b = l * NVL
        v[:, b + 0] = inp["q_norm_g"][l]
        v[:, b + 1] = inp["k_norm_g"][l]
        for name, c0 in (("pool_scale", 2), ("ln1_g", 10), ("ln1_b", 18), ("ln2_g", 26), ("ln2_b", 34)):
            v[:, b + c0:b + c0 + 8] = inp[name][l].reshape(KC, 128).T
        v[:, b + 42:b + 50] = inp["sink_logit"][l][None, :]
    v[:, 2 * NVL] = 128.0 * 1e-6
    v[:, 2 * NVL + 1] = 1e-5
    return v


_NC_CACHE = {}


def _run(inp, debug=False):
    xp = np.asarray(inp["x_prompt"], np.float32)
    xs = np.asarray(inp["x_sample"], np.float32)
    NR = xp.shape[1]
    assert xp.shape[0] == 8 and xs.shape[0] == 2 and xs.shape[1] == 4 * NR
    key = (NR, debug)
    if key not in _NC_CACHE:
        _NC_CACHE[key] = Builder(NR, debug).build()
    nc = _NC_CACHE[key]
    f = lambda n: np.ascontiguousarray(np.asarray(inp[n], np.float32))
    shared = {
        "xm": _fm(np.asarray(inp["meta_tokens"], np.float32)),
        "w_in": f("w_in"), "w_ba": f("w_branch_a"), "w_bb": f("w_branch_b"), "w_out": f("w_out"),
        "w_up": f("w_up"), "w_down": f("w_down"), "pool_w": f("pool_w"),
        "vecs": _vecs({k: np.asarray(v, np.float32) for k, v in inp.items()}),
        "rope_p": _rope_tables(NR, 0), "c32": _consts(),
    }
    L_s = NM + 4 * NR
    in_maps = []
    for c in range(8):
        q = c % 4
        bl = np.zeros((128, 16), np.float32)
        if q > 0:
            bl[:, q - 1] = 1.0
        else:
            bl[:, 8] = 1.0
        if q < 3:
            bl[:, 4 + q + 1] = 1.0
        m = dict(shared)
        m["xp"] = _fm(xp[c])
        m["xs"] = _fm(xs[c // 4, q * NR:(q + 1) * NR])
        m["rope_s"] = _rope_tables(NR, q)
        m["cb16"] = _cb16(q, q > 0, q < 3)
        m["blend"] = bl
        m["invc"] = _invc(NR, L_s, q)
        in_maps.append(m)
    res = run_bass_kernel_spmd(nc, in_maps, core_ids=list(range(8)))
    r = res.results

    def unfm(a):
        return np.ascontiguousarray(a.transpose(2, 1, 0).reshape(a.shape[2], D))

    y_p = np.stack([unfm(r[c]["yp"]) for c in range(8)], 0).astype(np.float32)
    y_s = np.stack([np.concatenate([unfm(r[4 * b + q]["ys"]) for q in range(4)], 0) for b in range(2)], 0).astype(np.float32)
    if debug:
        return (y_p, y_s), r
    return (y_p, y_s)


def kernel(**inputs):
    return _run(inputs)
```
